# Optimizing a Trainium2 kernel written in Bass

```python
import jax, jax.numpy as jnp
from jax import lax
import numpy as np

D_MODEL = 1024
BATCH = 8
SEQ = 8192
DEPTH = 2

CHUNK = 64
MEM_LEN = 256
FOX_HEADS = 8
FOX_HEAD_DIM = 64
FOX_WIDTH = FOX_HEADS * FOX_HEAD_DIM
POOL_WIDTH = D_MODEL - FOX_WIDTH
POOL_WINDOWS = (2, 4, 8, 16)
POOL_GROUPS = len(POOL_WINDOWS)
POOL_GROUP_DIM = POOL_WIDTH // POOL_GROUPS
IN_COLS = 3 * FOX_WIDTH + POOL_WIDTH + FOX_HEADS
MEM_HEADS = 4
MEM_HEAD_DIM = 128
MEM_WIDTH = MEM_HEADS * MEM_HEAD_DIM
D_FF = ((-(-8 * D_MODEL // 3) + 255) // 256) * 256
Q_BLOCK = 128
EPS = 1e-6

kernel_name = "fox_pool_hybrid_encoder"


def rmsnorm(x, g):
    xf = x.astype(jnp.float32)
    y = xf * lax.rsqrt(jnp.mean(xf * xf, axis=-1, keepdims=True) + EPS)
    return (y * g.astype(jnp.float32)).astype(x.dtype)


def forgetting_attention(q, k, v, f_logit):
    c = jnp.cumsum(jax.nn.log_sigmoid(f_logit.astype(jnp.float32)), axis=-1)
    scale = FOX_HEAD_DIM ** -0.5
    S = q.shape[2]
    outs = []
    for start in range(0, S, Q_BLOCK):
        end = start + Q_BLOCK
        qb = q[:, :, start:end]
        kb = k[:, :, :end]
        vb = v[:, :, :end]
        s = jnp.einsum('bhqd,bhkd->bhqk', qb, kb, preferred_element_type=jnp.float32) * scale
        s = s + c[:, :, start:end, None] - c[:, :, None, :end]
        qpos = start + jnp.arange(Q_BLOCK)
        kpos = jnp.arange(end)
        s = jnp.where(kpos[None, :] <= qpos[:, None], s, -jnp.inf)
        p = jax.nn.softmax(s, axis=-1)
        outs.append(jnp.einsum('bhqk,bhkd->bhqd', p.astype(vb.dtype), vb))
    return jnp.concatenate(outs, axis=2)


def multiscale_pool(p, w_pool, pool_scale):
    B, S, _ = p.shape
    pf = p.astype(jnp.float32).reshape(B, S, POOL_GROUPS, POOL_GROUP_DIM)
    cs = jnp.concatenate([jnp.zeros((B, 1, POOL_GROUPS, POOL_GROUP_DIM), jnp.float32),
                          jnp.cumsum(pf, axis=1)], axis=1)
    t = jnp.arange(S)
    outs = []
    for g, w in enumerate(POOL_WINDOWS):
        csg = cs[:, :, g]
        lower = jnp.pad(csg, ((0, 0), (w - 1, 0), (0, 0)))[:, :S]
        cnt = jnp.minimum(t + 1, w).astype(jnp.float32)[None, :, None]
        mixed = (csg[:, 1:] - lower) / cnt - pf[:, :, g]
        outs.append(jnp.einsum('bsc,cd->bsd', mixed, w_pool[g].astype(jnp.float32)))
    y = jnp.concatenate(outs, axis=-1) * pool_scale.astype(jnp.float32)
    return y.astype(p.dtype)


def head_split(t, n_heads, head_dim):
    B, S, _ = t.shape
    return t.reshape(B, S, n_heads, head_dim)


def setup_inputs(seed: int = 0) -> dict:
    key = jax.random.key(seed)
    ks = jax.random.split(key, 24)
    f32 = jnp.float32
    D = D_MODEL

    def nrm(k, shape, fan_in):
        return jax.random.normal(k, shape, f32) * (fan_in ** -0.5)

    def gain(k, shape):
        return 1.0 + 0.05 * jax.random.normal(k, shape, f32)

    return {
        "x": jax.random.normal(ks[0], (BATCH, SEQ, D), f32),
        "mem": jax.random.normal(ks[1], (BATCH, MEM_LEN, D), f32),
        "g_mix": gain(ks[2], (DEPTH, D)),
        "w_in": nrm(ks[3], (DEPTH, D, IN_COLS), D),
        "b_forget": 3.0 + 0.5 * jax.random.normal(ks[4], (DEPTH, FOX_HEADS), f32),
        "g_q_fox": gain(ks[5], (DEPTH, FOX_HEAD_DIM)),
        "g_k_fox": gain(ks[6], (DEPTH, FOX_HEAD_DIM)),
        "w_pool": nrm(ks[7], (DEPTH, POOL_GROUPS, POOL_GROUP_DIM, POOL_GROUP_DIM), POOL_GROUP_DIM),
        "pool_scale": gain(ks[8], (DEPTH, POOL_WIDTH)),
        "w_out": nrm(ks[9], (DEPTH, D, D), D),
        "g_mem_q": gain(ks[10], (DEPTH, D)),
        "g_mem_kv": gain(ks[11], (DEPTH, D)),
        "w_mem_q": nrm(ks[12], (DEPTH, D, MEM_WIDTH), D),
        "w_mem_kv": nrm(ks[13], (DEPTH, D, 2 * MEM_WIDTH), D),
        "g_q_mem": gain(ks[14], (DEPTH, MEM_HEAD_DIM)),
        "g_k_mem": gain(ks[15], (DEPTH, MEM_HEAD_DIM)),
        "w_mem_out": nrm(ks[16], (DEPTH, MEM_WIDTH, D), MEM_WIDTH),
        "g_ffn": gain(ks[17], (DEPTH, D)),
        "w_gate_up": nrm(ks[18], (DEPTH, D, 2 * D_FF), D),
        "w_down": nrm(ks[19], (DEPTH, D_FF, D), D_FF),
    }


def reference(x, mem, g_mix, w_in, b_forget, g_q_fox, g_k_fox, w_pool, pool_scale, w_out,
              g_mem_q, g_mem_kv, w_mem_q, w_mem_kv, g_q_mem, g_k_mem, w_mem_out,
              g_ffn, w_gate_up, w_down):
    B, S, _ = x.shape
    h = x
    for l in range(DEPTH):
        xn = rmsnorm(h, g_mix[l])
        z = jnp.einsum('bsd,dc->bsc', xn, w_in[l])
        q = z[..., :FOX_WIDTH]
        k = z[..., FOX_WIDTH:2 * FOX_WIDTH]
        v = z[..., 2 * FOX_WIDTH:3 * FOX_WIDTH]
        p_in = z[..., 3 * FOX_WIDTH:3 * FOX_WIDTH + POOL_WIDTH]
        f_logit = z[..., 3 * FOX_WIDTH + POOL_WIDTH:] + b_forget[l]
        q = rmsnorm(head_split(q, FOX_HEADS, FOX_HEAD_DIM), g_q_fox[l]).transpose(0, 2, 1, 3)
        k = rmsnorm(head_split(k, FOX_HEADS, FOX_HEAD_DIM), g_k_fox[l]).transpose(0, 2, 1, 3)
        v = head_split(v, FOX_HEADS, FOX_HEAD_DIM).transpose(0, 2, 1, 3)
        fox = forgetting_attention(q, k, v, f_logit.transpose(0, 2, 1))
        fox = fox.transpose(0, 2, 1, 3).reshape(B, S, FOX_WIDTH)
        pool = multiscale_pool(p_in, w_pool[l], pool_scale[l])
        h = h + jnp.einsum('bsc,cd->bsd', jnp.concatenate([fox, pool], axis=-1), w_out[l])

        hn = rmsnorm(h, g_mem_q[l])
        mn = rmsnorm(mem, g_mem_kv[l])
        mq = rmsnorm(head_split(jnp.einsum('bsd,dc->bsc', hn, w_mem_q[l]), MEM_HEADS, MEM_HEAD_DIM), g_q_mem[l])
        mkv = jnp.einsum('bmd,dc->bmc', mn, w_mem_kv[l])
        mk = rmsnorm(head_split(mkv[..., :MEM_WIDTH], MEM_HEADS, MEM_HEAD_DIM), g_k_mem[l])
        mv = head_split(mkv[..., MEM_WIDTH:], MEM_HEADS, MEM_HEAD_DIM)
        sc = jnp.einsum('bshd,bmhd->bhsm', mq, mk, preferred_element_type=jnp.float32) * (MEM_HEAD_DIM ** -0.5)
        pm = jax.nn.softmax(sc, axis=-1).astype(mv.dtype)
        mo = jnp.einsum('bhsm,bmhd->bshd', pm, mv).reshape(B, S, MEM_WIDTH)
        h = h + jnp.einsum('bsc,cd->bsd', mo, w_mem_out[l])

        hn = rmsnorm(h, g_ffn[l])
        gu = jnp.einsum('bsd,df->bsf', hn, w_gate_up[l])
        act = jax.nn.silu(gu[..., :D_FF]) * gu[..., D_FF:]
        h = h + jnp.einsum('bsf,fd->bsd', act, w_down[l])
    return h
```

```python
import numpy as np
from contextlib import ExitStack
import concourse.bass as bass
import concourse.mybir as mybir
from concourse.bass_utils import run_bass_kernel_spmd

F32 = mybir.dt.float32
BF16 = mybir.dt.bfloat16
ALU = mybir.AluOpType
AF = mybir.ActivationFunctionType

D = 1024
EPS = 1e-6
NVEC = 72
DFF = 2816
NFC = 22
MEM = 256


class Tok:
    __slots__ = ("s", "v", "clk")

    def __init__(self, s, v, clk):
        self.s = s
        self.v = v
        self.clk = clk


class Buf:
    __slots__ = ("w", "rd", "excl")

    def __init__(self, excl=False):
        self.w = None
        self.rd = {}
        self.excl = excl


class RR:
    def __init__(self, items):
        self.items = list(items)
        self.i = 0

    def next(self):
        r = self.items[self.i % len(self.items)]
        self.i += 1
        return r


class Sched:
    COMPUTE = ("pe", "act", "dve", "pool")

    def __init__(self, nc, es, nslots=8):
        self.nc = nc
        self.h = {"pe": nc.tensor, "act": nc.scalar, "dve": nc.vector, "pool": nc.gpsimd, "sp": nc.sync}
        self.sems = []
        self.own = {}
        for e in self.COMPUTE:
            self.own[e] = len(self.sems)
            self.sems.append(es.enter_context(nc.semaphore("sem_" + e)))
        self.slots = {}
        for q in ("sp", "pool"):
            self.slots[q] = []
            for i in range(nslots):
                self.slots[q].append(len(self.sems))
                self.sems.append(es.enter_context(nc.semaphore("dma_%s_%d" % (q, i))))
        self.n = len(self.sems)
        self.cnt = [0] * self.n
        self.clk = {e: [0] * self.n for e in self.h}
        self.dman = {"sp": 0, "pool": 0}
        self.nops = 0

    def op(self, eng, fn, reads=(), writes=(), dma=False):
        clk = self.clk[eng]
        own = self.own.get(eng)
        deps = []
        wr = list(writes)
        for b in reads:
            if b.excl:
                wr.append(b)
            elif b.w is not None:
                deps.append((b.w, 0))
        for b in wr:
            if b.w is not None:
                deps.append((b.w, 1))
            for t in b.rd.values():
                deps.append((t, 2))
        waits = {}
        for t, kind in deps:
            if (not dma) and t.s == own:
                if kind != 0 or eng == "pe":
                    continue
            if clk[t.s] >= t.v:
                continue
            waits[t.s] = max(waits.get(t.s, 0), t.v)
            tc = t.clk
            for i in range(self.n):
                if tc[i] > clk[i]:
                    clk[i] = tc[i]
            if clk[t.s] < t.v:
                clk[t.s] = t.v
        if dma:
            k = self.dman[eng]
            self.dman[eng] += 1
            sl = self.slots[eng][k % len(self.slots[eng])]
            prev = self.cnt[sl]
            if clk[sl] < prev:
                waits[sl] = max(waits.get(sl, 0), prev)
                clk[sl] = prev
            self.cnt[sl] += 16
            tok = Tok(sl, self.cnt[sl], list(clk))
            inc = (sl, 16)
        else:
            self.cnt[own] += 1
            snap = list(clk)
            snap[own] = self.cnt[own]
            tok = Tok(own, self.cnt[own], snap)
            inc = (own, 1)
        for b in reads:
            if not b.excl:
                b.rd[tok.s] = tok
        for b in wr:
            b.w = tok
            b.rd = {}
        e = self.h[eng]
        wl = sorted(waits.items())
        attach = None
        if (not dma) and wl:
            attach = wl.pop()
        for s, v in wl:
            e.wait_ge(self.sems[s], v)
        r = fn(e)
        first, last = r if isinstance(r, tuple) else (r, r)
        if attach is not None:
            first._wait_ge(self.sems[attach[0]], attach[1])
        last.then_inc(self.sems[inc[0]], inc[1])
        self.nops += 1
        return tok

    def barrier(self):
        for eng, e in self.h.items():
            clk = self.clk[eng]
            for s in range(self.n):
                if s == self.own.get(eng):
                    continue
                if clk[s] < self.cnt[s]:
                    e.wait_ge(self.sems[s], self.cnt[s])
                    clk[s] = self.cnt[s]
        for eng in self.h:
            own = self.own.get(eng)
            for s in range(self.n):
                if s != own:
                    self.clk[eng][s] = self.cnt[s]

    def finish(self):
        e = self.h["sp"]
        clk = self.clk["sp"]
        for s in range(self.n):
            if clk[s] < self.cnt[s]:
                e.wait_ge(self.sems[s], self.cnt[s])
                clk[s] = self.cnt[s]


def build(S, NL):
    NB = S // 512
    NT = S // 128
    nc = bass.Bass("TRN2", target_bir_lowering=False)

    def din(name, shape, dt=F32):
        return nc.dram_tensor(name, list(shape), dt, kind="ExternalInput").ap()

    def dscr(name, shape, dt):
        return nc.dram_tensor(name, list(shape), dt, kind="Internal").ap()

    x_d = din("x", [S, D])
    mem_d = din("mem", [MEM, D])
    w_in_d = din("w_in", [NL, D, 2056])
    w_pool_d = din("w_pool", [NL, 4, 128, 128])
    w_out_d = din("w_out", [NL, D, D])
    w_mq_d = din("w_mem_q", [NL, D, 512])
    w_mkv_d = din("w_mem_kv", [NL, D, 1024])
    w_mo_d = din("w_mem_out", [NL, 512, D])
    w_gu_d = din("w_gate_up", [NL, D, 2 * DFF])
    w_dn_d = din("w_down", [NL, DFF, D])
    vecs_d = din("vecs", [NL, 128, NVEC])
    consts_d = din("consts", [128, 640])
    out_d = nc.dram_tensor("out", [S, D], F32, kind="ExternalOutput").ap()

    s_in = dscr("s_in", [NL, 128, 8, 2056], BF16)
    s_pool = dscr("s_pool", [NL, 128, 4, 128], BF16)
    s_out = dscr("s_out", [NL, 128, 8, 1024], BF16)
    s_mq = dscr("s_mq", [NL, 128, 8, 512], BF16)
    s_mkv = dscr("s_mkv", [NL, 128, 8, 1024], BF16)
    s_mo = dscr("s_mo", [NL, 128, 4, 1024], BF16)
    s_gu = dscr("s_gu", [NL, 128, 8, 2 * DFF], BF16)
    s_dn = dscr("s_dn", [NL, 128, NFC, 1024], BF16)
    hscr = dscr("hscr", [8, 128, S], F32)
    kscr = dscr("kscr", [4, 128, S], BF16)
    vscr = dscr("vscr", [4, 128, NT, 128], BF16)
    hscr_v = hscr.rearrange("k p s -> p k s")
    kscr_v = kscr.rearrange("c p s -> p c s")

    with ExitStack() as es:
        sc = Sched(nc, es)
        op = sc.op

        uniq = [0]

        def sb(name, shape, dt, stack=es):
            uniq[0] += 1
            return stack.enter_context(nc.sbuf_tensor("%s_%d" % (name, uniq[0]), list(shape), dt))

        ps = [es.enter_context(nc.psum_tensor("ps%d" % i, [128, 512], F32)) for i in range(8)]
        pb = [Buf(excl=True) for _ in range(8)]

        cst = sb("cst", [128, 640], F32)
        cstb = Buf()
        ident = cst[:, 0:128]
        triu_f = cst[:, 128:256]
        ones_f = cst[:, 384:512]
        invcnt = cst[:, 512:576]
        cbf = sb("cbf", [128, 384], BF16)
        cbfb = Buf()
        triu_b = cbf[:, 0:128]
        bd64_b = cbf[:, 128:256]
        ones_b = cbf[:, 256:384]
        epsT = sb("epsT", [128, 2], F32)
        epsb = Buf()
        op("sp", lambda e: e.dma_start(out=cst[:], in_=consts_d[:, :]), writes=[cstb], dma=True)
        op("dve", lambda e: e.tensor_copy(out=cbf[:], in_=cst[:, 128:512]), reads=[cstb], writes=[cbfb])
        op("dve", lambda e: e.memset(epsT[:, 0:1], EPS), writes=[epsb])
        op("dve", lambda e: e.memset(epsT[:, 1:2], 1.0), writes=[epsb])

        with ExitStack() as ps_es:
            CW = 2816
            NST = 3
            stf = [sb("stf%d" % i, [128, CW], F32, ps_es) for i in range(NST)]
            stb = [sb("stb%d" % i, [128, CW], BF16, ps_es) for i in range(NST)]
            stfb = [Buf() for _ in range(NST)]
            stbb = [Buf() for _ in range(NST)]
            cnt = [0]
            cast_eng = ["dve", "act", "pool"]

            def prep(src, dst, npart, ncols):
                for c0 in range(0, ncols, CW):
                    c1 = min(ncols, c0 + CW)
                    w = c1 - c0
                    i = cnt[0] % NST
                    ce = cast_eng[cnt[0] % 3]
                    cnt[0] += 1
                    op("sp", lambda e: e.dma_start(out=stf[i][0:npart, 0:w], in_=src[:, c0:c1]),
                       writes=[stfb[i]], dma=True)
                    if ce == "act":
                        op("act", lambda e: e.activation(out=stb[i][0:npart, 0:w], in_=stf[i][0:npart, 0:w], func=AF.Copy),
                           reads=[stfb[i]], writes=[stbb[i]])
                    else:
                        op(ce, lambda e: e.tensor_copy(out=stb[i][0:npart, 0:w], in_=stf[i][0:npart, 0:w]),
                           reads=[stfb[i]], writes=[stbb[i]])
                    op("pool", lambda e: e.dma_start(out=dst[:, c0:c1], in_=stb[i][0:npart, 0:w]),
                       reads=[stbb[i]], dma=True)

            for l in range(NL):
                for kc in range(8):
                    prep(w_in_d[l, kc * 128:(kc + 1) * 128, :], s_in[l, :, kc, :], 128, 2056)
                for g in range(4):
                    prep(w_pool_d[l, g, :, :], s_pool[l, :, g, :], 128, 128)
                for kc in range(8):
                    prep(w_out_d[l, kc * 128:(kc + 1) * 128, :], s_out[l, :, kc, :], 128, 1024)
                for kc in range(8):
                    prep(w_mq_d[l, kc * 128:(kc + 1) * 128, :], s_mq[l, :, kc, :], 128, 512)
                    prep(w_mkv_d[l, kc * 128:(kc + 1) * 128, :], s_mkv[l, :, kc, :], 128, 1024)
                for hm in range(4):
                    prep(w_mo_d[l, hm * 128:(hm + 1) * 128, :], s_mo[l, :, hm, :], 128, 1024)
                for kc in range(8):
                    prep(w_gu_d[l, kc * 128:(kc + 1) * 128, :], s_gu[l, :, kc, :], 128, 2 * DFF)
                for fc in range(NFC):
                    prep(w_dn_d[l, fc * 128:(fc + 1) * 128, :], s_dn[l, :, fc, :], 128, 1024)
            sc.barrier()

        hsb = [Buf() for _ in range(NB)]

        def fm_norm(hT, hTb, hb, hbb, vec, vecb, gcol, n, A0, A0b, A1, A1b, bank, part=None):
            if part in (None, "A"):
                for kc in range(8):
                    op("act", lambda e: e.activation(out=hb[:, kc, 0:n], in_=hT[:, kc, 0:n], func=AF.Square),
                       reads=[hTb[kc]], writes=[hbb[kc]])
            if part == "A":
                return

            def mm(e):
                f = None
                for kc in range(8):
                    i = e.matmul(ps[bank][:, 0:n], ones_b, hb[:, kc, 0:n], start=(kc == 0), stop=(kc == 7))
                    f = i if f is None else f
                return f, i
            op("pe", mm, reads=hbb + [cbfb], writes=[pb[bank]])
            op("act", lambda e: e.activation(out=A0[:, 0:n], in_=ps[bank][:, 0:n], func=AF.Sqrt,
                                             bias=epsT[:, 0:1], scale=1.0 / D),
               reads=[epsb, pb[bank]], writes=[A0b])
            op("dve", lambda e: e.reciprocal(out=A1[:, 0:n], in_=A0[:, 0:n]), reads=[A0b], writes=[A1b])
            for kc in range(8):
                op("dve", lambda e: e.scalar_tensor_tensor(out=hb[:, kc, 0:n], in0=hT[:, kc, 0:n],
                                                           scalar=vec[:, gcol + kc:gcol + kc + 1], in1=A1[:, 0:n],
                                                           op0=ALU.mult, op1=ALU.mult),
                   reads=[hTb[kc], A1b, vecb], writes=[hbb[kc]])

        def head_norm(bank, bank2, stat_lhsT, inv_n, n, vec, gcol, vecb, dsts, ysq, ysqb, rA, rAb, rB, rBb):
            op("act", lambda e: e.activation(out=ysq[:, 0:n], in_=ps[bank][:, 0:n], func=AF.Square),
               reads=[pb[bank]], writes=[ysqb])
            op("pe", lambda e: e.matmul(ps[bank2][:, 0:n], stat_lhsT, ysq[:, 0:n], start=True, stop=True),
               reads=[ysqb, cbfb], writes=[pb[bank2]])
            op("act", lambda e: e.activation(out=rA[:, 0:n], in_=ps[bank2][:, 0:n], func=AF.Sqrt,
                                             bias=epsT[:, 0:1], scale=inv_n),
               reads=[epsb, pb[bank2]], writes=[rAb])
            op("dve", lambda e: e.reciprocal(out=rB[:, 0:n], in_=rA[:, 0:n]), reads=[rAb], writes=[rBb])
            for (r0_, r1_, dst, dstb) in dsts:
                op("dve", lambda e: e.scalar_tensor_tensor(out=dst, in0=ps[bank][r0_:r1_, 0:n], scalar=vec[r0_:r1_, gcol:gcol + 1],
                                                           in1=rB[r0_:r1_, 0:n], op0=ALU.mult, op1=ALU.mult),
                   reads=[rBb, pb[bank], vecb], writes=[dstb])

        def head_norm_multi(items, n, vec, vecb):
            for (bank, bank2, lhs, inv_n, gcol, dsts, ysq, ysqb, r, rb) in items:
                op("act", lambda e: e.activation(out=ysq[:, 0:n], in_=ps[bank][:, 0:n], func=AF.Square),
                   reads=[pb[bank]], writes=[ysqb])
            for (bank, bank2, lhs, inv_n, gcol, dsts, ysq, ysqb, r, rb) in items:
                op("pe", lambda e: e.matmul(ps[bank2][:, 0:n], lhs, ysq[:, 0:n], start=True, stop=True),
                   reads=[ysqb, cbfb], writes=[pb[bank2]])
            for (bank, bank2, lhs, inv_n, gcol, dsts, ysq, ysqb, r, rb) in items:
                op("act", lambda e: e.activation(out=r[:, 0:n], in_=ps[bank2][:, 0:n], func=AF.Sqrt,
                                                 bias=epsT[:, 0:1], scale=inv_n),
                   reads=[epsb, pb[bank2]], writes=[rb])
            for (bank, bank2, lhs, inv_n, gcol, dsts, ysq, ysqb, r, rb) in items:
                op("dve", lambda e: e.reciprocal(out=r[:, 0:n], in_=r[:, 0:n]), reads=[rb], writes=[rb])
            for (bank, bank2, lhs, inv_n, gcol, dsts, ysq, ysqb, r, rb) in items:
                for (r0_, r1_, dst, dstb) in dsts:
                    op("dve", lambda e: e.scalar_tensor_tensor(out=dst, in0=ps[bank][r0_:r1_, 0:n], scalar=vec[r0_:r1_, gcol:gcol + 1],
                                                               in1=r[r0_:r1_, 0:n], op0=ALU.mult, op1=ALU.mult),
                       reads=[rb, pb[bank], vecb], writes=[dstb])

        def pipeline(stage_lists, delays):
            n = len(stage_lists)
            for t in range(n + max(delays)):
                for k, d in enumerate(delays):
                    i = t - d
                    if 0 <= i < n:
                        stage_lists[i][k]()

        def norm_stages(bank, bank2, lhs, inv_n, n, vec, vecb, gcol, dsts, ysq, ysqb, r, rb):
            def sq():
                op("act", lambda e: e.activation(out=ysq[:, 0:n], in_=ps[bank][:, 0:n], func=AF.Square),
                   reads=[pb[bank]], writes=[ysqb])

            def st():
                op("pe", lambda e: e.matmul(ps[bank2][:, 0:n], lhs, ysq[:, 0:n], start=True, stop=True),
                   reads=[ysqb, cbfb], writes=[pb[bank2]])

            def sr():
                op("act", lambda e: e.activation(out=r[:, 0:n], in_=ps[bank2][:, 0:n], func=AF.Sqrt,
                                                 bias=epsT[:, 0:1], scale=inv_n),
                   reads=[epsb, pb[bank2]], writes=[rb])

            def rc():
                op("dve", lambda e: e.reciprocal(out=r[:, 0:n], in_=r[:, 0:n]), reads=[rb], writes=[rb])

            def stt():
                for (r0_, r1_, dst, dstb) in dsts:
                    op("dve", lambda e: e.scalar_tensor_tensor(out=dst, in0=ps[bank][r0_:r1_, 0:n], scalar=vec[r0_:r1_, gcol:gcol + 1],
                                                               in1=r[r0_:r1_, 0:n], op0=ALU.mult, op1=ALU.mult),
                       reads=[rb, pb[bank], vecb], writes=[dstb])
            return [sq, st, sr, rc, stt]

        for l in range(NL):
            first_layer = (l == 0)
            last_layer = (l == NL - 1)
            with ExitStack() as mx:
                vec = sb("vec", [128, NVEC], F32, mx)
                vecb = Buf()
                op("sp", lambda e: e.dma_start(out=vec[:], in_=vecs_d[l, :, :]), writes=[vecb], dma=True)
                wf = sb("wf", [128, 8, 8], BF16, mx)
                wfb = Buf()
                op("sp", lambda e: e.dma_start(out=wf[:], in_=s_in[l, :, :, 2048:2056]), writes=[wfb], dma=True)
                wpl = sb("wpl", [128, 4, 128], BF16, mx)
                wplb = Buf()
                op("sp", lambda e: e.dma_start(out=wpl[:], in_=s_pool[l, :, :, :]), writes=[wplb], dma=True)

                hT = sb("hT", [128, 8, 512], F32, mx)
                hTb = [Buf() for _ in range(8)]
                hb = sb("hb", [128, 8, 512], BF16, mx)
                hbb = [Buf() for _ in range(8)]
                NF = 6
                ft = [sb("ft%d" % i, [128, 512], F32, mx) for i in range(NF)]
                ftb = [Buf() for _ in range(NF)]
                ftr = RR(range(NF))
                qT = sb("qT", [128, 8, 512], BF16, mx)
                qTb = [Buf() for _ in range(8)]
                kcur = sb("kcur", [128, 4, 512], BF16, mx)
                kcb = [Buf() for _ in range(4)]
                vcur = sb("vcur", [128, 4, 512], BF16, mx)
                vcb = [Buf() for _ in range(4)]
                NR = 4
                ring = [sb("ring%d" % i, [128, 4096], BF16, mx) for i in range(NR)]
                ringb = [Buf() for _ in range(NR)]
                NPT = 4
                PT = [sb("PT%d" % i, [128, 512], BF16, mx) for i in range(NPT)]
                PTb = [Buf() for _ in range(NPT)]
                ptr = RR(range(NPT))
                biasb = [sb("biasb%d" % i, [128, NT], F32, mx) for i in range(2)]
                biasbb = [Buf() for _ in range(2)]
                foxT = sb("foxT", [128, 4, 512], BF16, mx)
                foxb = [Buf() for _ in range(8)]
                pin = sb("pin", [128, 4, 528], F32, mx)
                pinb = [Buf() for _ in range(4)]
                ptmp = [sb("ptmp%d" % i, [128, 528], F32, mx) for i in range(2)]
                ptmpb = [Buf() for _ in range(2)]
                mixed = sb("mixed", [128, 4, 512], BF16, mx)
                mixb = [Buf() for _ in range(4)]
                poolT = sb("poolT", [128, 4, 512], BF16, mx)
                poolb = [Buf() for _ in range(4)]
                ysq = [sb("ysq%d" % i, [128, 512], BF16, mx) for i in range(4)]
                ysqb = [Buf() for _ in range(4)]
                mqT = [sb("mqT%d" % i, [128, 512], BF16, mx) for i in range(4)]
                mqTb = [Buf() for _ in range(4)]
                xin = [sb("xin%d" % i, [128, 512], F32, mx) for i in range(3)]
                xinb = [Buf() for _ in range(3)]
                xr = RR(range(3))
                negc = sb("negc", [128, NT, 8], F32, mx)
                negcb = Buf()
                cend = sb("cend", [128, NT + 1, 8], F32, mx)
                cendb = Buf()
                fl = sb("fl", [128, 3, 32], F32, mx)
                flb = [Buf() for _ in range(3)]
                mkT = sb("mkT", [128, 4, MEM], BF16, mx)
                mkTb = Buf()
                mv = sb("mv", [128, 2, 512], BF16, mx)
                mvb = Buf()
                ksb = [Buf() for _ in range(NB)]
                vsb = [Buf() for _ in range(NB)]

                bank_proj = RR([0, 1])
                bank_stat = RR([7, 4])
                bank_S = RR([2, 3, 4, 1])
                bank_O = RR([5, 6])
                bank_R = RR([7, 0])

                op("pool", lambda e: e.memset(qT[:], 0.0), writes=qTb)
                op("dve", lambda e: e.memset(cend[:, 0, :], 0.0), writes=[cendb])
                op("pool", lambda e: e.memset(pin[:, :, 0:16], 0.0), writes=pinb)

                chunks = []
                for b in range(NB):
                    chunks.append(("inQ", s_in[l, :, :, 0:512], 128, (8, 512)))
                    chunks.append(("inK", s_in[l, :, :, 512:1024], 128, (8, 512)))
                    chunks.append(("inV", s_in[l, :, :, 1024:1536], 128, (8, 512)))
                    chunks.append(("inP", s_in[l, :, :, 1536:2048], 128, (8, 512)))
                    for hf in range(2):
                        chunks.append(("out", s_out[l, :, :, hf * 512:(hf + 1) * 512], 128, (8, 512)))
                    chunks.append(("mq", s_mq[l, :, :, :], 128, (8, 512)))
                    chunks.append(("mo", s_mo[l, :, :, :], 128, (4, 1024)))
                wstate = {"issued": 0, "cur": 0}

                def wview(k):
                    name, src, npart, (a, bcols) = chunks[k]
                    return ring[k % NR][0:npart, 0:a * bcols].rearrange("p (a b) -> p a b", a=a)

                def wissue(upto):
                    while wstate["issued"] < min(upto, len(chunks)):
                        k = wstate["issued"]
                        src = chunks[k][1]
                        dst = wview(k)
                        op("sp", lambda e: e.dma_start(out=dst, in_=src), writes=[ringb[k % NR]], dma=True)
                        wstate["issued"] += 1

                def wnext(name):
                    k = wstate["cur"]
                    assert chunks[k][0] == name, (chunks[k][0], name)
                    wissue(k + 1)
                    wstate["cur"] += 1
                    return wview(k), ringb[k % NR], k

                def wprefetch():
                    wissue(wstate["cur"] + NR)

                with ExitStack() as mk:
                    memT = sb("memT", [128, 8, MEM], F32, mk)
                    memTb = [Buf() for _ in range(8)]
                    mnb = sb("mnb", [128, 8, MEM], BF16, mk)
                    mnbb = [Buf() for _ in range(8)]
                    wkv = sb("wkv", [128, 8, 1024], BF16, mk)
                    wkvb = Buf()
                    op("sp", lambda e: e.dma_start(out=wkv[:], in_=s_mkv[l, :, :, :]), writes=[wkvb], dma=True)
                    for i in range(2):
                        for hf in range(2):
                            xi = xr.next()
                            op("pool", lambda e: e.dma_start(out=xin[xi][:], in_=mem_d[i * 128:(i + 1) * 128, hf * 512:(hf + 1) * 512]),
                               writes=[xinb[xi]], dma=True)
                            bk = bank_proj.next()

                            def tr(e):
                                f = None
                                for k in range(4):
                                    ins = e.transpose(ps[bk][:, k * 128:(k + 1) * 128], xin[xi][:, k * 128:(k + 1) * 128], ident)
                                    f = ins if f is None else f
                                return f, ins
                            op("pe", tr, reads=[xinb[xi], cstb], writes=[pb[bk]])
                            op("dve", lambda e: e.tensor_copy(out=memT[:, hf * 4:(hf + 1) * 4, i * 128:(i + 1) * 128],
                                                              in_=ps[bk][:, 0:512].rearrange("p (k t) -> p k t", k=4)),
                               reads=[pb[bk]], writes=memTb[hf * 4:(hf + 1) * 4])
                    a0, a1 = ftr.next(), ftr.next()
                    fm_norm(memT, memTb, mnb, mnbb, vec, vecb, 24, MEM, ft[a0], ftb[a0], ft[a1], ftb[a1], bank_stat.next())
                    for hm in range(4):
                        bk = bank_proj.next()

                        def mmk(e):
                            f = None
                            for kc in range(8):
                                ins = e.matmul(ps[bk][:, 0:MEM], wkv[:, kc, hm * 128:(hm + 1) * 128], mnb[:, kc, :],
                                               start=(kc == 0), stop=(kc == 7))
                                f = ins if f is None else f
                            return f, ins
                        op("pe", mmk, reads=mnbb + [wkvb], writes=[pb[bk]])
                        a0, a1 = ftr.next(), ftr.next()
                        head_norm(bk, bank_stat.next(), ones_b, 1.0 / 128, MEM, vec, 35, vecb, [(0, 128, mkT[:, hm, :], mkTb)],
                                  ysq[hm % 2], ysqb[hm % 2], ft[a0], ftb[a0], ft[a1], ftb[a1])
                    for mt in range(2):
                        bk = bank_proj.next()

                        def mmv(e):
                            f = None
                            for kc in range(8):
                                ins = e.matmul(ps[bk][:, :], mnb[:, kc, mt * 128:(mt + 1) * 128], wkv[:, kc, 512:1024],
                                               start=(kc == 0), stop=(kc == 7))
                                f = ins if f is None else f
                            return f, ins
                        op("pe", mmv, reads=mnbb + [wkvb], writes=[pb[bk]])
                        op("dve", lambda e: e.tensor_copy(out=mv[:, mt, :], in_=ps[bk][:, :]), reads=[pb[bk]], writes=[mvb])
                    sc.barrier()

                NKS = max(NB - 1, 1)
                Kst = [sb("Kst%d" % i, [128, NKS * 512], BF16, mx) for i in range(2)]
                Kstb = [Buf() for _ in range(2)]
                Vst = [sb("Vst%d" % i, [128, NKS * 4, 128], BF16, mx) for i in range(2)]
                Vstb = [Buf() for _ in range(2)]

                for b in range(NB):
                    t0 = b * 512
                    wprefetch()
                    if first_layer:
                        for i in range(4):
                            for hf in range(2):
                                xi = xr.next()
                                op("pool", lambda e: e.dma_start(out=xin[xi][:], in_=x_d[t0 + i * 128:t0 + (i + 1) * 128, hf * 512:(hf + 1) * 512]),
                                   writes=[xinb[xi]], dma=True)
                                bk = bank_proj.next()

                                def tr(e):
                                    f = None
                                    for k in range(4):
                                        ins = e.transpose(ps[bk][:, k * 128:(k + 1) * 128], xin[xi][:, k * 128:(k + 1) * 128], ident)
                                        f = ins if f is None else f
                                    return f, ins
                                op("pe", tr, reads=[xinb[xi], cstb], writes=[pb[bk]])
                                op("dve", lambda e: e.tensor_copy(out=hT[:, hf * 4:(hf + 1) * 4, i * 128:(i + 1) * 128],
                                                                  in_=ps[bk][:, 0:512].rearrange("p (k t) -> p k t", k=4)),
                                   reads=[pb[bk]], writes=hTb[hf * 4:(hf + 1) * 4])
                    else:
                        op("pool", lambda e: e.dma_start(out=hT[:], in_=hscr_v[:, :, t0:t0 + 512]),
                           reads=[hsb[b]], writes=hTb, dma=True)

                    a0, a1 = ftr.next(), ftr.next()
                    fm_norm(hT, hTb, hb, hbb, vec, vecb, 0, 512, ft[a0], ftb[a0], ft[a1], ftb[a1], bank_stat.next())

                    wq_, wqb_, _ = wnext("inQ")
                    wk_, wkb_, _ = wnext("inK")
                    ysl = [(ysq[i], ysqb[i]) for i in range(4)]
                    stage_lists = []
                    for it in range(8):
                        which, c = ("inQ", it) if it < 4 else ("inK", it - 4)
                        wv_, wb_ = (wq_, wqb_) if it < 4 else (wk_, wkb_)
                        bk = it % 4
                        if which == "inQ":
                            dsts, gcol = [(0, 64, qT[0:64, 2 * c, :], qTb[2 * c]), (64, 128, qT[64:128, 2 * c + 1, :], qTb[2 * c + 1])], 32
                        else:
                            dsts, gcol = [(0, 128, kcur[:, c, :], kcb[c])], 33

                        def proj(bk=bk, wv_=wv_, wb_=wb_, c=c):
                            def mmq(e):
                                f = None
                                for kc in range(8):
                                    ins = e.matmul(ps[bk][:, :], wv_[:, kc, c * 128:(c + 1) * 128], hb[:, kc, :],
                                                   start=(kc == 0), stop=(kc == 7))
                                    f = ins if f is None else f
                                return f, ins
                            op("pe", mmq, reads=hbb + [wb_], writes=[pb[bk]])
                        ys, ysb_ = ysl[it % 4]
                        stage_lists.append([proj] + norm_stages(bk, 4 + (it % 2), bd64_b, 1.0 / 64, 512, vec, vecb, gcol, dsts,
                                                                ys, ysb_, ft[it % 4], ftb[it % 4]))
                    pipeline(stage_lists, [0, 1, 1, 2, 2, 3])
                    wprefetch()
                    if b < NB - 1:
                        op("pool", lambda e: e.dma_start(out=kscr_v[:, :, t0:t0 + 512], in_=kcur[:]),
                           reads=kcb, writes=[ksb[b]], dma=True)

                    wv_, wb_, _ = wnext("inV")
                    BF = 4
                    for i in range(4):
                        bk = i

                        def mmv2(e):
                            f = None
                            for kc in range(8):
                                ins = e.matmul(ps[bk][:, :], hb[:, kc, i * 128:(i + 1) * 128], wv_[:, kc, :],
                                               start=(kc == 0), stop=(kc == 7))
                                f = ins if f is None else f
                            return f, ins
                        op("pe", mmv2, reads=hbb + [wb_], writes=[pb[bk]])

                    def mmf(e):
                        f = None
                        for i in range(4):
                            for kc in range(8):
                                ins = e.matmul(ps[BF][:, i * 8:(i + 1) * 8], hb[:, kc, i * 128:(i + 1) * 128], wf[:, kc, :],
                                               start=(kc == 0), stop=(kc == 7))
                                f = ins if f is None else f
                        return f, ins
                    op("pe", mmf, reads=hbb + [wfb], writes=[pb[BF]])
                    for i in range(4):
                        op("dve", lambda e: e.tensor_copy(out=vcur[:, i, :], in_=ps[i][:, :]),
                           reads=[pb[i]], writes=[vcb[i]])
                    op("dve", lambda e: e.tensor_tensor(out=fl[:, 0, :], in0=ps[BF][:, 0:32], in1=vec[:, 40:72], op=ALU.add),
                       reads=[pb[BF], vecb], writes=[flb[0]])
                    op("act", lambda e: e.activation(out=fl[:, 1, :], in_=fl[:, 0, :], func=AF.Exp, scale=-1.0),
                       reads=[flb[0]], writes=[flb[1]])
                    op("act", lambda e: e.activation(out=fl[:, 2, :], in_=fl[:, 1, :], func=AF.Ln, bias=epsT[:, 1:2], scale=1.0),
                       reads=[flb[1], epsb], writes=[flb[2]])
                    BC = 5

                    def mmc(e):
                        f = None
                        for i in range(4):
                            ins = e.matmul(ps[BC][:, i * 8:(i + 1) * 8], ident, cend[:, 4 * b, :], start=True, stop=False)
                            f = ins if f is None else f
                            for i2 in range(i):
                                e.matmul(ps[BC][:, i * 8:(i + 1) * 8], ones_f, fl[:, 2, i2 * 8:(i2 + 1) * 8], start=False, stop=False)
                            e.matmul(ps[BC][:, i * 8:(i + 1) * 8], triu_f, fl[:, 2, i * 8:(i + 1) * 8], start=False, stop=True)
                            e.matmul(ps[BC][:, 32 + i * 8:32 + (i + 1) * 8], ident, cend[:, 4 * b, :], start=True, stop=False)
                            for i2 in range(i + 1):
                                ins = e.matmul(ps[BC][:, 32 + i * 8:32 + (i + 1) * 8], ones_f, fl[:, 2, i2 * 8:(i2 + 1) * 8],
                                               start=False, stop=(i2 == i))
                        return f, ins
                    op("pe", mmc, reads=[flb[2], cstb, cendb], writes=[pb[BC]])
                    op("dve", lambda e: e.tensor_copy(out=negc[:, 4 * b:4 * b + 4, :], in_=ps[BC][:, 0:32].rearrange("p (t h) -> p t h", t=4)),
                       reads=[pb[BC]], writes=[negcb])
                    op("dve", lambda e: e.tensor_copy(out=cend[:, 4 * b + 1:4 * b + 5, :], in_=ps[BC][:, 32:64].rearrange("p (t h) -> p t h", t=4)),
                       reads=[pb[BC]], writes=[cendb])
                    if b < NB - 1:
                        for c in range(4):
                            op("pool", lambda e: e.dma_start(out=vscr[c, :, 4 * b:4 * b + 4, :],
                                                             in_=vcur[:, :, c * 128:(c + 1) * 128]),
                               reads=vcb, writes=[vsb[b]], dma=True)
                    wprefetch()

                    wv_, wb_, _ = wnext("inP")
                    for g in range(4):
                        bk = bank_proj.next()

                        def mmp(e):
                            f = None
                            for kc in range(8):
                                ins = e.matmul(ps[bk][:, :], wv_[:, kc, g * 128:(g + 1) * 128], hb[:, kc, :],
                                               start=(kc == 0), stop=(kc == 7))
                                f = ins if f is None else f
                            return f, ins
                        op("pe", mmp, reads=hbb + [wb_], writes=[pb[bk]])
                        op("act", lambda e: e.activation(out=pin[:, g, 16:528], in_=ps[bk][:, :], func=AF.Copy),
                           reads=[pb[bk]], writes=[pinb[g]])
                    wprefetch()

                    for g in range(4):
                        w = 2 << g
                        cur, curb = pin[:, g, :], pinb[g]
                        lo, off = 0, 1
                        for st in range(g + 1):
                            nx, nxb = ptmp[st % 2], ptmpb[st % 2]
                            lo2 = lo + off
                            op("pool", lambda e: e.tensor_tensor(out=nx[:, lo2:528], in0=cur[:, lo2:528], in1=cur[:, lo:528 - off], op=ALU.add),
                               reads=[curb], writes=[nxb])
                            cur, curb = nx[:, :], nxb
                            lo, off = lo2, off * 2
                        op("dve", lambda e: e.scalar_tensor_tensor(out=mixed[:, g, :], in0=cur[:, 16:528], scalar=1.0 / w,
                                                                    in1=pin[:, g, 16:528], op0=ALU.mult, op1=ALU.subtract),
                           reads=[curb, pinb[g]], writes=[mixb[g]])
                        if b == 0:
                            oth, othb = ptmp[(g + 1) % 2], ptmpb[(g + 1) % 2]
                            op("pool", lambda e: e.tensor_tensor(out=oth[:, 0:16], in0=cur[:, 16:32], in1=invcnt[:, g * 16:(g + 1) * 16], op=ALU.mult),
                               reads=[curb, cstb], writes=[othb])
                            op("pool", lambda e: e.tensor_tensor(out=mixed[:, g, 0:16], in0=oth[:, 0:16], in1=pin[:, g, 16:32], op=ALU.subtract),
                               reads=[othb, pinb[g]], writes=[mixb[g]])
                    op("pool", lambda e: e.tensor_copy(out=pin[:, :, 0:16], in_=pin[:, :, 512:528]), reads=[], writes=pinb)

                    nj = 4 * b + 4
                    items = [(h, j) for h in range(8) for j in range(nj)]
                    SKEW = 3
                    st_ = {}
                    obank = {}
                    pend = []

                    def stage_load(c):
                        if b == 0:
                            return
                        i = c % 2
                        op("sp", lambda e: e.dma_start(out=Kst[i][:, 0:t0], in_=kscr[c, :, 0:t0]),
                           reads=ksb[0:b], writes=[Kstb[i]], dma=True)
                        op("sp", lambda e: e.dma_start(out=Vst[i][:, 0:4 * b, :], in_=vscr[c, :, 0:4 * b, :]),
                           reads=vsb[0:b], writes=[Vstb[i]], dma=True)

                    def emit_S(h, j):
                        c, r0 = h // 2, (h % 2) * 64
                        if j == 0:
                            bi = h % 2
                            op("dve", lambda e: e.tensor_scalar(out=biasb[bi][:, 0:nj], in0=negc[:, 0:nj, h],
                                                                scalar1=cend[:, 4 * b + 2, h:h + 1], scalar2=None, op0=ALU.subtract),
                               reads=[negcb, cendb], writes=[biasbb[bi]])
                        if j < 4 * b:
                            lhsT = Kst[c % 2][:, j * 128:(j + 1) * 128]
                            kb = Kstb[c % 2]
                            col0 = 0
                        else:
                            i = j - 4 * b
                            lhsT = kcur[:, c, i * 128:(i + 1) * 128]
                            kb = kcb[c]
                            col0 = i * 128
                        bs = bank_S.next()
                        op("pe", lambda e: e.matmul(ps[bs][:, col0:512], lhsT, qT[:, h, col0:512], start=True, stop=True),
                           reads=[kb, qTb[h]], writes=[pb[bs]])
                        pt = ptr.next()
                        op("act", lambda e: e.activation(out=PT[pt][:, col0:512], in_=ps[bs][:, col0:512], func=AF.Exp,
                                                         bias=biasb[h % 2][:, j:j + 1], scale=0.125),
                           reads=[pb[bs], biasbb[h % 2]], writes=[PTb[pt]])
                        if j >= 4 * b:
                            op("dve", lambda e: e.tensor_tensor(out=PT[pt][:, col0:col0 + 128], in0=PT[pt][:, col0:col0 + 128],
                                                                in1=triu_b, op=ALU.mult),
                               reads=[cbfb], writes=[PTb[pt]])
                        st_[(h, j)] = (pt, col0)

                    def emit_PV(h, j):
                        c = h // 2
                        pt, col0 = st_.pop((h, j))
                        if j == 0:
                            obank[h] = (bank_O.next(), bank_R.next())
                        bo, br = obank[h]
                        if j < 4 * b:
                            lhsT = Vst[c % 2][:, j, :]
                            vb_ = Vstb[c % 2]
                        else:
                            lhsT = vcur[:, j - 4 * b, c * 128:(c + 1) * 128]
                            vb_ = vcb[j - 4 * b]

                        def mm2(e):
                            i1 = e.matmul(ps[bo][:, col0:512], lhsT, PT[pt][:, col0:512], start=(j == 0), stop=(j == nj - 1))
                            i2 = e.matmul(ps[br][:, col0:512], ones_b, PT[pt][:, col0:512], start=(j == 0), stop=(j == nj - 1))
                            return i1, i2
                        op("pe", mm2, reads=[PTb[pt], vb_, cbfb], writes=[pb[bo], pb[br]])
                        if j == nj - 1:
                            r0_ = (h % 2) * 64
                            r1 = ftr.next()
                            op("dve", lambda e: e.reciprocal(out=ft[r1][r0_:r0_ + 64, :], in_=ps[br][r0_:r0_ + 64, :]),
                               reads=[pb[br]], writes=[ftb[r1]])
                            op("dve", lambda e: e.tensor_tensor(out=foxT[r0_:r0_ + 64, c, :], in0=ps[bo][r0_:r0_ + 64, :],
                                                                in1=ft[r1][r0_:r0_ + 64, :], op=ALU.mult),
                               reads=[pb[bo], ftb[r1]], writes=[foxb[h]])

                    def tick():
                        for p in list(pend):
                            p[0] -= 1
                            if p[0] <= 0:
                                p[1]()
                                pend.remove(p)

                    stage_load(0)
                    stage_load(1)
                    for idx in range(len(items) + SKEW):
                        if idx < len(items):
                            emit_S(*items[idx])
                        if idx >= SKEW:
                            h_, j_ = items[idx - SKEW]
                            emit_PV(h_, j_)
                            if j_ == nj - 1 and h_ % 2 == 1 and h_ // 2 + 2 < 4:
                                stage_load(h_ // 2 + 2)
                        tick()
                    while pend:
                        tick()

                    for g in range(4):
                        bk = bank_proj.next()
                        op("pe", lambda e: e.matmul(ps[bk][:, :], wpl[:, g, :], mixed[:, g, :], start=True, stop=True),
                           reads=[mixb[g], wplb], writes=[pb[bk]])
                        op("act", lambda e: e.activation(out=poolT[:, g, :], in_=ps[bk][:, :], func=AF.Copy, scale=vec[:, 36 + g:37 + g]),
                           reads=[pb[bk], vecb], writes=[poolb[g]])

                    for hf in range(2):
                        wW, wWb, _ = wnext("out")
                        for oc in range(4):
                            kc = hf * 4 + oc
                            bk = bank_proj.next()

                            def mmo(e):
                                f = None
                                for c in range(4):
                                    ins = e.matmul(ps[bk][:, :], wW[:, c, oc * 128:(oc + 1) * 128], foxT[:, c, :],
                                                   start=(c == 0), stop=False)
                                    f = ins if f is None else f
                                for g in range(4):
                                    ins = e.matmul(ps[bk][:, :], wW[:, 4 + g, oc * 128:(oc + 1) * 128], poolT[:, g, :],
                                                   start=False, stop=(g == 3))
                                return f, ins
                            op("pe", mmo, reads=foxb + poolb + [wWb], writes=[pb[bk]])
                            op("dve", lambda e: e.tensor_tensor(out=hT[:, kc, :], in0=ps[bk][:, :], in1=hT[:, kc, :], op=ALU.add),
                               reads=[pb[bk], hTb[kc]], writes=[hTb[kc]])
                        wprefetch()

                    a0, a1 = ftr.next(), ftr.next()
                    fm_norm(hT, hTb, hb, hbb, vec, vecb, 8, 512, ft[a0], ftb[a0], ft[a1], ftb[a1], bank_stat.next())
                    wQ, wQb, _ = wnext("mq")
                    moT = foxT
                    stage_lists = []
                    for hm in range(4):
                        bk = hm
                        p0, p1 = 2 * (hm % 2), 2 * (hm % 2) + 1
                        r1 = 4 + (hm % 2)
                        bst = 4 + (hm % 2)

                        def proj(bk=bk, hm=hm):
                            def mmq2(e):
                                f = None
                                for kc in range(8):
                                    ins = e.matmul(ps[bk][:, :], wQ[:, kc, hm * 128:(hm + 1) * 128], hb[:, kc, :],
                                                   start=(kc == 0), stop=(kc == 7))
                                    f = ins if f is None else f
                                return f, ins
                            op("pe", mmq2, reads=hbb + [wQb], writes=[pb[bk]])

                        def scores(hm=hm):
                            for mt in range(2):
                                op("pe", lambda e: e.matmul(ps[6 + mt][:, :], mkT[:, hm, mt * 128:(mt + 1) * 128], mqT[hm][:, :], start=True, stop=True),
                                   reads=[mkTb, mqTb[hm]], writes=[pb[6 + mt]])

                        def exps(hm=hm, p0=p0):
                            for mt in range(2):
                                op("act", lambda e: e.activation(out=PT[p0 + mt][:, :], in_=ps[6 + mt][:, :], func=AF.Exp, scale=128.0 ** -0.5),
                                   reads=[pb[6 + mt]], writes=[PTb[p0 + mt]])

                        def pv(hm=hm, bk=bk, p0=p0, p1=p1, bst=bst):
                            def mmpv(e):
                                i1 = e.matmul(ps[bk][:, :], mv[:, 0, hm * 128:(hm + 1) * 128], PT[p0][:, :], start=True, stop=False)
                                e.matmul(ps[bk][:, :], mv[:, 1, hm * 128:(hm + 1) * 128], PT[p1][:, :], start=False, stop=True)
                                e.matmul(ps[bst][:, :], ones_b, PT[p0][:, :], start=True, stop=False)
                                i2 = e.matmul(ps[bst][:, :], ones_b, PT[p1][:, :], start=False, stop=True)
                                return i1, i2
                            op("pe", mmpv, reads=[mvb, cbfb, PTb[p0], PTb[p1]], writes=[pb[bk], pb[bst]])

                        def fin(hm=hm, bk=bk, r1=r1, bst=bst):
                            op("dve", lambda e: e.reciprocal(out=ft[r1][:, :], in_=ps[bst][:, :]), reads=[pb[bst]], writes=[ftb[r1]])
                            op("dve", lambda e: e.tensor_tensor(out=moT[:, hm, :], in0=ps[bk][:, :], in1=ft[r1][:, :], op=ALU.mult),
                               reads=[pb[bk], ftb[r1]], writes=[foxb[2 * hm], foxb[2 * hm + 1]])
                        stage_lists.append([proj] + norm_stages(bk, bst, ones_b, 1.0 / 128, 512, vec, vecb, 34,
                                                                [(0, 128, mqT[hm][:, :], mqTb[hm])], ysq[hm], ysqb[hm], ft[hm], ftb[hm])
                                           + [scores, exps, pv, fin])
                    pipeline(stage_lists, [0, 1, 1, 2, 2, 3, 3, 3, 4, 4])
                    wprefetch()
                    wO, wOb, _ = wnext("mo")
                    for oc in range(8):
                        bk = bank_proj.next()

                        def mmo2(e):
                            f = None
                            for hm in range(4):
                                ins = e.matmul(ps[bk][:, :], wO[:, hm, oc * 128:(oc + 1) * 128], moT[:, hm, :],
                                               start=(hm == 0), stop=(hm == 3))
                                f = ins if f is None else f
                            return f, ins
                        op("pe", mmo2, reads=foxb + [wOb], writes=[pb[bk]])
                        op("dve", lambda e: e.tensor_tensor(out=hT[:, oc, :], in0=ps[bk][:, :], in1=hT[:, oc, :], op=ALU.add),
                           reads=[pb[bk], hTb[oc]], writes=[hTb[oc]])
                    wprefetch()
                    op("pool", lambda e: e.dma_start(out=hscr_v[:, :, t0:t0 + 512], in_=hT[:]),
                       reads=hTb, writes=[hsb[b]], dma=True)
                sc.barrier()

            with ExitStack() as fx:
                vec = sb("vecF", [128, NVEC], F32, fx)
                vecb = Buf()
                op("sp", lambda e: e.dma_start(out=vec[:], in_=vecs_d[l, :, :]), writes=[vecb], dma=True)
                wgu = sb("wgu", [128, 8, 2 * DFF], BF16, fx)
                wgub = [Buf() for _ in range(8)]
                wdn = sb("wdn", [128, NFC, 1024], BF16, fx)
                wdnb = [Buf() for _ in range(NFC)]
                for kc in range(8):
                    op("sp", lambda e: e.dma_start(out=wgu[:, kc, :], in_=s_gu[l, :, kc, :]), writes=[wgub[kc]], dma=True)
                for f0 in range(0, NFC, 6):
                    f1 = min(NFC, f0 + 6)
                    op("sp", lambda e: e.dma_start(out=wdn[:, f0:f1, :], in_=s_dn[l, :, f0:f1, :]), writes=wdnb[f0:f1], dma=True)
                hT2 = [sb("hTF%d" % i, [128, 8, 512], F32, fx) for i in range(2)]
                hT2b = [[Buf() for _ in range(8)] for _ in range(2)]
                hb = sb("hbF", [128, 8, 512], BF16, fx)
                hbb = [Buf() for _ in range(8)]
                actT = sb("actT", [128, NFC, 512], BF16, fx)
                actb = [Buf() for _ in range(NFC)]
                sg = [sb("sg%d" % i, [128, 512], F32, fx) for i in range(2)]
                sgb = [Buf() for _ in range(2)]
                fa = [sb("fa%d" % i, [128, 512], F32, fx) for i in range(2)]
                fab = [Buf() for _ in range(2)]
                ot, otb = sg, sgb
                otr = RR(range(2))
                bank_gu = RR([0, 1, 2, 3])
                bank_dn = RR([4, 5])
                bank_st = RR([6, 7])

                def load_h(b):
                    op("pool", lambda e: e.dma_start(out=hT2[b % 2][:], in_=hscr_v[:, :, b * 512:(b + 1) * 512]),
                       reads=[hsb[b]], writes=hT2b[b % 2], dma=True)

                def norm3(b, part):
                    fm_norm(hT2[b % 2], hT2b[b % 2], hb, hbb, vec, vecb, 16, 512, fa[0], fab[0], fa[1], fab[1], 6 + (b % 2), part=part)

                load_h(0)
                norm3(0, None)
                for b in range(NB):
                    t0 = b * 512
                    hT, hTb = hT2[b % 2], hT2b[b % 2]
                    if b + 1 < NB:
                        load_h(b + 1)
                    for fc in range(NFC):
                        bg, bu = bank_gu.next(), bank_gu.next()

                        def mmg(e):
                            f = None
                            for kc in range(8):
                                ins = e.matmul(ps[bg][:, :], wgu[:, kc, fc * 128:(fc + 1) * 128], hb[:, kc, :],
                                               start=(kc == 0), stop=(kc == 7))
                                f = ins if f is None else f
                            return f, ins
                        op("pe", mmg, reads=hbb + wgub, writes=[pb[bg]])

                        def mmu(e):
                            f = None
                            for kc in range(8):
                                ins = e.matmul(ps[bu][:, :], wgu[:, kc, DFF + fc * 128:DFF + (fc + 1) * 128], hb[:, kc, :],
                                               start=(kc == 0), stop=(kc == 7))
                                f = ins if f is None else f
                            return f, ins
                        op("pe", mmu, reads=hbb + wgub, writes=[pb[bu]])
                        si = fc % 2
                        op("act", lambda e: e.activation(out=sg[si][:, :], in_=ps[bg][:, :], func=AF.Silu),
                           reads=[pb[bg]], writes=[sgb[si]])
                        op("dve", lambda e: e.tensor_tensor(out=actT[:, fc, :], in0=ps[bu][:, :], in1=sg[si][:, :], op=ALU.mult),
                           reads=[pb[bu], sgb[si]], writes=[actb[fc]])
                    if b + 1 < NB:
                        norm3(b + 1, "A")
                    for oc in range(8):
                        bk = bank_dn.next()

                        def mmd(e):
                            f = None
                            for fc in range(NFC):
                                ins = e.matmul(ps[bk][:, :], wdn[:, fc, oc * 128:(oc + 1) * 128], actT[:, fc, :],
                                               start=(fc == 0), stop=(fc == NFC - 1))
                                f = ins if f is None else f
                            return f, ins
                        op("pe", mmd, reads=actb + wdnb, writes=[pb[bk]])
                        op("dve", lambda e: e.tensor_tensor(out=hT[:, oc, :], in0=ps[bk][:, :], in1=hT[:, oc, :], op=ALU.add),
                           reads=[pb[bk], hTb[oc]], writes=[hTb[oc]])
                        if oc == 3 and b + 1 < NB:
                            norm3(b + 1, "B")
                    if last_layer:
                        for i in range(4):
                            for hf in range(2):
                                bk = bank_st.next()

                                def trb(e):
                                    f = None
                                    for k in range(4):
                                        ins = e.transpose(ps[bk][:, k * 128:(k + 1) * 128], hT[:, hf * 4 + k, i * 128:(i + 1) * 128], ident)
                                        f = ins if f is None else f
                                    return f, ins
                                op("pe", trb, reads=hTb[hf * 4:(hf + 1) * 4] + [cstb], writes=[pb[bk]])
                                oi = otr.next()
                                op("act", lambda e: e.activation(out=ot[oi][:, :], in_=ps[bk][:, :], func=AF.Copy),
                                   reads=[pb[bk]], writes=[otb[oi]])
                                op("pool", lambda e: e.dma_start(out=out_d[t0 + i * 128:t0 + (i + 1) * 128, hf * 512:(hf + 1) * 512], in_=ot[oi][:, :]),
                                   reads=[otb[oi]], dma=True)
                    else:
                        op("pool", lambda e: e.dma_start(out=hscr_v[:, :, t0:t0 + 512], in_=hT[:]),
                           reads=hTb, writes=[hsb[b]], dma=True)
                sc.barrier()
        sc.finish()
    return nc


def make_consts():
    c = np.zeros((128, 640), np.float32)
    c[:, 0:128] = np.eye(128, dtype=np.float32)
    s = np.arange(128)[:, None]
    t = np.arange(128)[None, :]
    c[:, 128:256] = (s <= t).astype(np.float32)
    c[:, 256:384] = ((s // 64) == (t // 64)).astype(np.float32)
    c[:, 384:512] = 1.0
    for g, w in enumerate((2, 4, 8, 16)):
        c[:, 512 + g * 16:512 + (g + 1) * 16] = (1.0 / np.minimum(np.arange(16) + 1, w)).astype(np.float32)[None, :]
    return c


def make_vecs(inp, NL):
    v = np.zeros((NL, 128, NVEC), np.float32)
    p = np.arange(128)
    for l in range(NL):
        for k, name in enumerate(("g_mix", "g_mem_q", "g_ffn", "g_mem_kv")):
            v[l, :, 8 * k:8 * k + 8] = np.asarray(inp[name][l]).reshape(8, 128).T
        v[l, :, 32] = np.asarray(inp["g_q_fox"][l])[p % 64]
        v[l, :, 33] = np.asarray(inp["g_k_fox"][l])[p % 64]
        v[l, :, 34] = np.asarray(inp["g_q_mem"][l])
        v[l, :, 35] = np.asarray(inp["g_k_mem"][l])
        v[l, :, 36:40] = np.asarray(inp["pool_scale"][l]).reshape(4, 128).T
        v[l, :, 40:72] = np.tile(np.asarray(inp["b_forget"][l]), 4)[None, :]
    return v


_NC_CACHE = {}


def run(inp, S, NL, ncores, trace=False):
    key = (S, NL)
    if key not in _NC_CACHE:
        _NC_CACHE[key] = build(S, NL)
    nc = _NC_CACHE[key]
    consts = make_consts()
    vecs = make_vecs(inp, NL)
    shared = {
        "w_in": np.ascontiguousarray(inp["w_in"], np.float32),
        "w_pool": np.ascontiguousarray(inp["w_pool"], np.float32),
        "w_out": np.ascontiguousarray(inp["w_out"], np.float32),
        "w_mem_q": np.ascontiguousarray(inp["w_mem_q"], np.float32),
        "w_mem_kv": np.ascontiguousarray(inp["w_mem_kv"], np.float32),
        "w_mem_out": np.ascontiguousarray(inp["w_mem_out"], np.float32),
        "w_gate_up": np.ascontiguousarray(inp["w_gate_up"], np.float32),
        "w_down": np.ascontiguousarray(inp["w_down"], np.float32),
        "vecs": vecs,
        "consts": consts,
    }
    x = np.asarray(inp["x"], np.float32)
    mem = np.asarray(inp["mem"], np.float32)
    in_maps = []
    for i in range(ncores):
        m = dict(shared)
        m["x"] = np.ascontiguousarray(x[i])
        m["mem"] = np.ascontiguousarray(mem[i])
        in_maps.append(m)
    res = run_bass_kernel_spmd(nc, in_maps, core_ids=list(range(ncores)), trace=trace)
    out = np.stack([np.asarray(r["out"], np.float32) for r in res.results], axis=0)
    return out, res


def kernel(**inputs):
    x = np.asarray(inputs["x"])
    B, S, _ = x.shape
    NL = int(np.asarray(inputs["w_in"]).shape[0])
    out, _ = run(inputs, S, NL, B)
    return out.astype(np.float32)
```

```python
import numpy as np
from contextlib import ExitStack
import concourse.bass as bass
import concourse.mybir as mybir
from concourse.bass_utils import run_bass_kernel_spmd

F32 = mybir.dt.float32
BF16 = mybir.dt.bfloat16
ALU = mybir.AluOpType
AF = mybir.ActivationFunctionType

D = 1024
EPS = 1e-6
NVEC = 72
DFF = 2816
NFC = 22
MEM = 256


class Tok:
    __slots__ = ("s", "v", "clk")

    def __init__(self, s, v, clk):
        self.s = s
        self.v = v
        self.clk = clk


class Buf:
    __slots__ = ("w", "rd", "excl")

    def __init__(self, excl=False):
        self.w = None
        self.rd = {}
        self.excl = excl


class RR:
    def __init__(self, items):
        self.items = list(items)
        self.i = 0

    def next(self):
        r = self.items[self.i % len(self.items)]
        self.i += 1
        return r


class Sched:
    COMPUTE = ("pe", "act", "dve", "pool")

    def __init__(self, nc, es, nslots=8):
        self.nc = nc
        self.h = {"pe": nc.tensor, "act": nc.scalar, "dve": nc.vector, "pool": nc.gpsimd, "sp": nc.sync}
        self.sems = []
        self.own = {}
        for e in self.COMPUTE:
            self.own[e] = len(self.sems)
            self.sems.append(es.enter_context(nc.semaphore("sem_" + e)))
        self.slots = {}
        for q in ("sp", "pool"):
            self.slots[q] = []
            for i in range(nslots):
                self.slots[q].append(len(self.sems))
                self.sems.append(es.enter_context(nc.semaphore("dma_%s_%d" % (q, i))))
        self.n = len(self.sems)
        self.cnt = [0] * self.n
        self.clk = {e: [0] * self.n for e in self.h}
        self.dman = {"sp": 0, "pool": 0}
        self.nops = 0

    def op(self, eng, fn, reads=(), writes=(), dma=False):
        clk = self.clk[eng]
        own = self.own.get(eng)
        deps = []
        wr = list(writes)
        for b in reads:
            if b.excl:
                wr.append(b)
            elif b.w is not None:
                deps.append((b.w, 0))
        for b in wr:
            if b.w is not None:
                deps.append((b.w, 1))
            for t in b.rd.values():
                deps.append((t, 2))
        waits = {}
        for t, kind in deps:
            if (not dma) and t.s == own:
                if kind != 0 or eng == "pe":
                    continue
            if clk[t.s] >= t.v:
                continue
            waits[t.s] = max(waits.get(t.s, 0), t.v)
            tc = t.clk
            for i in range(self.n):
                if tc[i] > clk[i]:
                    clk[i] = tc[i]
            if clk[t.s] < t.v:
                clk[t.s] = t.v
        if dma:
            k = self.dman[eng]
            self.dman[eng] += 1
            sl = self.slots[eng][k % len(self.slots[eng])]
            prev = self.cnt[sl]
            if clk[sl] < prev:
                waits[sl] = max(waits.get(sl, 0), prev)
                clk[sl] = prev
            self.cnt[sl] += 16
            tok = Tok(sl, self.cnt[sl], list(clk))
            inc = (sl, 16)
        else:
            self.cnt[own] += 1
            snap = list(clk)
            snap[own] = self.cnt[own]
            tok = Tok(own, self.cnt[own], snap)
            inc = (own, 1)
        for b in reads:
            if not b.excl:
                b.rd[tok.s] = tok
        for b in wr:
            b.w = tok
            b.rd = {}
        e = self.h[eng]
        wl = sorted(waits.items())
        attach = None
        if (not dma) and wl:
            attach = wl.pop()
        for s, v in wl:
            e.wait_ge(self.sems[s], v)
        r = fn(e)
        first, last = r if isinstance(r, tuple) else (r, r)
        if attach is not None:
            first._wait_ge(self.sems[attach[0]], attach[1])
        last.then_inc(self.sems[inc[0]], inc[1])
        self.nops += 1
        return tok

    def barrier(self):
        for eng, e in self.h.items():
            clk = self.clk[eng]
            for s in range(self.n):
                if s == self.own.get(eng):
                    continue
                if clk[s] < self.cnt[s]:
                    e.wait_ge(self.sems[s], self.cnt[s])
                    clk[s] = self.cnt[s]
        for eng in self.h:
            own = self.own.get(eng)
            for s in range(self.n):
                if s != own:
                    self.clk[eng][s] = self.cnt[s]

    def finish(self):
        e = self.h["sp"]
        clk = self.clk["sp"]
        for s in range(self.n):
            if clk[s] < self.cnt[s]:
                e.wait_ge(self.sems[s], self.cnt[s])
                clk[s] = self.cnt[s]


def build(S, NL):
    NB = S // 512
    NT = S // 128
    nc = bass.Bass("TRN2", target_bir_lowering=False)

    def din(name, shape, dt=F32):
        return nc.dram_tensor(name, list(shape), dt, kind="ExternalInput").ap()

    def dscr(name, shape, dt):
        return nc.dram_tensor(name, list(shape), dt, kind="Internal").ap()

    x_d = din("x", [S, D])
    mem_d = din("mem", [MEM, D])
    w_in_d = din("w_in", [NL, D, 2056])
    w_pool_d = din("w_pool", [NL, 4, 128, 128])
    w_out_d = din("w_out", [NL, D, D])
    w_mq_d = din("w_mem_q", [NL, D, 512])
    w_mkv_d = din("w_mem_kv", [NL, D, 1024])
    w_mo_d = din("w_mem_out", [NL, 512, D])
    w_gu_d = din("w_gate_up", [NL, D, 2 * DFF])
    w_dn_d = din("w_down", [NL, DFF, D])
    vecs_d = din("vecs", [NL, 128, NVEC])
    consts_d = din("consts", [128, 640])
    out_d = nc.dram_tensor("out", [S, D], F32, kind="ExternalOutput").ap()

    s_in = dscr("s_in", [NL, 128, 8, 2056], BF16)
    s_pool = dscr("s_pool", [NL, 128, 4, 128], BF16)
    s_out = dscr("s_out", [NL, 128, 8, 1024], BF16)
    s_mq = dscr("s_mq", [NL, 128, 8, 512], BF16)
    s_mkv = dscr("s_mkv", [NL, 128, 8, 1024], BF16)
    s_mo = dscr("s_mo", [NL, 128, 4, 1024], BF16)
    s_gu = dscr("s_gu", [NL, 128, 8, 2 * DFF], BF16)
    s_dn = dscr("s_dn", [NL, 128, NFC, 1024], BF16)
    hscr = dscr("hscr", [8, 128, S], F32)
    kscr = dscr("kscr", [4, 128, S], BF16)
    vscr = dscr("vscr", [4, 128, NT, 192], BF16)
    hscr_v = hscr.rearrange("k p s -> p k s")
    kscr_v = kscr.rearrange("c p s -> p c s")

    with ExitStack() as es:
        sc = Sched(nc, es)
        op = sc.op

        uniq = [0]

        def sb(name, shape, dt, stack=es):
            uniq[0] += 1
            return stack.enter_context(nc.sbuf_tensor("%s_%d" % (name, uniq[0]), list(shape), dt))

        ps = [es.enter_context(nc.psum_tensor("ps%d" % i, [128, 512], F32)) for i in range(8)]
        pb = [Buf(excl=True) for _ in range(8)]

        cst = sb("cst", [128, 640], F32)
        cstb = Buf()
        ident = cst[:, 0:128]
        triu_f = cst[:, 128:256]
        ones_f = cst[:, 384:512]
        invcnt = cst[:, 512:576]
        cbf = sb("cbf", [128, 384], BF16)
        cbfb = Buf()
        triu_b = cbf[:, 0:128]
        bd64_b = cbf[:, 128:256]
        ones_b = cbf[:, 256:384]
        epsT = sb("epsT", [128, 2], F32)
        epsb = Buf()
        op("sp", lambda e: e.dma_start(out=cst[:], in_=consts_d[:, :]), writes=[cstb], dma=True)
        op("dve", lambda e: e.tensor_copy(out=cbf[:], in_=cst[:, 128:512]), reads=[cstb], writes=[cbfb])
        op("dve", lambda e: e.memset(epsT[:, 0:1], EPS), writes=[epsb])
        op("dve", lambda e: e.memset(epsT[:, 1:2], 1.0), writes=[epsb])

        with ExitStack() as ps_es:
            CW = 2816
            NST = 3
            stf = [sb("stf%d" % i, [128, CW], F32, ps_es) for i in range(NST)]
            stb = [sb("stb%d" % i, [128, CW], BF16, ps_es) for i in range(NST)]
            stfb = [Buf() for _ in range(NST)]
            stbb = [Buf() for _ in range(NST)]
            cnt = [0]
            cast_eng = ["dve", "act", "pool"]

            def prep(src, dst, npart, ncols):
                for c0 in range(0, ncols, CW):
                    c1 = min(ncols, c0 + CW)
                    w = c1 - c0
                    i = cnt[0] % NST
                    ce = cast_eng[cnt[0] % 3]
                    cnt[0] += 1
                    op("sp", lambda e: e.dma_start(out=stf[i][0:npart, 0:w], in_=src[:, c0:c1]),
                       writes=[stfb[i]], dma=True)
                    if ce == "act":
                        op("act", lambda e: e.activation(out=stb[i][0:npart, 0:w], in_=stf[i][0:npart, 0:w], func=AF.Copy),
                           reads=[stfb[i]], writes=[stbb[i]])
                    else:
                        op(ce, lambda e: e.tensor_copy(out=stb[i][0:npart, 0:w], in_=stf[i][0:npart, 0:w]),
                           reads=[stfb[i]], writes=[stbb[i]])
                    op("pool", lambda e: e.dma_start(out=dst[:, c0:c1], in_=stb[i][0:npart, 0:w]),
                       reads=[stbb[i]], dma=True)

            for l in range(NL):
                for kc in range(8):
                    prep(w_in_d[l, kc * 128:(kc + 1) * 128, :], s_in[l, :, kc, :], 128, 2056)
                for g in range(4):
                    prep(w_pool_d[l, g, :, :], s_pool[l, :, g, :], 128, 128)
                for kc in range(8):
                    prep(w_out_d[l, kc * 128:(kc + 1) * 128, :], s_out[l, :, kc, :], 128, 1024)
                for kc in range(8):
                    prep(w_mq_d[l, kc * 128:(kc + 1) * 128, :], s_mq[l, :, kc, :], 128, 512)
                    prep(w_mkv_d[l, kc * 128:(kc + 1) * 128, :], s_mkv[l, :, kc, :], 128, 1024)
                for hm in range(4):
                    prep(w_mo_d[l, hm * 128:(hm + 1) * 128, :], s_mo[l, :, hm, :], 128, 1024)
                for kc in range(8):
                    prep(w_gu_d[l, kc * 128:(kc + 1) * 128, :], s_gu[l, :, kc, :], 128, 2 * DFF)
                for fc in range(NFC):
                    prep(w_dn_d[l, fc * 128:(fc + 1) * 128, :], s_dn[l, :, fc, :], 128, 1024)
            sc.barrier()

        hsb = [Buf() for _ in range(NB)]

        def fm_norm(hT, hTb, hb, hbb, vec, vecb, gcol, n, A0, A0b, A1, A1b, bank, part=None):
            if part in (None, "A"):
                for kc in range(8):
                    op("act", lambda e: e.activation(out=hb[:, kc, 0:n], in_=hT[:, kc, 0:n], func=AF.Square),
                       reads=[hTb[kc]], writes=[hbb[kc]])
            if part == "A":
                return

            def mm(e):
                f = None
                for kc in range(8):
                    i = e.matmul(ps[bank][:, 0:n], ones_b, hb[:, kc, 0:n], start=(kc == 0), stop=(kc == 7))
                    f = i if f is None else f
                return f, i
            op("pe", mm, reads=hbb + [cbfb], writes=[pb[bank]])
            op("act", lambda e: e.activation(out=A0[:, 0:n], in_=ps[bank][:, 0:n], func=AF.Sqrt,
                                             bias=epsT[:, 0:1], scale=1.0 / D),
               reads=[epsb, pb[bank]], writes=[A0b])
            op("dve", lambda e: e.reciprocal(out=A1[:, 0:n], in_=A0[:, 0:n]), reads=[A0b], writes=[A1b])
            for kc in range(8):
                op("dve", lambda e: e.scalar_tensor_tensor(out=hb[:, kc, 0:n], in0=hT[:, kc, 0:n],
                                                           scalar=vec[:, gcol + kc:gcol + kc + 1], in1=A1[:, 0:n],
                                                           op0=ALU.mult, op1=ALU.mult),
                   reads=[hTb[kc], A1b, vecb], writes=[hbb[kc]])

        def head_norm(bank, bank2, stat_lhsT, inv_n, n, vec, gcol, vecb, dsts, ysq, ysqb, rA, rAb, rB, rBb):
            op("act", lambda e: e.activation(out=ysq[:, 0:n], in_=ps[bank][:, 0:n], func=AF.Square),
               reads=[pb[bank]], writes=[ysqb])
            op("pe", lambda e: e.matmul(ps[bank2][:, 0:n], stat_lhsT, ysq[:, 0:n], start=True, stop=True),
               reads=[ysqb, cbfb], writes=[pb[bank2]])
            op("act", lambda e: e.activation(out=rA[:, 0:n], in_=ps[bank2][:, 0:n], func=AF.Sqrt,
                                             bias=epsT[:, 0:1], scale=inv_n),
               reads=[epsb, pb[bank2]], writes=[rAb])
            op("dve", lambda e: e.reciprocal(out=rB[:, 0:n], in_=rA[:, 0:n]), reads=[rAb], writes=[rBb])
            for (r0_, r1_, dst, dstb) in dsts:
                op("dve", lambda e: e.scalar_tensor_tensor(out=dst, in0=ps[bank][r0_:r1_, 0:n], scalar=vec[r0_:r1_, gcol:gcol + 1],
                                                           in1=rB[r0_:r1_, 0:n], op0=ALU.mult, op1=ALU.mult),
                   reads=[rBb, pb[bank], vecb], writes=[dstb])

        def head_norm_multi(items, n, vec, vecb):
            for (bank, bank2, lhs, inv_n, gcol, dsts, ysq, ysqb, r, rb) in items:
                op("act", lambda e: e.activation(out=ysq[:, 0:n], in_=ps[bank][:, 0:n], func=AF.Square),
                   reads=[pb[bank]], writes=[ysqb])
            for (bank, bank2, lhs, inv_n, gcol, dsts, ysq, ysqb, r, rb) in items:
                op("pe", lambda e: e.matmul(ps[bank2][:, 0:n], lhs, ysq[:, 0:n], start=True, stop=True),
                   reads=[ysqb, cbfb], writes=[pb[bank2]])
            for (bank, bank2, lhs, inv_n, gcol, dsts, ysq, ysqb, r, rb) in items:
                op("act", lambda e: e.activation(out=r[:, 0:n], in_=ps[bank2][:, 0:n], func=AF.Sqrt,
                                                 bias=epsT[:, 0:1], scale=inv_n),
                   reads=[epsb, pb[bank2]], writes=[rb])
            for (bank, bank2, lhs, inv_n, gcol, dsts, ysq, ysqb, r, rb) in items:
                op("dve", lambda e: e.reciprocal(out=r[:, 0:n], in_=r[:, 0:n]), reads=[rb], writes=[rb])
            for (bank, bank2, lhs, inv_n, gcol, dsts, ysq, ysqb, r, rb) in items:
                for (r0_, r1_, dst, dstb) in dsts:
                    op("dve", lambda e: e.scalar_tensor_tensor(out=dst, in0=ps[bank][r0_:r1_, 0:n], scalar=vec[r0_:r1_, gcol:gcol + 1],
                                                               in1=r[r0_:r1_, 0:n], op0=ALU.mult, op1=ALU.mult),
                       reads=[rb, pb[bank], vecb], writes=[dstb])

        def pipeline(stage_lists, delays, filler=None):
            n = len(stage_lists)
            for t in range(n + max(delays)):
                for k, d in enumerate(delays):
                    i = t - d
                    if 0 <= i < n:
                        stage_lists[i][k]()
                if filler is not None:
                    filler()

        def norm_stages(bank, bank2, lhs, inv_n, n, vec, vecb, gcol, dsts, ysq, ysqb, r, rb):
            def sq():
                op("act", lambda e: e.activation(out=ysq[:, 0:n], in_=ps[bank][:, 0:n], func=AF.Square),
                   reads=[pb[bank]], writes=[ysqb])

            def st():
                op("pe", lambda e: e.matmul(ps[bank2][:, 0:n], lhs, ysq[:, 0:n], start=True, stop=True),
                   reads=[ysqb, cbfb], writes=[pb[bank2]])

            def sr():
                op("act", lambda e: e.activation(out=r[:, 0:n], in_=ps[bank2][:, 0:n], func=AF.Sqrt,
                                                 bias=epsT[:, 0:1], scale=inv_n),
                   reads=[epsb, pb[bank2]], writes=[rb])

            def rc():
                op("dve", lambda e: e.reciprocal(out=r[:, 0:n], in_=r[:, 0:n]), reads=[rb], writes=[rb])

            def stt():
                for (r0_, r1_, dst, dstb) in dsts:
                    op("dve", lambda e: e.scalar_tensor_tensor(out=dst, in0=ps[bank][r0_:r1_, 0:n], scalar=vec[r0_:r1_, gcol:gcol + 1],
                                                               in1=r[r0_:r1_, 0:n], op0=ALU.mult, op1=ALU.mult),
                       reads=[rb, pb[bank], vecb], writes=[dstb])
            return [sq, st, sr, rc, stt]

        for l in range(NL):
            first_layer = (l == 0)
            last_layer = (l == NL - 1)
            with ExitStack() as mx:
                vec = sb("vec", [128, NVEC], F32, mx)
                vecb = Buf()
                op("sp", lambda e: e.dma_start(out=vec[:], in_=vecs_d[l, :, :]), writes=[vecb], dma=True)
                wf = sb("wf", [128, 8, 8], BF16, mx)
                wfb = Buf()
                op("sp", lambda e: e.dma_start(out=wf[:], in_=s_in[l, :, :, 2048:2056]), writes=[wfb], dma=True)
                wpl = sb("wpl", [128, 4, 128], BF16, mx)
                wplb = Buf()
                op("sp", lambda e: e.dma_start(out=wpl[:], in_=s_pool[l, :, :, :]), writes=[wplb], dma=True)

                hT2 = [sb("hT%d" % i, [128, 8, 512], F32, mx) for i in range(2)]
                hT2b = [[Buf() for _ in range(8)] for _ in range(2)]
                hb1 = sb("hb1", [128, 8, 512], BF16, mx)
                hb1b = [Buf() for _ in range(8)]
                hb2 = sb("hb2", [128, 8, 512], BF16, mx)
                hb2b = [Buf() for _ in range(8)]
                pa = [sb("pa%d" % i, [128, 512], F32, mx) for i in range(2)]
                pab = [Buf() for _ in range(2)]
                bank_pro = RR([6, 7])
                NF = 6
                ft = [sb("ft%d" % i, [128, 512], F32, mx) for i in range(NF)]
                ftb = [Buf() for _ in range(NF)]
                ftr = RR(range(NF))
                qT = sb("qT", [128, 8, 512], BF16, mx)
                qTb = [Buf() for _ in range(8)]
                kcur = sb("kcur", [128, 4, 512], BF16, mx)
                kcb = [Buf() for _ in range(4)]
                vcur = sb("vcur", [128, 4, 4, 192], BF16, mx)
                vcb = [Buf() for _ in range(4)]
                NR = 4
                ring = [sb("ring%d" % i, [128, 4096], BF16, mx) for i in range(NR)]
                ringb = [Buf() for _ in range(NR)]
                NPT = 4
                PT = [sb("PT%d" % i, [128, 512], BF16, mx) for i in range(NPT)]
                PTb = [Buf() for _ in range(NPT)]
                ptr = RR(range(NPT))
                biasb = [sb("biasb%d" % i, [128, NT], F32, mx) for i in range(2)]
                biasbb = [Buf() for _ in range(2)]
                foxT = sb("foxT", [128, 4, 512], BF16, mx)
                foxb = [Buf() for _ in range(8)]
                pin = sb("pin", [128, 4, 528], F32, mx)
                pinb = [Buf() for _ in range(4)]
                ptmp = [sb("ptmp%d" % i, [128, 528], F32, mx) for i in range(2)]
                ptmpb = [Buf() for _ in range(2)]
                mixed = sb("mixed", [128, 4, 512], BF16, mx)
                mixb = [Buf() for _ in range(4)]
                poolT = sb("poolT", [128, 4, 512], BF16, mx)
                poolb = [Buf() for _ in range(4)]
                ysq = [sb("ysq%d" % i, [128, 512], BF16, mx) for i in range(4)]
                ysqb = [Buf() for _ in range(4)]
                mqT = [sb("mqT%d" % i, [128, 512], BF16, mx) for i in range(4)]
                mqTb = [Buf() for _ in range(4)]
                xin = [sb("xin%d" % i, [128, 512], F32, mx) for i in range(3)]
                xinb = [Buf() for _ in range(3)]
                xr = RR(range(3))
                negc = sb("negc", [128, NT, 8], F32, mx)
                negcb = Buf()
                cend = sb("cend", [128, NT + 1, 8], F32, mx)
                cendb = Buf()
                fl = sb("fl", [128, 3, 32], F32, mx)
                flb = [Buf() for _ in range(3)]
                mkT = sb("mkT", [128, 4, MEM], BF16, mx)
                mkTb = Buf()
                mv = sb("mv", [128, 2, 512], BF16, mx)
                mvb = Buf()
                ksb = [Buf() for _ in range(NB)]
                vsb = [Buf() for _ in range(NB)]

                bank_proj = RR([0, 1])
                bank_stat = RR([7, 4])
                bank_S = RR([2, 3, 4])
                bank_O = RR([5, 6])
                bank_O2 = RR([(5, 6), (7, 0)])
                BANK_RB = 1

                op("pool", lambda e: e.memset(qT[:], 0.0), writes=qTb)
                op("pool", lambda e: e.memset(vcur[:, :, :, 64:128], 1.0), writes=vcb)
                op("dve", lambda e: e.memset(cend[:, 0, :], 0.0), writes=[cendb])
                op("pool", lambda e: e.memset(pin[:, :, 0:16], 0.0), writes=pinb)

                chunks = []
                for b in range(NB):
                    chunks.append(("inQ", s_in[l, :, :, 0:512], 128, (8, 512)))
                    chunks.append(("inK", s_in[l, :, :, 512:1024], 128, (8, 512)))
                    chunks.append(("inV", s_in[l, :, :, 1024:1536], 128, (8, 512)))
                    chunks.append(("inP", s_in[l, :, :, 1536:2048], 128, (8, 512)))
                    for hf in range(2):
                        chunks.append(("out", s_out[l, :, :, hf * 512:(hf + 1) * 512], 128, (8, 512)))
                    chunks.append(("mq", s_mq[l, :, :, :], 128, (8, 512)))
                    chunks.append(("mo", s_mo[l, :, :, :], 128, (4, 1024)))
                wstate = {"issued": 0, "cur": 0}

                def wview(k):
                    name, src, npart, (a, bcols) = chunks[k]
                    return ring[k % NR][0:npart, 0:a * bcols].rearrange("p (a b) -> p a b", a=a)

                def wissue(upto):
                    while wstate["issued"] < min(upto, len(chunks)):
                        k = wstate["issued"]
                        src = chunks[k][1]
                        dst = wview(k)
                        op("sp", lambda e: e.dma_start(out=dst, in_=src), writes=[ringb[k % NR]], dma=True)
                        wstate["issued"] += 1

                def wnext(name):
                    k = wstate["cur"]
                    assert chunks[k][0] == name, (chunks[k][0], name)
                    wissue(k + 1)
                    wstate["cur"] += 1
                    return wview(k), ringb[k % NR], k

                def wprefetch():
                    wissue(wstate["cur"] + NR)

                with ExitStack() as mk:
                    memT = sb("memT", [128, 8, MEM], F32, mk)
                    memTb = [Buf() for _ in range(8)]
                    mnb = sb("mnb", [128, 8, MEM], BF16, mk)
                    mnbb = [Buf() for _ in range(8)]
                    wkv = sb("wkv", [128, 8, 1024], BF16, mk)
                    wkvb = Buf()
                    op("sp", lambda e: e.dma_start(out=wkv[:], in_=s_mkv[l, :, :, :]), writes=[wkvb], dma=True)
                    for i in range(2):
                        for hf in range(2):
                            xi = xr.next()
                            op("pool", lambda e: e.dma_start(out=xin[xi][:], in_=mem_d[i * 128:(i + 1) * 128, hf * 512:(hf + 1) * 512]),
                               writes=[xinb[xi]], dma=True)
                            bk = bank_proj.next()

                            def tr(e):
                                f = None
                                for k in range(4):
                                    ins = e.transpose(ps[bk][:, k * 128:(k + 1) * 128], xin[xi][:, k * 128:(k + 1) * 128], ident)
                                    f = ins if f is None else f
                                return f, ins
                            op("pe", tr, reads=[xinb[xi], cstb], writes=[pb[bk]])
                            op("dve", lambda e: e.tensor_copy(out=memT[:, hf * 4:(hf + 1) * 4, i * 128:(i + 1) * 128],
                                                              in_=ps[bk][:, 0:512].rearrange("p (k t) -> p k t", k=4)),
                               reads=[pb[bk]], writes=memTb[hf * 4:(hf + 1) * 4])
                    a0, a1 = ftr.next(), ftr.next()
                    fm_norm(memT, memTb, mnb, mnbb, vec, vecb, 24, MEM, ft[a0], ftb[a0], ft[a1], ftb[a1], bank_stat.next())
                    for hm in range(4):
                        bk = bank_proj.next()

                        def mmk(e):
                            f = None
                            for kc in range(8):
                                ins = e.matmul(ps[bk][:, 0:MEM], wkv[:, kc, hm * 128:(hm + 1) * 128], mnb[:, kc, :],
                                               start=(kc == 0), stop=(kc == 7))
                                f = ins if f is None else f
                            return f, ins
                        op("pe", mmk, reads=mnbb + [wkvb], writes=[pb[bk]])
                        a0, a1 = ftr.next(), ftr.next()
                        head_norm(bk, bank_stat.next(), ones_b, 1.0 / 128, MEM, vec, 35, vecb, [(0, 128, mkT[:, hm, :], mkTb)],
                                  ysq[hm % 2], ysqb[hm % 2], ft[a0], ftb[a0], ft[a1], ftb[a1])
                    for mt in range(2):
                        bk = bank_proj.next()

                        def mmv(e):
                            f = None
                            for kc in range(8):
                                ins = e.matmul(ps[bk][:, :], mnb[:, kc, mt * 128:(mt + 1) * 128], wkv[:, kc, 512:1024],
                                               start=(kc == 0), stop=(kc == 7))
                                f = ins if f is None else f
                            return f, ins
                        op("pe", mmv, reads=mnbb + [wkvb], writes=[pb[bk]])
                        op("dve", lambda e: e.tensor_copy(out=mv[:, mt, :], in_=ps[bk][:, :]), reads=[pb[bk]], writes=[mvb])
                    sc.barrier()

                CK = 8
                NCH = 4
                Kc = [sb("Kc%d" % i, [128, CK * 128], BF16, mx) for i in range(NCH)]
                Vc = [sb("Vc%d" % i, [128, CK, 192], BF16, mx) for i in range(NCH)]
                kvb = [Buf() for _ in range(NCH)]
                fz = [sb("fz%d" % i, [128, 512], F32, mx) for i in range(4)]
                fzb = [Buf() for _ in range(4)]
                fzr = RR(range(4))
                kvchunks = []
                kvbase = {}
                for b_ in range(1, NB):
                    for c_ in range(4):
                        kvbase[(b_, c_)] = len(kvchunks)
                        for lo in range(0, 4 * b_, CK):
                            kvchunks.append((b_, c_, lo, min(lo + CK, 4 * b_)))
                kvstate = {"issued": 0, "released": 0}

                def kv_issue():
                    lim = min(len(kvchunks), kvstate["released"] + NCH)
                    while kvstate["issued"] < lim:
                        k = kvstate["issued"]
                        b_, c_, lo, hi = kvchunks[k]
                        if b_ > kvstate["maxb"]:
                            break
                        sl = k % NCH
                        op("sp", lambda e: e.dma_start(out=Kc[sl][:, 0:(hi - lo) * 128], in_=kscr[c_, :, lo * 128:hi * 128]),
                           reads=ksb[0:b_], writes=[kvb[sl]], dma=True)
                        op("sp", lambda e: e.dma_start(out=Vc[sl][:, 0:hi - lo, :], in_=vscr[c_, :, lo:hi, :]),
                           reads=vsb[0:b_], writes=[kvb[sl]], dma=True)
                        kvstate["issued"] += 1
                kvstate["maxb"] = 0

                def prologue(bn):
                    hTn, hTnb = hT2[bn % 2], hT2b[bn % 2]
                    tn = bn * 512
                    if first_layer:
                        for i in range(4):
                            for hf in range(2):
                                xi = xr.next()
                                op("pool", lambda e: e.dma_start(out=xin[xi][:], in_=x_d[tn + i * 128:tn + (i + 1) * 128, hf * 512:(hf + 1) * 512]),
                                   writes=[xinb[xi]], dma=True)
                                bk = bank_pro.next()

                                def tr(e):
                                    f = None
                                    for k in range(4):
                                        ins = e.transpose(ps[bk][:, k * 128:(k + 1) * 128], xin[xi][:, k * 128:(k + 1) * 128], ident)
                                        f = ins if f is None else f
                                    return f, ins
                                op("pe", tr, reads=[xinb[xi], cstb], writes=[pb[bk]])
                                op("dve", lambda e: e.tensor_copy(out=hTn[:, hf * 4:(hf + 1) * 4, i * 128:(i + 1) * 128],
                                                                  in_=ps[bk][:, 0:512].rearrange("p (k t) -> p k t", k=4)),
                                   reads=[pb[bk]], writes=hTnb[hf * 4:(hf + 1) * 4])
                            yield
                    else:
                        op("pool", lambda e: e.dma_start(out=hTn[:], in_=hscr_v[:, :, tn:tn + 512]),
                           reads=[hsb[bn]], writes=hTnb, dma=True)
                        yield
                    bkn = bank_pro.next()
                    fm_norm(hTn, hTnb, hb1, hb1b, vec, vecb, 0, 512, pa[0], pab[0], pa[1], pab[1], bkn, part="A")
                    yield
                    fm_norm(hTn, hTnb, hb1, hb1b, vec, vecb, 0, 512, pa[0], pab[0], pa[1], pab[1], bkn, part="B")
                    yield

                for _ in prologue(0):
                    pass
                for b in range(NB):
                    t0 = b * 512
                    wprefetch()
                    hT, hTb = hT2[b % 2], hT2b[b % 2]
                    hb, hbb = hb1, hb1b

                    wq_, wqb_, _ = wnext("inQ")
                    wk_, wkb_, _ = wnext("inK")
                    ysl = [(ysq[i], ysqb[i]) for i in range(4)]
                    stage_lists = []
                    for it in range(8):
                        which, c = ("inQ", it) if it < 4 else ("inK", it - 4)
                        wv_, wb_ = (wq_, wqb_) if it < 4 else (wk_, wkb_)
                        bk = it % 4
                        if which == "inQ":
                            dsts, gcol = [(0, 64, qT[0:64, 2 * c, :], qTb[2 * c]), (64, 128, qT[64:128, 2 * c + 1, :], qTb[2 * c + 1])], 32
                        else:
                            dsts, gcol = [(0, 128, kcur[:, c, :], kcb[c])], 33

                        def proj(bk=bk, wv_=wv_, wb_=wb_, c=c):
                            def mmq(e):
                                f = None
                                for kc in range(8):
                                    ins = e.matmul(ps[bk][:, :], wv_[:, kc, c * 128:(c + 1) * 128], hb[:, kc, :],
                                                   start=(kc == 0), stop=(kc == 7))
                                    f = ins if f is None else f
                                return f, ins
                            op("pe", mmq, reads=hbb + [wb_], writes=[pb[bk]])
                        ys, ysb_ = ysl[it % 4]
                        stage_lists.append([proj] + norm_stages(bk, 4 + (it % 2), bd64_b, 1.0 / 64, 512, vec, vecb, gcol, dsts,
                                                                ys, ysb_, ft[it % 4], ftb[it % 4]))
                    pipeline(stage_lists, [0, 1, 1, 2, 2, 3])
                    wprefetch()
                    if b < NB - 1:
                        op("pool", lambda e: e.dma_start(out=kscr_v[:, :, t0:t0 + 512], in_=kcur[:]),
                           reads=kcb, writes=[ksb[b]], dma=True)

                    wv_, wb_, _ = wnext("inV")
                    BF = 4
                    for i in range(4):
                        bk = i

                        def mmv2(e):
                            f = None
                            for kc in range(8):
                                ins = e.matmul(ps[bk][:, :], hb[:, kc, i * 128:(i + 1) * 128], wv_[:, kc, :],
                                               start=(kc == 0), stop=(kc == 7))
                                f = ins if f is None else f
                            return f, ins
                        op("pe", mmv2, reads=hbb + [wb_], writes=[pb[bk]])

                    def mmf(e):
                        f = None
                        for i in range(4):
                            for kc in range(8):
                                ins = e.matmul(ps[BF][:, i * 8:(i + 1) * 8], hb[:, kc, i * 128:(i + 1) * 128], wf[:, kc, :],
                                               start=(kc == 0), stop=(kc == 7))
                                f = ins if f is None else f
                        return f, ins
                    op("pe", mmf, reads=hbb + [wfb], writes=[pb[BF]])
                    for i in range(4):
                        pv4 = ps[i][:, 0:512].rearrange("p (c two d) -> p c two d", c=4, two=2)
                        op("dve", lambda e: e.tensor_copy(out=vcur[:, i, :, 0:64], in_=pv4[:, :, 0, :]),
                           reads=[pb[i]], writes=[vcb[i]])
                        op("dve", lambda e: e.tensor_copy(out=vcur[:, i, :, 128:192], in_=pv4[:, :, 1, :]),
                           reads=[pb[i]], writes=[vcb[i]])
                    op("dve", lambda e: e.tensor_tensor(out=fl[:, 0, :], in0=ps[BF][:, 0:32], in1=vec[:, 40:72], op=ALU.add),
                       reads=[pb[BF], vecb], writes=[flb[0]])
                    op("act", lambda e: e.activation(out=fl[:, 1, :], in_=fl[:, 0, :], func=AF.Exp, scale=-1.0),
                       reads=[flb[0]], writes=[flb[1]])
                    op("act", lambda e: e.activation(out=fl[:, 2, :], in_=fl[:, 1, :], func=AF.Ln, bias=epsT[:, 1:2], scale=1.0),
                       reads=[flb[1], epsb], writes=[flb[2]])
                    BC = 5

                    def mmc(e):
                        f = None
                        for i in range(4):
                            ins = e.matmul(ps[BC][:, i * 8:(i + 1) * 8], ident, cend[:, 4 * b, :], start=True, stop=False)
                            f = ins if f is None else f
                            for i2 in range(i):
                                e.matmul(ps[BC][:, i * 8:(i + 1) * 8], ones_f, fl[:, 2, i2 * 8:(i2 + 1) * 8], start=False, stop=False)
                            e.matmul(ps[BC][:, i * 8:(i + 1) * 8], triu_f, fl[:, 2, i * 8:(i + 1) * 8], start=False, stop=True)
                            e.matmul(ps[BC][:, 32 + i * 8:32 + (i + 1) * 8], ident, cend[:, 4 * b, :], start=True, stop=False)
                            for i2 in range(i + 1):
                                ins = e.matmul(ps[BC][:, 32 + i * 8:32 + (i + 1) * 8], ones_f, fl[:, 2, i2 * 8:(i2 + 1) * 8],
                                               start=False, stop=(i2 == i))
                        return f, ins
                    op("pe", mmc, reads=[flb[2], cstb, cendb], writes=[pb[BC]])
                    op("dve", lambda e: e.tensor_copy(out=negc[:, 4 * b:4 * b + 4, :], in_=ps[BC][:, 0:32].rearrange("p (t h) -> p t h", t=4)),
                       reads=[pb[BC]], writes=[negcb])
                    op("dve", lambda e: e.tensor_copy(out=cend[:, 4 * b + 1:4 * b + 5, :], in_=ps[BC][:, 32:64].rearrange("p (t h) -> p t h", t=4)),
                       reads=[pb[BC]], writes=[cendb])
                    if b < NB - 1:
                        for c in range(4):
                            op("pool", lambda e: e.dma_start(out=vscr[c, :, 4 * b:4 * b + 4, :],
                                                             in_=vcur[:, :, c, :]),
                               reads=vcb, writes=[vsb[b]], dma=True)
                    wprefetch()

                    wv_, wb_, _ = wnext("inP")
                    for g in range(4):
                        bk = bank_proj.next()

                        def mmp(e):
                            f = None
                            for kc in range(8):
                                ins = e.matmul(ps[bk][:, :], wv_[:, kc, g * 128:(g + 1) * 128], hb[:, kc, :],
                                               start=(kc == 0), stop=(kc == 7))
                                f = ins if f is None else f
                            return f, ins
                        op("pe", mmp, reads=hbb + [wb_], writes=[pb[bk]])
                        op("act", lambda e: e.activation(out=pin[:, g, 16:528], in_=ps[bk][:, :], func=AF.Copy),
                           reads=[pb[bk]], writes=[pinb[g]])
                    wprefetch()

                    for g in range(4):
                        w = 2 << g
                        cur, curb = pin[:, g, :], pinb[g]
                        lo, off = 0, 1
                        for st in range(g + 1):
                            nx, nxb = ptmp[st % 2], ptmpb[st % 2]
                            lo2 = lo + off
                            op("pool", lambda e: e.tensor_tensor(out=nx[:, lo2:528], in0=cur[:, lo2:528], in1=cur[:, lo:528 - off], op=ALU.add),
                               reads=[curb], writes=[nxb])
                            cur, curb = nx[:, :], nxb
                            lo, off = lo2, off * 2
                        op("dve", lambda e: e.scalar_tensor_tensor(out=mixed[:, g, :], in0=cur[:, 16:528], scalar=1.0 / w,
                                                                    in1=pin[:, g, 16:528], op0=ALU.mult, op1=ALU.subtract),
                           reads=[curb, pinb[g]], writes=[mixb[g]])
                        if b == 0:
                            oth, othb = ptmp[(g + 1) % 2], ptmpb[(g + 1) % 2]
                            op("pool", lambda e: e.tensor_tensor(out=oth[:, 0:16], in0=cur[:, 16:32], in1=invcnt[:, g * 16:(g + 1) * 16], op=ALU.mult),
                               reads=[curb, cstb], writes=[othb])
                            op("pool", lambda e: e.tensor_tensor(out=mixed[:, g, 0:16], in0=oth[:, 0:16], in1=pin[:, g, 16:32], op=ALU.subtract),
                               reads=[othb, pinb[g]], writes=[mixb[g]])
                    op("pool", lambda e: e.tensor_copy(out=pin[:, :, 0:16], in_=pin[:, :, 512:528]), reads=[], writes=pinb)

                    nj = 4 * b + 4
                    items = [(2 * c + hh, j) for c in range(4) for j in range(nj) for hh in range(2)]
                    SKEW = 3
                    st_ = {}
                    obank = {}
                    pend = []
                    kvstate["maxb"] = b
                    kv_issue()

                    def kv_of(c, j):
                        q = j // CK
                        k = kvbase[(b, c)] + q
                        assert k < kvstate["issued"], (b, c, j, k, kvstate)
                        return k % NCH, j - q * CK, k

                    def emit_S(h, j):
                        c = h // 2
                        if j == 0:
                            bi = h % 2
                            op("dve", lambda e: e.tensor_scalar(out=biasb[bi][:, 0:nj], in0=negc[:, 0:nj, h],
                                                                scalar1=cend[:, 4 * b + 2, h:h + 1], scalar2=None, op0=ALU.subtract),
                               reads=[negcb, cendb], writes=[biasbb[bi]])
                        if j < 4 * b:
                            sl, jj, _ = kv_of(c, j)
                            lhsT = Kc[sl][:, jj * 128:(jj + 1) * 128]
                            kb = kvb[sl]
                            col0 = 0
                        else:
                            i = j - 4 * b
                            lhsT = kcur[:, c, i * 128:(i + 1) * 128]
                            kb = kcb[c]
                            col0 = i * 128
                        bs = bank_S.next()
                        op("pe", lambda e: e.matmul(ps[bs][:, col0:512], lhsT, qT[:, h, col0:512], start=True, stop=True),
                           reads=[kb, qTb[h]], writes=[pb[bs]])
                        pt = ptr.next()
                        op("act", lambda e: e.activation(out=PT[pt][:, col0:512], in_=ps[bs][:, col0:512], func=AF.Exp,
                                                         bias=biasb[h % 2][:, j:j + 1], scale=0.125),
                           reads=[pb[bs], biasbb[h % 2]], writes=[PTb[pt]])
                        if j >= 4 * b:
                            op("dve", lambda e: e.tensor_tensor(out=PT[pt][:, col0:col0 + 128], in0=PT[pt][:, col0:col0 + 128],
                                                                in1=triu_b, op=ALU.mult),
                               reads=[cbfb], writes=[PTb[pt]])
                        st_[(h, j)] = (pt, col0)

                    def emit_PV(h, j):
                        c = h // 2
                        pt, col0 = st_.pop((h, j))
                        if j == 0 and h % 2 == 0:
                            bo2 = bank_O2.next()
                            obank[h], obank[h + 1] = bo2
                        bo = obank[h]
                        v0 = 0 if h % 2 == 0 else 64
                        if j < 4 * b:
                            sl, jj, k = kv_of(c, j)
                            lhsT = Vc[sl][:, jj, v0:v0 + 128]
                            vb_ = kvb[sl]
                        else:
                            lhsT = vcur[:, j - 4 * b, c, v0:v0 + 128]
                            vb_ = vcb[j - 4 * b]
                        op("pe", lambda e: e.matmul(ps[bo][:, col0:512], lhsT, PT[pt][:, col0:512], start=(j == 0), stop=(j == nj - 1)),
                           reads=[PTb[pt], vb_], writes=[pb[bo]])
                        if j < 4 * b and h % 2 == 1:
                            _, _, k = kv_of(c, j)
                            lo, hi = kvchunks[k][2], kvchunks[k][3]
                            if j == hi - 1:
                                kvstate["released"] = k + 1
                                kv_issue()
                        if j == nj - 1:
                            finalize(h, bo)

                    def finalize(h, bo):
                        c = h // 2
                        r1, r2 = fzr.next(), fzr.next()
                        if h % 2 == 0:
                            rr, o0 = 64, 0
                        else:
                            rr, o0 = 0, 64
                        op("dve", lambda e: e.reciprocal(out=fz[r1][rr:rr + 1, :], in_=ps[bo][rr:rr + 1, :]),
                           reads=[pb[bo]], writes=[fzb[r1]])

                        def fin_pe():
                            if h % 2 == 0:
                                op("pe", lambda e: e.matmul(ps[BANK_RB][0:64, :], ones_f[64:65, 0:64], fz[r1][64:65, :], start=True, stop=True),
                                   reads=[fzb[r1], cstb], writes=[pb[BANK_RB]])
                            else:
                                op("pe", lambda e: e.matmul(ps[BANK_RB][:, :], ones_f[0:1, :], fz[r1][0:1, :], start=True, stop=True),
                                   reads=[fzb[r1], cstb], writes=[pb[BANK_RB]])
                            op("dve", lambda e: e.tensor_copy(out=fz[r2][o0:o0 + 64, :], in_=ps[BANK_RB][o0:o0 + 64, :]),
                               reads=[pb[BANK_RB]], writes=[fzb[r2]])
                            op("dve", lambda e: e.tensor_tensor(out=foxT[o0:o0 + 64, c, :], in0=ps[bo][o0:o0 + 64, :], in1=fz[r2][o0:o0 + 64, :], op=ALU.mult),
                               reads=[pb[bo], fzb[r2]], writes=[foxb[h]])
                        pend.append([3, fin_pe])

                    def tick():
                        for p in list(pend):
                            p[0] -= 1
                            if p[0] <= 0:
                                p[1]()
                                pend.remove(p)

                    for idx in range(len(items) + SKEW):
                        if idx < len(items):
                            emit_S(*items[idx])
                        if idx >= SKEW:
                            emit_PV(*items[idx - SKEW])
                        tick()
                    while pend:
                        tick()
                    kvstate["maxb"] = b + 1
                    kv_issue()

                    for g in range(4):
                        bk = bank_proj.next()
                        op("pe", lambda e: e.matmul(ps[bk][:, :], wpl[:, g, :], mixed[:, g, :], start=True, stop=True),
                           reads=[mixb[g], wplb], writes=[pb[bk]])
                        op("act", lambda e: e.activation(out=poolT[:, g, :], in_=ps[bk][:, :], func=AF.Copy, scale=vec[:, 36 + g:37 + g]),
                           reads=[pb[bk], vecb], writes=[poolb[g]])

                    for hf in range(2):
                        wW, wWb, _ = wnext("out")
                        for oc in range(4):
                            kc = hf * 4 + oc
                            bk = bank_proj.next()

                            def mmo(e):
                                f = None
                                for c in range(4):
                                    ins = e.matmul(ps[bk][:, :], wW[:, c, oc * 128:(oc + 1) * 128], foxT[:, c, :],
                                                   start=(c == 0), stop=False)
                                    f = ins if f is None else f
                                for g in range(4):
                                    ins = e.matmul(ps[bk][:, :], wW[:, 4 + g, oc * 128:(oc + 1) * 128], poolT[:, g, :],
                                                   start=False, stop=(g == 3))
                                return f, ins
                            op("pe", mmo, reads=foxb + poolb + [wWb], writes=[pb[bk]])
                            op("dve", lambda e: e.tensor_tensor(out=hT[:, kc, :], in0=ps[bk][:, :], in1=hT[:, kc, :], op=ALU.add),
                               reads=[pb[bk], hTb[kc]], writes=[hTb[kc]])
                        wprefetch()

                    a0, a1 = ftr.next(), ftr.next()
                    hb, hbb = hb2, hb2b
                    fm_norm(hT, hTb, hb, hbb, vec, vecb, 8, 512, ft[a0], ftb[a0], ft[a1], ftb[a1], bank_stat.next())
                    gen = prologue(b + 1) if b + 1 < NB else iter(())
                    wQ, wQb, _ = wnext("mq")
                    moT = foxT
                    stage_lists = []
                    for hm in range(4):
                        bk = hm
                        p0, p1 = 2 * (hm % 2), 2 * (hm % 2) + 1
                        r1 = 4 + (hm % 2)
                        bst = 4 + (hm % 2)

                        def proj(bk=bk, hm=hm):
                            def mmq2(e):
                                f = None
                                for kc in range(8):
                                    ins = e.matmul(ps[bk][:, :], wQ[:, kc, hm * 128:(hm + 1) * 128], hb[:, kc, :],
                                                   start=(kc == 0), stop=(kc == 7))
                                    f = ins if f is None else f
                                return f, ins
                            op("pe", mmq2, reads=hbb + [wQb], writes=[pb[bk]])

                        def scores(hm=hm):
                            for mt in range(2):
                                op("pe", lambda e: e.matmul(ps[6 + mt][:, :], mkT[:, hm, mt * 128:(mt + 1) * 128], mqT[hm][:, :], start=True, stop=True),
                                   reads=[mkTb, mqTb[hm]], writes=[pb[6 + mt]])

                        def exps(hm=hm, p0=p0):
                            for mt in range(2):
                                op("act", lambda e: e.activation(out=PT[p0 + mt][:, :], in_=ps[6 + mt][:, :], func=AF.Exp, scale=128.0 ** -0.5),
                                   reads=[pb[6 + mt]], writes=[PTb[p0 + mt]])

                        def pv(hm=hm, bk=bk, p0=p0, p1=p1, bst=bst):
                            def mmpv(e):
                                i1 = e.matmul(ps[bk][:, :], mv[:, 0, hm * 128:(hm + 1) * 128], PT[p0][:, :], start=True, stop=False)
                                e.matmul(ps[bk][:, :], mv[:, 1, hm * 128:(hm + 1) * 128], PT[p1][:, :], start=False, stop=True)
                                e.matmul(ps[bst][:, :], ones_b, PT[p0][:, :], start=True, stop=False)
                                i2 = e.matmul(ps[bst][:, :], ones_b, PT[p1][:, :], start=False, stop=True)
                                return i1, i2
                            op("pe", mmpv, reads=[mvb, cbfb, PTb[p0], PTb[p1]], writes=[pb[bk], pb[bst]])

                        def fin(hm=hm, bk=bk, r1=r1, bst=bst):
                            op("dve", lambda e: e.reciprocal(out=ft[r1][:, :], in_=ps[bst][:, :]), reads=[pb[bst]], writes=[ftb[r1]])
                            op("dve", lambda e: e.tensor_tensor(out=moT[:, hm, :], in0=ps[bk][:, :], in1=ft[r1][:, :], op=ALU.mult),
                               reads=[pb[bk], ftb[r1]], writes=[foxb[2 * hm], foxb[2 * hm + 1]])
                        stage_lists.append([proj] + norm_stages(bk, bst, ones_b, 1.0 / 128, 512, vec, vecb, 34,
                                                                [(0, 128, mqT[hm][:, :], mqTb[hm])], ysq[hm], ysqb[hm], ft[hm], ftb[hm])
                                           + [scores, exps, pv, fin])
                    pipeline(stage_lists, [0, 1, 1, 2, 2, 3, 3, 3, 4, 4], filler=lambda: next(gen, None))
                    for _ in gen:
                        pass
                    wprefetch()
                    wO, wOb, _ = wnext("mo")
                    for oc in range(8):
                        bk = bank_proj.next()

                        def mmo2(e):
                            f = None
                            for hm in range(4):
                                ins = e.matmul(ps[bk][:, :], wO[:, hm, oc * 128:(oc + 1) * 128], moT[:, hm, :],
                                               start=(hm == 0), stop=(hm == 3))
                                f = ins if f is None else f
                            return f, ins
                        op("pe", mmo2, reads=foxb + [wOb], writes=[pb[bk]])
                        op("dve", lambda e: e.tensor_tensor(out=hT[:, oc, :], in0=ps[bk][:, :], in1=hT[:, oc, :], op=ALU.add),
                           reads=[pb[bk], hTb[oc]], writes=[hTb[oc]])
                    wprefetch()
                    op("pool", lambda e: e.dma_start(out=hscr_v[:, :, t0:t0 + 512], in_=hT[:]),
                       reads=hTb, writes=[hsb[b]], dma=True)
                sc.barrier()

            with ExitStack() as fx:
                vec = sb("vecF", [128, NVEC], F32, fx)
                vecb = Buf()
                op("sp", lambda e: e.dma_start(out=vec[:], in_=vecs_d[l, :, :]), writes=[vecb], dma=True)
                wgu = sb("wgu", [128, 8, 2 * DFF], BF16, fx)
                wgub = [Buf() for _ in range(8)]
                wdn = sb("wdn", [128, NFC, 1024], BF16, fx)
                wdnb = [Buf() for _ in range(NFC)]
                for kc in range(8):
                    op("sp", lambda e: e.dma_start(out=wgu[:, kc, :], in_=s_gu[l, :, kc, :]), writes=[wgub[kc]], dma=True)
                for f0 in range(0, NFC, 6):
                    f1 = min(NFC, f0 + 6)
                    op("sp", lambda e: e.dma_start(out=wdn[:, f0:f1, :], in_=s_dn[l, :, f0:f1, :]), writes=wdnb[f0:f1], dma=True)
                hT2 = [sb("hTF%d" % i, [128, 8, 512], F32, fx) for i in range(2)]
                hT2b = [[Buf() for _ in range(8)] for _ in range(2)]
                hb = sb("hbF", [128, 8, 512], BF16, fx)
                hbb = [Buf() for _ in range(8)]
                actT = sb("actT", [128, NFC, 512], BF16, fx)
                actb = [Buf() for _ in range(NFC)]
                sg = [sb("sg%d" % i, [128, 512], F32, fx) for i in range(2)]
                sgb = [Buf() for _ in range(2)]
                fa = [sb("fa%d" % i, [128, 512], F32, fx) for i in range(2)]
                fab = [Buf() for _ in range(2)]
                ot, otb = sg, sgb
                otr = RR(range(2))
                bank_gu = RR([0, 1, 2, 3])
                bank_dn = RR([4, 5])
                bank_st = RR([6, 7])

                def load_h(b):
                    op("pool", lambda e: e.dma_start(out=hT2[b % 2][:], in_=hscr_v[:, :, b * 512:(b + 1) * 512]),
                       reads=[hsb[b]], writes=hT2b[b % 2], dma=True)

                def norm3(b, part):
                    fm_norm(hT2[b % 2], hT2b[b % 2], hb, hbb, vec, vecb, 16, 512, fa[0], fab[0], fa[1], fab[1], 6 + (b % 2), part=part)

                load_h(0)
                norm3(0, None)
                for b in range(NB):
                    t0 = b * 512
                    hT, hTb = hT2[b % 2], hT2b[b % 2]
                    if b + 1 < NB:
                        load_h(b + 1)
                    for fc in range(NFC):
                        bg, bu = bank_gu.next(), bank_gu.next()

                        def mmg(e):
                            f = None
                            for kc in range(8):
                                ins = e.matmul(ps[bg][:, :], wgu[:, kc, fc * 128:(fc + 1) * 128], hb[:, kc, :],
                                               start=(kc == 0), stop=(kc == 7))
                                f = ins if f is None else f
                            return f, ins
                        op("pe", mmg, reads=hbb + wgub, writes=[pb[bg]])

                        def mmu(e):
                            f = None
                            for kc in range(8):
                                ins = e.matmul(ps[bu][:, :], wgu[:, kc, DFF + fc * 128:DFF + (fc + 1) * 128], hb[:, kc, :],
                                               start=(kc == 0), stop=(kc == 7))
                                f = ins if f is None else f
                            return f, ins
                        op("pe", mmu, reads=hbb + wgub, writes=[pb[bu]])
                        si = fc % 2
                        op("act", lambda e: e.activation(out=sg[si][:, :], in_=ps[bg][:, :], func=AF.Silu),
                           reads=[pb[bg]], writes=[sgb[si]])
                        op("dve", lambda e: e.tensor_tensor(out=actT[:, fc, :], in0=ps[bu][:, :], in1=sg[si][:, :], op=ALU.mult),
                           reads=[pb[bu], sgb[si]], writes=[actb[fc]])
                    if b + 1 < NB:
                        norm3(b + 1, "A")
                    for oc in range(8):
                        bk = bank_dn.next()

                        def mmd(e):
                            f = None
                            for fc in range(NFC):
                                ins = e.matmul(ps[bk][:, :], wdn[:, fc, oc * 128:(oc + 1) * 128], actT[:, fc, :],
                                               start=(fc == 0), stop=(fc == NFC - 1))
                                f = ins if f is None else f
                            return f, ins
                        op("pe", mmd, reads=actb + wdnb, writes=[pb[bk]])
                        op("dve", lambda e: e.tensor_tensor(out=hT[:, oc, :], in0=ps[bk][:, :], in1=hT[:, oc, :], op=ALU.add),
                           reads=[pb[bk], hTb[oc]], writes=[hTb[oc]])
                        if oc == 3 and b + 1 < NB:
                            norm3(b + 1, "B")
                    if last_layer:
                        for i in range(4):
                            for hf in range(2):
                                bk = bank_st.next()

                                def trb(e):
                                    f = None
                                    for k in range(4):
                                        ins = e.transpose(ps[bk][:, k * 128:(k + 1) * 128], hT[:, hf * 4 + k, i * 128:(i + 1) * 128], ident)
                                        f = ins if f is None else f
                                    return f, ins
                                op("pe", trb, reads=hTb[hf * 4:(hf + 1) * 4] + [cstb], writes=[pb[bk]])
                                oi = otr.next()
                                op("act", lambda e: e.activation(out=ot[oi][:, :], in_=ps[bk][:, :], func=AF.Copy),
                                   reads=[pb[bk]], writes=[otb[oi]])
                                op("pool", lambda e: e.dma_start(out=out_d[t0 + i * 128:t0 + (i + 1) * 128, hf * 512:(hf + 1) * 512], in_=ot[oi][:, :]),
                                   reads=[otb[oi]], dma=True)
                    else:
                        op("pool", lambda e: e.dma_start(out=hscr_v[:, :, t0:t0 + 512], in_=hT[:]),
                           reads=hTb, writes=[hsb[b]], dma=True)
                sc.barrier()
        sc.finish()
    return nc


def make_consts():
    c = np.zeros((128, 640), np.float32)
    c[:, 0:128] = np.eye(128, dtype=np.float32)
    s = np.arange(128)[:, None]
    t = np.arange(128)[None, :]
    c[:, 128:256] = (s <= t).astype(np.float32)
    c[:, 256:384] = ((s // 64) == (t // 64)).astype(np.float32)
    c[:, 384:512] = 1.0
    for g, w in enumerate((2, 4, 8, 16)):
        c[:, 512 + g * 16:512 + (g + 1) * 16] = (1.0 / np.minimum(np.arange(16) + 1, w)).astype(np.float32)[None, :]
    return c


def make_vecs(inp, NL):
    v = np.zeros((NL, 128, NVEC), np.float32)
    p = np.arange(128)
    for l in range(NL):
        for k, name in enumerate(("g_mix", "g_mem_q", "g_ffn", "g_mem_kv")):
            v[l, :, 8 * k:8 * k + 8] = np.asarray(inp[name][l]).reshape(8, 128).T
        v[l, :, 32] = np.asarray(inp["g_q_fox"][l])[p % 64]
        v[l, :, 33] = np.asarray(inp["g_k_fox"][l])[p % 64]
        v[l, :, 34] = np.asarray(inp["g_q_mem"][l])
        v[l, :, 35] = np.asarray(inp["g_k_mem"][l])
        v[l, :, 36:40] = np.asarray(inp["pool_scale"][l]).reshape(4, 128).T
        v[l, :, 40:72] = np.tile(np.asarray(inp["b_forget"][l]), 4)[None, :]
    return v


_NC_CACHE = {}


def run(inp, S, NL, ncores, trace=False):
    key = (S, NL)
    if key not in _NC_CACHE:
        _NC_CACHE[key] = build(S, NL)
    nc = _NC_CACHE[key]
    consts = make_consts()
    vecs = make_vecs(inp, NL)
    shared = {
        "w_in": np.ascontiguousarray(inp["w_in"], np.float32),
        "w_pool": np.ascontiguousarray(inp["w_pool"], np.float32),
        "w_out": np.ascontiguousarray(inp["w_out"], np.float32),
        "w_mem_q": np.ascontiguousarray(inp["w_mem_q"], np.float32),
        "w_mem_kv": np.ascontiguousarray(inp["w_mem_kv"], np.float32),
        "w_mem_out": np.ascontiguousarray(inp["w_mem_out"], np.float32),
        "w_gate_up": np.ascontiguousarray(inp["w_gate_up"], np.float32),
        "w_down": np.ascontiguousarray(inp["w_down"], np.float32),
        "vecs": vecs,
        "consts": consts,
    }
    x = np.asarray(inp["x"], np.float32)
    mem = np.asarray(inp["mem"], np.float32)
    in_maps = []
    for i in range(ncores):
        m = dict(shared)
        m["x"] = np.ascontiguousarray(x[i])
        m["mem"] = np.ascontiguousarray(mem[i])
        in_maps.append(m)
    res = run_bass_kernel_spmd(nc, in_maps, core_ids=list(range(ncores)), trace=trace)
    out = np.stack([np.asarray(r["out"], np.float32) for r in res.results], axis=0)
    return out, res


def kernel(**inputs):
    x = np.asarray(inputs["x"])
    B, S, _ = x.shape
    NL = int(np.asarray(inputs["w_in"]).shape[0])
    out, _ = run(inputs, S, NL, B)
    return out.astype(np.float32)
```

```python
import numpy as np
from contextlib import ExitStack
import concourse.bass as bass
import concourse.mybir as mybir
from concourse.bass_utils import run_bass_kernel_spmd

F32 = mybir.dt.float32
BF16 = mybir.dt.bfloat16
ALU = mybir.AluOpType
AF = mybir.ActivationFunctionType

D = 1024
EPS = 1e-6
NVEC = 72
DFF = 2816
NFC = 22
MEM = 256


class Tok:
    __slots__ = ("s", "v", "clk")

    def __init__(self, s, v, clk):
        self.s = s
        self.v = v
        self.clk = clk


class Buf:
    __slots__ = ("w", "rd", "excl")

    def __init__(self, excl=False):
        self.w = None
        self.rd = {}
        self.excl = excl


class RR:
    def __init__(self, items):
        self.items = list(items)
        self.i = 0

    def next(self):
        r = self.items[self.i % len(self.items)]
        self.i += 1
        return r


class Sched:
    COMPUTE = ("pe", "act", "dve", "pool")

    def __init__(self, nc, es, nslots=8):
        self.nc = nc
        self.h = {"pe": nc.tensor, "act": nc.scalar, "dve": nc.vector, "pool": nc.gpsimd, "sp": nc.sync}
        self.sems = []
        self.own = {}
        for e in self.COMPUTE:
            self.own[e] = len(self.sems)
            self.sems.append(es.enter_context(nc.semaphore("sem_" + e)))
        self.slots = {}
        for q in ("sp", "pool"):
            self.slots[q] = []
            for i in range(nslots):
                self.slots[q].append(len(self.sems))
                self.sems.append(es.enter_context(nc.semaphore("dma_%s_%d" % (q, i))))
        self.n = len(self.sems)
        self.cnt = [0] * self.n
        self.clk = {e: [0] * self.n for e in self.h}
        self.dman = {"sp": 0, "pool": 0}
        self.nops = 0

    def op(self, eng, fn, reads=(), writes=(), dma=False):
        clk = self.clk[eng]
        own = self.own.get(eng)
        deps = []
        wr = list(writes)
        for b in reads:
            if b.excl:
                wr.append(b)
            elif b.w is not None:
                deps.append((b.w, 0))
        for b in wr:
            if b.w is not None:
                deps.append((b.w, 1))
            for t in b.rd.values():
                deps.append((t, 2))
        waits = {}
        for t, kind in deps:
            if (not dma) and t.s == own:
                if kind != 0 or eng == "pe":
                    continue
            if clk[t.s] >= t.v:
                continue
            waits[t.s] = max(waits.get(t.s, 0), t.v)
            tc = t.clk
            for i in range(self.n):
                if tc[i] > clk[i]:
                    clk[i] = tc[i]
            if clk[t.s] < t.v:
                clk[t.s] = t.v
        if dma:
            k = self.dman[eng]
            self.dman[eng] += 1
            sl = self.slots[eng][k % len(self.slots[eng])]
            prev = self.cnt[sl]
            if clk[sl] < prev:
                waits[sl] = max(waits.get(sl, 0), prev)
                clk[sl] = prev
            self.cnt[sl] += 16
            tok = Tok(sl, self.cnt[sl], list(clk))
            inc = (sl, 16)
        else:
            self.cnt[own] += 1
            snap = list(clk)
            snap[own] = self.cnt[own]
            tok = Tok(own, self.cnt[own], snap)
            inc = (own, 1)
        for b in reads:
            if not b.excl:
                b.rd[tok.s] = tok
        for b in wr:
            b.w = tok
            b.rd = {}
        e = self.h[eng]
        wl = sorted(waits.items())
        attach = None
        if (not dma) and wl:
            attach = wl.pop()
        for s, v in wl:
            e.wait_ge(self.sems[s], v)
        r = fn(e)
        first, last = r if isinstance(r, tuple) else (r, r)
        if attach is not None:
            first._wait_ge(self.sems[attach[0]], attach[1])
        last.then_inc(self.sems[inc[0]], inc[1])
        self.nops += 1
        return tok

    def barrier(self):
        for eng, e in self.h.items():
            clk = self.clk[eng]
            for s in range(self.n):
                if s == self.own.get(eng):
                    continue
                if clk[s] < self.cnt[s]:
                    e.wait_ge(self.sems[s], self.cnt[s])
                    clk[s] = self.cnt[s]
        for eng in self.h:
            own = self.own.get(eng)
            for s in range(self.n):
                if s != own:
                    self.clk[eng][s] = self.cnt[s]

    def finish(self):
        e = self.h["sp"]
        clk = self.clk["sp"]
        for s in range(self.n):
            if clk[s] < self.cnt[s]:
                e.wait_ge(self.sems[s], self.cnt[s])
                clk[s] = self.cnt[s]


def build(S, NL):
    NB = S // 512
    NT = S // 128
    nc = bass.Bass("TRN2", target_bir_lowering=False)

    def din(name, shape, dt=F32):
        return nc.dram_tensor(name, list(shape), dt, kind="ExternalInput").ap()

    def dscr(name, shape, dt):
        return nc.dram_tensor(name, list(shape), dt, kind="Internal").ap()

    x_d = din("x", [S, D])
    mem_d = din("mem", [MEM, D])
    w_in_d = din("w_in", [NL, D, 2056])
    w_pool_d = din("w_pool", [NL, 4, 128, 128])
    w_out_d = din("w_out", [NL, D, D])
    w_mq_d = din("w_mem_q", [NL, D, 512])
    w_mkv_d = din("w_mem_kv", [NL, D, 1024])
    w_mo_d = din("w_mem_out", [NL, 512, D])
    w_gu_d = din("w_gate_up", [NL, D, 2 * DFF])
    w_dn_d = din("w_down", [NL, DFF, D])
    vecs_d = din("vecs", [NL, 128, NVEC])
    consts_d = din("consts", [128, 640])
    out_d = nc.dram_tensor("out", [S, D], F32, kind="ExternalOutput").ap()

    s_in = dscr("s_in", [NL, 128, 8, 2056], BF16)
    s_pool = dscr("s_pool", [NL, 128, 4, 128], BF16)
    s_out = dscr("s_out", [NL, 128, 8, 1024], BF16)
    s_mq = dscr("s_mq", [NL, 128, 8, 512], BF16)
    s_mkv = dscr("s_mkv", [NL, 128, 8, 1024], BF16)
    s_mo = dscr("s_mo", [NL, 128, 4, 1024], BF16)
    s_gu = dscr("s_gu", [NL, 128, 8, 2 * DFF], BF16)
    s_dn = dscr("s_dn", [NL, 128, NFC, 1024], BF16)
    hscr = dscr("hscr", [8, 128, S], F32)
    kscr = dscr("kscr", [4, 128, S], BF16)
    vscr = dscr("vscr", [4, 128, NT, 192], BF16)
    hscr_v = hscr.rearrange("k p s -> p k s")
    kscr_v = kscr.rearrange("c p s -> p c s")

    with ExitStack() as es:
        sc = Sched(nc, es)
        op = sc.op

        uniq = [0]

        def sb(name, shape, dt, stack=es):
            uniq[0] += 1
            return stack.enter_context(nc.sbuf_tensor("%s_%d" % (name, uniq[0]), list(shape), dt))

        ps = [es.enter_context(nc.psum_tensor("ps%d" % i, [128, 512], F32)) for i in range(8)]
        pb = [Buf(excl=True) for _ in range(8)]

        cst = sb("cst", [128, 640], F32)
        cstb = Buf()
        ident = cst[:, 0:128]
        triu_f = cst[:, 128:256]
        ones_f = cst[:, 384:512]
        invcnt = cst[:, 512:576]
        cbf = sb("cbf", [128, 384], BF16)
        cbfb = Buf()
        triu_b = cbf[:, 0:128]
        bd64_b = cbf[:, 128:256]
        ones_b = cbf[:, 256:384]
        epsT = sb("epsT", [128, 2], F32)
        epsb = Buf()
        op("sp", lambda e: e.dma_start(out=cst[:], in_=consts_d[:, :]), writes=[cstb], dma=True)
        op("dve", lambda e: e.tensor_copy(out=cbf[:], in_=cst[:, 128:512]), reads=[cstb], writes=[cbfb])
        op("dve", lambda e: e.memset(epsT[:, 0:1], EPS), writes=[epsb])
        op("dve", lambda e: e.memset(epsT[:, 1:2], 1.0), writes=[epsb])

        with ExitStack() as ps_es:
            CW = 2816
            NST = 3
            stf = [sb("stf%d" % i, [128, CW], F32, ps_es) for i in range(NST)]
            stb = [sb("stb%d" % i, [128, CW], BF16, ps_es) for i in range(NST)]
            stfb = [Buf() for _ in range(NST)]
            stbb = [Buf() for _ in range(NST)]
            cnt = [0]
            cast_eng = ["dve", "act", "pool"]

            def prep(src, dst, npart, ncols):
                for c0 in range(0, ncols, CW):
                    c1 = min(ncols, c0 + CW)
                    w = c1 - c0
                    i = cnt[0] % NST
                    ce = cast_eng[cnt[0] % 3]
                    cnt[0] += 1
                    op("sp", lambda e: e.dma_start(out=stf[i][0:npart, 0:w], in_=src[:, c0:c1]),
                       writes=[stfb[i]], dma=True)
                    if ce == "act":
                        op("act", lambda e: e.activation(out=stb[i][0:npart, 0:w], in_=stf[i][0:npart, 0:w], func=AF.Copy),
                           reads=[stfb[i]], writes=[stbb[i]])
                    else:
                        op(ce, lambda e: e.tensor_copy(out=stb[i][0:npart, 0:w], in_=stf[i][0:npart, 0:w]),
                           reads=[stfb[i]], writes=[stbb[i]])
                    op("pool", lambda e: e.dma_start(out=dst[:, c0:c1], in_=stb[i][0:npart, 0:w]),
                       reads=[stbb[i]], dma=True)

            for l in range(NL):
                for kc in range(8):
                    prep(w_in_d[l, kc * 128:(kc + 1) * 128, :], s_in[l, :, kc, :], 128, 2056)
                for g in range(4):
                    prep(w_pool_d[l, g, :, :], s_pool[l, :, g, :], 128, 128)
                for kc in range(8):
                    prep(w_out_d[l, kc * 128:(kc + 1) * 128, :], s_out[l, :, kc, :], 128, 1024)
                for kc in range(8):
                    prep(w_mq_d[l, kc * 128:(kc + 1) * 128, :], s_mq[l, :, kc, :], 128, 512)
                    prep(w_mkv_d[l, kc * 128:(kc + 1) * 128, :], s_mkv[l, :, kc, :], 128, 1024)
                for hm in range(4):
                    prep(w_mo_d[l, hm * 128:(hm + 1) * 128, :], s_mo[l, :, hm, :], 128, 1024)
                for kc in range(8):
                    prep(w_gu_d[l, kc * 128:(kc + 1) * 128, :], s_gu[l, :, kc, :], 128, 2 * DFF)
                for fc in range(NFC):
                    prep(w_dn_d[l, fc * 128:(fc + 1) * 128, :], s_dn[l, :, fc, :], 128, 1024)
            sc.barrier()

        hsb = [Buf() for _ in range(NB)]

        def fm_norm(hT, hTb, hb, hbb, vec, vecb, gcol, n, A0, A0b, A1, A1b, bank, part=None):
            if part in (None, "A"):
                for kc in range(8):
                    op("act", lambda e: e.activation(out=hb[:, kc, 0:n], in_=hT[:, kc, 0:n], func=AF.Square),
                       reads=[hTb[kc]], writes=[hbb[kc]])
            if part == "A":
                return

            def mm(e):
                f = None
                for kc in range(8):
                    i = e.matmul(ps[bank][:, 0:n], ones_b, hb[:, kc, 0:n], start=(kc == 0), stop=(kc == 7))
                    f = i if f is None else f
                return f, i
            op("pe", mm, reads=hbb + [cbfb], writes=[pb[bank]])
            op("act", lambda e: e.activation(out=A0[:, 0:n], in_=ps[bank][:, 0:n], func=AF.Sqrt,
                                             bias=epsT[:, 0:1], scale=1.0 / D),
               reads=[epsb, pb[bank]], writes=[A0b])
            op("dve", lambda e: e.reciprocal(out=A1[:, 0:n], in_=A0[:, 0:n]), reads=[A0b], writes=[A1b])
            for kc in range(8):
                op("dve", lambda e: e.scalar_tensor_tensor(out=hb[:, kc, 0:n], in0=hT[:, kc, 0:n],
                                                           scalar=vec[:, gcol + kc:gcol + kc + 1], in1=A1[:, 0:n],
                                                           op0=ALU.mult, op1=ALU.mult),
                   reads=[hTb[kc], A1b, vecb], writes=[hbb[kc]])

        def head_norm(bank, bank2, stat_lhsT, inv_n, n, vec, gcol, vecb, dsts, ysq, ysqb, rA, rAb, rB, rBb):
            op("act", lambda e: e.activation(out=ysq[:, 0:n], in_=ps[bank][:, 0:n], func=AF.Square),
               reads=[pb[bank]], writes=[ysqb])
            op("pe", lambda e: e.matmul(ps[bank2][:, 0:n], stat_lhsT, ysq[:, 0:n], start=True, stop=True),
               reads=[ysqb, cbfb], writes=[pb[bank2]])
            op("act", lambda e: e.activation(out=rA[:, 0:n], in_=ps[bank2][:, 0:n], func=AF.Sqrt,
                                             bias=epsT[:, 0:1], scale=inv_n),
               reads=[epsb, pb[bank2]], writes=[rAb])
            op("dve", lambda e: e.reciprocal(out=rB[:, 0:n], in_=rA[:, 0:n]), reads=[rAb], writes=[rBb])
            for (r0_, r1_, dst, dstb) in dsts:
                op("dve", lambda e: e.scalar_tensor_tensor(out=dst, in0=ps[bank][r0_:r1_, 0:n], scalar=vec[r0_:r1_, gcol:gcol + 1],
                                                           in1=rB[r0_:r1_, 0:n], op0=ALU.mult, op1=ALU.mult),
                   reads=[rBb, pb[bank], vecb], writes=[dstb])

        def head_norm_multi(items, n, vec, vecb):
            for (bank, bank2, lhs, inv_n, gcol, dsts, ysq, ysqb, r, rb) in items:
                op("act", lambda e: e.activation(out=ysq[:, 0:n], in_=ps[bank][:, 0:n], func=AF.Square),
                   reads=[pb[bank]], writes=[ysqb])
            for (bank, bank2, lhs, inv_n, gcol, dsts, ysq, ysqb, r, rb) in items:
                op("pe", lambda e: e.matmul(ps[bank2][:, 0:n], lhs, ysq[:, 0:n], start=True, stop=True),
                   reads=[ysqb, cbfb], writes=[pb[bank2]])
            for (bank, bank2, lhs, inv_n, gcol, dsts, ysq, ysqb, r, rb) in items:
                op("act", lambda e: e.activation(out=r[:, 0:n], in_=ps[bank2][:, 0:n], func=AF.Sqrt,
                                                 bias=epsT[:, 0:1], scale=inv_n),
                   reads=[epsb, pb[bank2]], writes=[rb])
            for (bank, bank2, lhs, inv_n, gcol, dsts, ysq, ysqb, r, rb) in items:
                op("dve", lambda e: e.reciprocal(out=r[:, 0:n], in_=r[:, 0:n]), reads=[rb], writes=[rb])
            for (bank, bank2, lhs, inv_n, gcol, dsts, ysq, ysqb, r, rb) in items:
                for (r0_, r1_, dst, dstb) in dsts:
                    op("dve", lambda e: e.scalar_tensor_tensor(out=dst, in0=ps[bank][r0_:r1_, 0:n], scalar=vec[r0_:r1_, gcol:gcol + 1],
                                                               in1=r[r0_:r1_, 0:n], op0=ALU.mult, op1=ALU.mult),
                       reads=[rb, pb[bank], vecb], writes=[dstb])

        def pipeline(stage_lists, delays, filler=None):
            n = len(stage_lists)
            for t in range(n + max(delays)):
                for k, d in enumerate(delays):
                    i = t - d
                    if 0 <= i < n:
                        stage_lists[i][k]()
                if filler is not None:
                    filler()

        def norm_stages(bank, bank2, lhs, inv_n, n, vec, vecb, gcol, dsts, ysq, ysqb, r, rb):
            def sq():
                op("act", lambda e: e.activation(out=ysq[:, 0:n], in_=ps[bank][:, 0:n], func=AF.Square),
                   reads=[pb[bank]], writes=[ysqb])

            def st():
                op("pe", lambda e: e.matmul(ps[bank2][:, 0:n], lhs, ysq[:, 0:n], start=True, stop=True),
                   reads=[ysqb, cbfb], writes=[pb[bank2]])

            def sr():
                op("act", lambda e: e.activation(out=r[:, 0:n], in_=ps[bank2][:, 0:n], func=AF.Sqrt,
                                                 bias=epsT[:, 0:1], scale=inv_n),
                   reads=[epsb, pb[bank2]], writes=[rb])

            def rc():
                op("dve", lambda e: e.reciprocal(out=r[:, 0:n], in_=r[:, 0:n]), reads=[rb], writes=[rb])

            def stt():
                for (r0_, r1_, dst, dstb) in dsts:
                    op("dve", lambda e: e.scalar_tensor_tensor(out=dst, in0=ps[bank][r0_:r1_, 0:n], scalar=vec[r0_:r1_, gcol:gcol + 1],
                                                               in1=r[r0_:r1_, 0:n], op0=ALU.mult, op1=ALU.mult),
                       reads=[rb, pb[bank], vecb], writes=[dstb])
            return [sq, st, sr, rc, stt]

        for l in range(NL):
            first_layer = (l == 0)
            last_layer = (l == NL - 1)
            with ExitStack() as mx:
                vec = sb("vec", [128, NVEC], F32, mx)
                vecb = Buf()
                op("sp", lambda e: e.dma_start(out=vec[:], in_=vecs_d[l, :, :]), writes=[vecb], dma=True)
                wf = sb("wf", [128, 8, 8], BF16, mx)
                wfb = Buf()
                op("sp", lambda e: e.dma_start(out=wf[:], in_=s_in[l, :, :, 2048:2056]), writes=[wfb], dma=True)
                wpl = sb("wpl", [128, 4, 128], BF16, mx)
                wplb = Buf()
                op("sp", lambda e: e.dma_start(out=wpl[:], in_=s_pool[l, :, :, :]), writes=[wplb], dma=True)

                hT2 = [sb("hT%d" % i, [128, 8, 512], F32, mx) for i in range(2)]
                hT2b = [[Buf() for _ in range(8)] for _ in range(2)]
                hb1 = sb("hb1", [128, 8, 512], BF16, mx)
                hb1b = [Buf() for _ in range(8)]
                hb2 = sb("hb2", [128, 8, 512], BF16, mx)
                hb2b = [Buf() for _ in range(8)]
                pa = [sb("pa%d" % i, [128, 512], F32, mx) for i in range(2)]
                pab = [Buf() for _ in range(2)]
                bank_pro = RR([6, 7])
                NF = 6
                ft = [sb("ft%d" % i, [128, 512], F32, mx) for i in range(NF)]
                ftb = [Buf() for _ in range(NF)]
                ftr = RR(range(NF))
                qT = sb("qT", [128, 8, 512], BF16, mx)
                qTb = [Buf() for _ in range(8)]
                kcur = sb("kcur", [128, 4, 512], BF16, mx)
                kcb = [Buf() for _ in range(4)]
                vcur = sb("vcur", [128, 4, 4, 192], BF16, mx)
                vcb = [Buf() for _ in range(4)]
                NR = 4
                ring = [sb("ring%d" % i, [128, 4096], BF16, mx) for i in range(NR)]
                ringb = [Buf() for _ in range(NR)]
                NPT = 4
                PT = [sb("PT%d" % i, [128, 512], BF16, mx) for i in range(NPT)]
                PTb = [Buf() for _ in range(NPT)]
                ptr = RR(range(NPT))
                biasb = [sb("biasb%d" % i, [128, NT], F32, mx) for i in range(8)]
                biasbb = [Buf() for _ in range(8)]
                foxT = sb("foxT", [128, 4, 512], BF16, mx)
                foxb = [Buf() for _ in range(8)]
                pin = sb("pin", [128, 4, 528], F32, mx)
                pinb = [Buf() for _ in range(4)]
                ptmp = [sb("ptmp%d" % i, [128, 528], F32, mx) for i in range(2)]
                ptmpb = [Buf() for _ in range(2)]
                mixed = sb("mixed", [128, 4, 512], BF16, mx)
                mixb = [Buf() for _ in range(4)]
                poolT = sb("poolT", [128, 4, 512], BF16, mx)
                poolb = [Buf() for _ in range(4)]
                ysq = [sb("ysq%d" % i, [128, 512], BF16, mx) for i in range(4)]
                ysqb = [Buf() for _ in range(4)]
                mqT = [sb("mqT%d" % i, [128, 512], BF16, mx) for i in range(4)]
                mqTb = [Buf() for _ in range(4)]
                xin = [sb("xin%d" % i, [128, 512], F32, mx) for i in range(3)]
                xinb = [Buf() for _ in range(3)]
                xr = RR(range(3))
                negc = sb("negc", [128, NT, 8], F32, mx)
                negcb = Buf()
                cend = sb("cend", [128, NT + 1, 8], F32, mx)
                cendb = Buf()
                fl = sb("fl", [128, 3, 32], F32, mx)
                flb = [Buf() for _ in range(3)]
                mkT = sb("mkT", [128, 4, MEM], BF16, mx)
                mkTb = Buf()
                mv = sb("mv", [128, 2, 512], BF16, mx)
                mvb = Buf()
                ksb = [Buf() for _ in range(NB)]
                vsb = [Buf() for _ in range(NB)]

                bank_proj = RR([0, 1])
                bank_stat = RR([7, 4])
                bank_S = RR([2, 3, 4])
                bank_O = RR([5, 6])
                bank_O2 = RR([(5, 6), (7, 0)])
                BANK_RB = 1

                op("pool", lambda e: e.memset(qT[:], 0.0), writes=qTb)
                op("pool", lambda e: e.memset(vcur[:, :, :, 64:128], 1.0), writes=vcb)
                op("dve", lambda e: e.memset(cend[:, 0, :], 0.0), writes=[cendb])
                op("pool", lambda e: e.memset(pin[:, :, 0:16], 0.0), writes=pinb)

                chunks = []
                for b in range(NB):
                    chunks.append(("inQ", s_in[l, :, :, 0:512], 128, (8, 512)))
                    chunks.append(("inK", s_in[l, :, :, 512:1024], 128, (8, 512)))
                    chunks.append(("inV", s_in[l, :, :, 1024:1536], 128, (8, 512)))
                    chunks.append(("inP", s_in[l, :, :, 1536:2048], 128, (8, 512)))
                    for hf in range(2):
                        chunks.append(("out", s_out[l, :, :, hf * 512:(hf + 1) * 512], 128, (8, 512)))
                    chunks.append(("mq", s_mq[l, :, :, :], 128, (8, 512)))
                    chunks.append(("mo", s_mo[l, :, :, :], 128, (4, 1024)))
                wstate = {"issued": 0, "cur": 0}

                def wview(k):
                    name, src, npart, (a, bcols) = chunks[k]
                    return ring[k % NR][0:npart, 0:a * bcols].rearrange("p (a b) -> p a b", a=a)

                def wissue(upto):
                    while wstate["issued"] < min(upto, len(chunks)):
                        k = wstate["issued"]
                        src = chunks[k][1]
                        dst = wview(k)
                        op("sp", lambda e: e.dma_start(out=dst, in_=src), writes=[ringb[k % NR]], dma=True)
                        wstate["issued"] += 1

                def wnext(name):
                    k = wstate["cur"]
                    assert chunks[k][0] == name, (chunks[k][0], name)
                    wissue(k + 1)
                    wstate["cur"] += 1
                    return wview(k), ringb[k % NR], k

                def wprefetch():
                    wissue(wstate["cur"] + NR)

                with ExitStack() as mk:
                    memT = sb("memT", [128, 8, MEM], F32, mk)
                    memTb = [Buf() for _ in range(8)]
                    mnb = sb("mnb", [128, 8, MEM], BF16, mk)
                    mnbb = [Buf() for _ in range(8)]
                    wkv = sb("wkv", [128, 8, 1024], BF16, mk)
                    wkvb = Buf()
                    op("sp", lambda e: e.dma_start(out=wkv[:], in_=s_mkv[l, :, :, :]), writes=[wkvb], dma=True)
                    for i in range(2):
                        for hf in range(2):
                            xi = xr.next()
                            op("pool", lambda e: e.dma_start(out=xin[xi][:], in_=mem_d[i * 128:(i + 1) * 128, hf * 512:(hf + 1) * 512]),
                               writes=[xinb[xi]], dma=True)
                            bk = bank_proj.next()

                            def tr(e):
                                f = None
                                for k in range(4):
                                    ins = e.transpose(ps[bk][:, k * 128:(k + 1) * 128], xin[xi][:, k * 128:(k + 1) * 128], ident)
                                    f = ins if f is None else f
                                return f, ins
                            op("pe", tr, reads=[xinb[xi], cstb], writes=[pb[bk]])
                            op("dve", lambda e: e.tensor_copy(out=memT[:, hf * 4:(hf + 1) * 4, i * 128:(i + 1) * 128],
                                                              in_=ps[bk][:, 0:512].rearrange("p (k t) -> p k t", k=4)),
                               reads=[pb[bk]], writes=memTb[hf * 4:(hf + 1) * 4])
                    a0, a1 = ftr.next(), ftr.next()
                    fm_norm(memT, memTb, mnb, mnbb, vec, vecb, 24, MEM, ft[a0], ftb[a0], ft[a1], ftb[a1], bank_stat.next())
                    for hm in range(4):
                        bk = bank_proj.next()

                        def mmk(e):
                            f = None
                            for kc in range(8):
                                ins = e.matmul(ps[bk][:, 0:MEM], wkv[:, kc, hm * 128:(hm + 1) * 128], mnb[:, kc, :],
                                               start=(kc == 0), stop=(kc == 7))
                                f = ins if f is None else f
                            return f, ins
                        op("pe", mmk, reads=mnbb + [wkvb], writes=[pb[bk]])
                        a0, a1 = ftr.next(), ftr.next()
                        head_norm(bk, bank_stat.next(), ones_b, 1.0 / 128, MEM, vec, 35, vecb, [(0, 128, mkT[:, hm, :], mkTb)],
                                  ysq[hm % 2], ysqb[hm % 2], ft[a0], ftb[a0], ft[a1], ftb[a1])
                    for mt in range(2):
                        bk = bank_proj.next()

                        def mmv(e):
                            f = None
                            for kc in range(8):
                                ins = e.matmul(ps[bk][:, :], mnb[:, kc, mt * 128:(mt + 1) * 128], wkv[:, kc, 512:1024],
                                               start=(kc == 0), stop=(kc == 7))
                                f = ins if f is None else f
                            return f, ins
                        op("pe", mmv, reads=mnbb + [wkvb], writes=[pb[bk]])
                        op("dve", lambda e: e.tensor_copy(out=mv[:, mt, :], in_=ps[bk][:, :]), reads=[pb[bk]], writes=[mvb])
                    sc.barrier()

                CK = 8
                NCH = 4
                Kc = [sb("Kc%d" % i, [128, CK * 128], BF16, mx) for i in range(NCH)]
                Vc = [sb("Vc%d" % i, [128, CK, 192], BF16, mx) for i in range(NCH)]
                kvb = [Buf() for _ in range(NCH)]
                fz = [sb("fz%d" % i, [128, 512], F32, mx) for i in range(4)]
                fzb = [Buf() for _ in range(4)]
                fzr = RR(range(4))
                kvchunks = []
                kvbase = {}
                for b_ in range(1, NB):
                    for c_ in range(4):
                        kvbase[(b_, c_)] = len(kvchunks)
                        for lo in range(0, 4 * b_, CK):
                            kvchunks.append((b_, c_, lo, min(lo + CK, 4 * b_)))
                kvstate = {"issued": 0, "released": 0}

                def kv_issue():
                    lim = min(len(kvchunks), kvstate["released"] + NCH)
                    while kvstate["issued"] < lim:
                        k = kvstate["issued"]
                        b_, c_, lo, hi = kvchunks[k]
                        if b_ > kvstate["maxb"]:
                            break
                        sl = k % NCH
                        op("sp", lambda e: e.dma_start(out=Kc[sl][:, 0:(hi - lo) * 128], in_=kscr[c_, :, lo * 128:hi * 128]),
                           reads=ksb[0:b_], writes=[kvb[sl]], dma=True)
                        op("sp", lambda e: e.dma_start(out=Vc[sl][:, 0:hi - lo, :], in_=vscr[c_, :, lo:hi, :]),
                           reads=vsb[0:b_], writes=[kvb[sl]], dma=True)
                        kvstate["issued"] += 1
                kvstate["maxb"] = 0

                def prologue(bn):
                    hTn, hTnb = hT2[bn % 2], hT2b[bn % 2]
                    tn = bn * 512
                    if first_layer:
                        for i in range(4):
                            for hf in range(2):
                                xi = xr.next()
                                op("pool", lambda e: e.dma_start(out=xin[xi][:], in_=x_d[tn + i * 128:tn + (i + 1) * 128, hf * 512:(hf + 1) * 512]),
                                   writes=[xinb[xi]], dma=True)
                                bk = bank_pro.next()

                                def tr(e):
                                    f = None
                                    for k in range(4):
                                        ins = e.transpose(ps[bk][:, k * 128:(k + 1) * 128], xin[xi][:, k * 128:(k + 1) * 128], ident)
                                        f = ins if f is None else f
                                    return f, ins
                                op("pe", tr, reads=[xinb[xi], cstb], writes=[pb[bk]])
                                op("dve", lambda e: e.tensor_copy(out=hTn[:, hf * 4:(hf + 1) * 4, i * 128:(i + 1) * 128],
                                                                  in_=ps[bk][:, 0:512].rearrange("p (k t) -> p k t", k=4)),
                                   reads=[pb[bk]], writes=hTnb[hf * 4:(hf + 1) * 4])
                            yield
                    else:
                        op("pool", lambda e: e.dma_start(out=hTn[:], in_=hscr_v[:, :, tn:tn + 512]),
                           reads=[hsb[bn]], writes=hTnb, dma=True)
                        yield
                    bkn = bank_pro.next()
                    fm_norm(hTn, hTnb, hb1, hb1b, vec, vecb, 0, 512, pa[0], pab[0], pa[1], pab[1], bkn, part="A")
                    yield
                    fm_norm(hTn, hTnb, hb1, hb1b, vec, vecb, 0, 512, pa[0], pab[0], pa[1], pab[1], bkn, part="B")
                    yield

                for _ in prologue(0):
                    pass
                for b in range(NB):
                    t0 = b * 512
                    wprefetch()
                    hT, hTb = hT2[b % 2], hT2b[b % 2]
                    hb, hbb = hb1, hb1b

                    wq_, wqb_, _ = wnext("inQ")
                    wk_, wkb_, _ = wnext("inK")
                    ysl = [(ysq[i], ysqb[i]) for i in range(4)]
                    stage_lists = []
                    for it in range(8):
                        which, c = ("inQ", it) if it < 4 else ("inK", it - 4)
                        wv_, wb_ = (wq_, wqb_) if it < 4 else (wk_, wkb_)
                        bk = it % 4
                        if which == "inQ":
                            dsts, gcol = [(0, 64, qT[0:64, 2 * c, :], qTb[2 * c]), (64, 128, qT[64:128, 2 * c + 1, :], qTb[2 * c + 1])], 32
                        else:
                            dsts, gcol = [(0, 128, kcur[:, c, :], kcb[c])], 33

                        def proj(bk=bk, wv_=wv_, wb_=wb_, c=c):
                            def mmq(e):
                                f = None
                                for kc in range(8):
                                    ins = e.matmul(ps[bk][:, :], wv_[:, kc, c * 128:(c + 1) * 128], hb[:, kc, :],
                                                   start=(kc == 0), stop=(kc == 7))
                                    f = ins if f is None else f
                                return f, ins
                            op("pe", mmq, reads=hbb + [wb_], writes=[pb[bk]])
                        ys, ysb_ = ysl[it % 4]
                        stage_lists.append([proj] + norm_stages(bk, 4 + (it % 2), bd64_b, 1.0 / 64, 512, vec, vecb, gcol, dsts,
                                                                ys, ysb_, ft[it % 4], ftb[it % 4]))
                    pipeline(stage_lists, [0, 1, 1, 2, 2, 3])
                    wprefetch()
                    if b < NB - 1:
                        op("pool", lambda e: e.dma_start(out=kscr_v[:, :, t0:t0 + 512], in_=kcur[:]),
                           reads=kcb, writes=[ksb[b]], dma=True)

                    wv_, wb_, _ = wnext("inV")
                    BF = 4
                    for i in range(4):
                        bk = i

                        def mmv2(e):
                            f = None
                            for kc in range(8):
                                ins = e.matmul(ps[bk][:, :], hb[:, kc, i * 128:(i + 1) * 128], wv_[:, kc, :],
                                               start=(kc == 0), stop=(kc == 7))
                                f = ins if f is None else f
                            return f, ins
                        op("pe", mmv2, reads=hbb + [wb_], writes=[pb[bk]])

                    def mmf(e):
                        f = None
                        for i in range(4):
                            for kc in range(8):
                                ins = e.matmul(ps[BF][:, i * 8:(i + 1) * 8], hb[:, kc, i * 128:(i + 1) * 128], wf[:, kc, :],
                                               start=(kc == 0), stop=(kc == 7))
                                f = ins if f is None else f
                        return f, ins
                    op("pe", mmf, reads=hbb + [wfb], writes=[pb[BF]])
                    for i in range(4):
                        pv4 = ps[i][:, 0:512].rearrange("p (c two d) -> p c two d", c=4, two=2)
                        op("dve", lambda e: e.tensor_copy(out=vcur[:, i, :, 0:64], in_=pv4[:, :, 0, :]),
                           reads=[pb[i]], writes=[vcb[i]])
                        op("dve", lambda e: e.tensor_copy(out=vcur[:, i, :, 128:192], in_=pv4[:, :, 1, :]),
                           reads=[pb[i]], writes=[vcb[i]])
                    op("dve", lambda e: e.tensor_tensor(out=fl[:, 0, :], in0=ps[BF][:, 0:32], in1=vec[:, 40:72], op=ALU.add),
                       reads=[pb[BF], vecb], writes=[flb[0]])
                    op("act", lambda e: e.activation(out=fl[:, 1, :], in_=fl[:, 0, :], func=AF.Exp, scale=-1.0),
                       reads=[flb[0]], writes=[flb[1]])
                    op("act", lambda e: e.activation(out=fl[:, 2, :], in_=fl[:, 1, :], func=AF.Ln, bias=epsT[:, 1:2], scale=1.0),
                       reads=[flb[1], epsb], writes=[flb[2]])
                    BC = 5

                    def mmc(e):
                        f = None
                        for i in range(4):
                            ins = e.matmul(ps[BC][:, i * 8:(i + 1) * 8], ident, cend[:, 4 * b, :], start=True, stop=False)
                            f = ins if f is None else f
                            for i2 in range(i):
                                e.matmul(ps[BC][:, i * 8:(i + 1) * 8], ones_f, fl[:, 2, i2 * 8:(i2 + 1) * 8], start=False, stop=False)
                            e.matmul(ps[BC][:, i * 8:(i + 1) * 8], triu_f, fl[:, 2, i * 8:(i + 1) * 8], start=False, stop=True)
                            e.matmul(ps[BC][:, 32 + i * 8:32 + (i + 1) * 8], ident, cend[:, 4 * b, :], start=True, stop=False)
                            for i2 in range(i + 1):
                                ins = e.matmul(ps[BC][:, 32 + i * 8:32 + (i + 1) * 8], ones_f, fl[:, 2, i2 * 8:(i2 + 1) * 8],
                                               start=False, stop=(i2 == i))
                        return f, ins
                    op("pe", mmc, reads=[flb[2], cstb, cendb], writes=[pb[BC]])
                    op("dve", lambda e: e.tensor_copy(out=negc[:, 4 * b:4 * b + 4, :], in_=ps[BC][:, 0:32].rearrange("p (t h) -> p t h", t=4)),
                       reads=[pb[BC]], writes=[negcb])
                    op("dve", lambda e: e.tensor_copy(out=cend[:, 4 * b + 1:4 * b + 5, :], in_=ps[BC][:, 32:64].rearrange("p (t h) -> p t h", t=4)),
                       reads=[pb[BC]], writes=[cendb])
                    for h in range(8):
                        op("dve", lambda e: e.tensor_scalar(out=biasb[h][:, 0:4 * b + 4], in0=negc[:, 0:4 * b + 4, h],
                                                            scalar1=cend[:, 4 * b + 2, h:h + 1], scalar2=None, op0=ALU.subtract),
                           reads=[negcb, cendb], writes=[biasbb[h]])
                    if b < NB - 1:
                        for c in range(4):
                            op("pool", lambda e: e.dma_start(out=vscr[c, :, 4 * b:4 * b + 4, :],
                                                             in_=vcur[:, :, c, :]),
                               reads=vcb, writes=[vsb[b]], dma=True)
                    wprefetch()

                    wv_, wb_, _ = wnext("inP")
                    for g in range(4):
                        bk = bank_proj.next()

                        def mmp(e):
                            f = None
                            for kc in range(8):
                                ins = e.matmul(ps[bk][:, :], wv_[:, kc, g * 128:(g + 1) * 128], hb[:, kc, :],
                                               start=(kc == 0), stop=(kc == 7))
                                f = ins if f is None else f
                            return f, ins
                        op("pe", mmp, reads=hbb + [wb_], writes=[pb[bk]])
                        op("act", lambda e: e.activation(out=pin[:, g, 16:528], in_=ps[bk][:, :], func=AF.Copy),
                           reads=[pb[bk]], writes=[pinb[g]])
                    wprefetch()

                    for g in range(4):
                        w = 2 << g
                        cur, curb = pin[:, g, :], pinb[g]
                        lo, off = 0, 1
                        for st in range(g + 1):
                            nx, nxb = ptmp[st % 2], ptmpb[st % 2]
                            lo2 = lo + off
                            op("pool", lambda e: e.tensor_tensor(out=nx[:, lo2:528], in0=cur[:, lo2:528], in1=cur[:, lo:528 - off], op=ALU.add),
                               reads=[curb], writes=[nxb])
                            cur, curb = nx[:, :], nxb
                            lo, off = lo2, off * 2
                        sc_t, sc_b = ptmp[(g + 1) % 2], ptmpb[(g + 1) % 2]
                        op("pool", lambda e: e.tensor_scalar(out=sc_t[:, 16:528], in0=cur[:, 16:528], scalar1=1.0 / w, scalar2=None, op0=ALU.mult),
                           reads=[curb], writes=[sc_b])
                        op("pool", lambda e: e.tensor_tensor(out=mixed[:, g, :], in0=sc_t[:, 16:528], in1=pin[:, g, 16:528], op=ALU.subtract),
                           reads=[sc_b, pinb[g]], writes=[mixb[g]])
                        if b == 0:
                            oth, othb = ptmp[(g + 1) % 2], ptmpb[(g + 1) % 2]
                            op("pool", lambda e: e.tensor_tensor(out=oth[:, 0:16], in0=cur[:, 16:32], in1=invcnt[:, g * 16:(g + 1) * 16], op=ALU.mult),
                               reads=[curb, cstb], writes=[othb])
                            op("pool", lambda e: e.tensor_tensor(out=mixed[:, g, 0:16], in0=oth[:, 0:16], in1=pin[:, g, 16:32], op=ALU.subtract),
                               reads=[othb, pinb[g]], writes=[mixb[g]])
                    op("pool", lambda e: e.tensor_copy(out=pin[:, :, 0:16], in_=pin[:, :, 512:528]), reads=[], writes=pinb)

                    nj = 4 * b + 4
                    items = [(2 * c + hh, j) for c in range(4) for j in range(nj) for hh in range(2)]
                    SKEW = 3
                    st_ = {}
                    obank = {}
                    pend = []
                    kvstate["maxb"] = b
                    kv_issue()

                    def kv_of(c, j):
                        q = j // CK
                        k = kvbase[(b, c)] + q
                        assert k < kvstate["issued"], (b, c, j, k, kvstate)
                        return k % NCH, j - q * CK, k

                    def emit_S(h, j):
                        c = h // 2
                        if j < 4 * b:
                            sl, jj, _ = kv_of(c, j)
                            lhsT = Kc[sl][:, jj * 128:(jj + 1) * 128]
                            kb = kvb[sl]
                            col0 = 0
                        else:
                            i = j - 4 * b
                            lhsT = kcur[:, c, i * 128:(i + 1) * 128]
                            kb = kcb[c]
                            col0 = i * 128
                        bs = bank_S.next()
                        op("pe", lambda e: e.matmul(ps[bs][:, col0:512], lhsT, qT[:, h, col0:512], start=True, stop=True),
                           reads=[kb, qTb[h]], writes=[pb[bs]])
                        pt = ptr.next()
                        op("act", lambda e: e.activation(out=PT[pt][:, col0:512], in_=ps[bs][:, col0:512], func=AF.Exp,
                                                         bias=biasb[h][:, j:j + 1], scale=0.125),
                           reads=[pb[bs], biasbb[h]], writes=[PTb[pt]])
                        if j >= 4 * b:
                            op("dve", lambda e: e.tensor_tensor(out=PT[pt][:, col0:col0 + 128], in0=PT[pt][:, col0:col0 + 128],
                                                                in1=triu_b, op=ALU.mult),
                               reads=[cbfb], writes=[PTb[pt]])
                        st_[(h, j)] = (pt, col0)

                    def emit_PV(h, j):
                        c = h // 2
                        pt, col0 = st_.pop((h, j))
                        if j == 0 and h % 2 == 0:
                            bo2 = bank_O2.next()
                            obank[h], obank[h + 1] = bo2
                        bo = obank[h]
                        v0 = 0 if h % 2 == 0 else 64
                        if j < 4 * b:
                            sl, jj, k = kv_of(c, j)
                            lhsT = Vc[sl][:, jj, v0:v0 + 128]
                            vb_ = kvb[sl]
                        else:
                            lhsT = vcur[:, j - 4 * b, c, v0:v0 + 128]
                            vb_ = vcb[j - 4 * b]
                        op("pe", lambda e: e.matmul(ps[bo][:, col0:512], lhsT, PT[pt][:, col0:512], start=(j == 0), stop=(j == nj - 1)),
                           reads=[PTb[pt], vb_], writes=[pb[bo]])
                        if j < 4 * b and h % 2 == 1:
                            _, _, k = kv_of(c, j)
                            lo, hi = kvchunks[k][2], kvchunks[k][3]
                            if j == hi - 1:
                                kvstate["released"] = k + 1
                                kv_issue()
                        if j == nj - 1:
                            finalize(h, bo)

                    def finalize(h, bo):
                        c = h // 2
                        r1, r2 = fzr.next(), fzr.next()
                        if h % 2 == 0:
                            rr, o0 = 64, 0
                        else:
                            rr, o0 = 0, 64
                        op("dve", lambda e: e.reciprocal(out=fz[r1][rr:rr + 1, :], in_=ps[bo][rr:rr + 1, :]),
                           reads=[pb[bo]], writes=[fzb[r1]])

                        def fin_pe():
                            if h % 2 == 0:
                                op("pe", lambda e: e.matmul(ps[BANK_RB][0:64, :], ones_f[64:65, 0:64], fz[r1][64:65, :], start=True, stop=True),
                                   reads=[fzb[r1], cstb], writes=[pb[BANK_RB]])
                            else:
                                op("pe", lambda e: e.matmul(ps[BANK_RB][:, :], ones_f[0:1, :], fz[r1][0:1, :], start=True, stop=True),
                                   reads=[fzb[r1], cstb], writes=[pb[BANK_RB]])
                            op("dve", lambda e: e.tensor_copy(out=fz[r2][o0:o0 + 64, :], in_=ps[BANK_RB][o0:o0 + 64, :]),
                               reads=[pb[BANK_RB]], writes=[fzb[r2]])
                            op("dve", lambda e: e.tensor_tensor(out=foxT[o0:o0 + 64, c, :], in0=ps[bo][o0:o0 + 64, :], in1=fz[r2][o0:o0 + 64, :], op=ALU.mult),
                               reads=[pb[bo], fzb[r2]], writes=[foxb[h]])
                        pend.append([6, fin_pe])

                    def tick():
                        for p in list(pend):
                            p[0] -= 1
                            if p[0] <= 0:
                                p[1]()
                                pend.remove(p)

                    for idx in range(len(items) + SKEW):
                        if idx < len(items):
                            emit_S(*items[idx])
                        if idx >= SKEW:
                            emit_PV(*items[idx - SKEW])
                        tick()
                    while pend:
                        tick()
                    kvstate["maxb"] = b + 1
                    kv_issue()

                    for g in range(4):
                        bk = bank_proj.next()
                        op("pe", lambda e: e.matmul(ps[bk][:, :], wpl[:, g, :], mixed[:, g, :], start=True, stop=True),
                           reads=[mixb[g], wplb], writes=[pb[bk]])
                        op("act", lambda e: e.activation(out=poolT[:, g, :], in_=ps[bk][:, :], func=AF.Copy, scale=vec[:, 36 + g:37 + g]),
                           reads=[pb[bk], vecb], writes=[poolb[g]])

                    for hf in range(2):
                        wW, wWb, _ = wnext("out")
                        for oc in range(4):
                            kc = hf * 4 + oc
                            bk = bank_proj.next()

                            def mmo(e):
                                f = None
                                for c in range(4):
                                    ins = e.matmul(ps[bk][:, :], wW[:, c, oc * 128:(oc + 1) * 128], foxT[:, c, :],
                                                   start=(c == 0), stop=False)
                                    f = ins if f is None else f
                                for g in range(4):
                                    ins = e.matmul(ps[bk][:, :], wW[:, 4 + g, oc * 128:(oc + 1) * 128], poolT[:, g, :],
                                                   start=False, stop=(g == 3))
                                return f, ins
                            op("pe", mmo, reads=foxb + poolb + [wWb], writes=[pb[bk]])
                            op("dve", lambda e: e.tensor_tensor(out=hT[:, kc, :], in0=ps[bk][:, :], in1=hT[:, kc, :], op=ALU.add),
                               reads=[pb[bk], hTb[kc]], writes=[hTb[kc]])
                        wprefetch()

                    a0, a1 = ftr.next(), ftr.next()
                    hb, hbb = hb2, hb2b
                    fm_norm(hT, hTb, hb, hbb, vec, vecb, 8, 512, ft[a0], ftb[a0], ft[a1], ftb[a1], bank_stat.next())
                    gen = prologue(b + 1) if b + 1 < NB else iter(())
                    wQ, wQb, _ = wnext("mq")
                    moT = foxT
                    stage_lists = []
                    for hm in range(4):
                        bk = hm
                        p0, p1 = 2 * (hm % 2), 2 * (hm % 2) + 1
                        r1 = 4 + (hm % 2)
                        bst = 4 + (hm % 2)

                        def proj(bk=bk, hm=hm):
                            def mmq2(e):
                                f = None
                                for kc in range(8):
                                    ins = e.matmul(ps[bk][:, :], wQ[:, kc, hm * 128:(hm + 1) * 128], hb[:, kc, :],
                                                   start=(kc == 0), stop=(kc == 7))
                                    f = ins if f is None else f
                                return f, ins
                            op("pe", mmq2, reads=hbb + [wQb], writes=[pb[bk]])

                        def scores(hm=hm):
                            for mt in range(2):
                                op("pe", lambda e: e.matmul(ps[6 + mt][:, :], mkT[:, hm, mt * 128:(mt + 1) * 128], mqT[hm][:, :], start=True, stop=True),
                                   reads=[mkTb, mqTb[hm]], writes=[pb[6 + mt]])

                        def exps(hm=hm, p0=p0):
                            for mt in range(2):
                                op("act", lambda e: e.activation(out=PT[p0 + mt][:, :], in_=ps[6 + mt][:, :], func=AF.Exp, scale=128.0 ** -0.5),
                                   reads=[pb[6 + mt]], writes=[PTb[p0 + mt]])

                        def pv(hm=hm, bk=bk, p0=p0, p1=p1, bst=bst):
                            def mmpv(e):
                                i1 = e.matmul(ps[bk][:, :], mv[:, 0, hm * 128:(hm + 1) * 128], PT[p0][:, :], start=True, stop=False)
                                e.matmul(ps[bk][:, :], mv[:, 1, hm * 128:(hm + 1) * 128], PT[p1][:, :], start=False, stop=True)
                                e.matmul(ps[bst][:, :], ones_b, PT[p0][:, :], start=True, stop=False)
                                i2 = e.matmul(ps[bst][:, :], ones_b, PT[p1][:, :], start=False, stop=True)
                                return i1, i2
                            op("pe", mmpv, reads=[mvb, cbfb, PTb[p0], PTb[p1]], writes=[pb[bk], pb[bst]])

                        def fin(hm=hm, bk=bk, r1=r1, bst=bst):
                            op("dve", lambda e: e.reciprocal(out=ft[r1][:, :], in_=ps[bst][:, :]), reads=[pb[bst]], writes=[ftb[r1]])
                            op("dve", lambda e: e.tensor_tensor(out=moT[:, hm, :], in0=ps[bk][:, :], in1=ft[r1][:, :], op=ALU.mult),
                               reads=[pb[bk], ftb[r1]], writes=[foxb[2 * hm], foxb[2 * hm + 1]])
                        stage_lists.append([proj] + norm_stages(bk, bst, ones_b, 1.0 / 128, 512, vec, vecb, 34,
                                                                [(0, 128, mqT[hm][:, :], mqTb[hm])], ysq[hm], ysqb[hm], ft[hm], ftb[hm])
                                           + [scores, exps, pv, fin])
                    pipeline(stage_lists, [0, 1, 1, 2, 2, 3, 3, 3, 4, 4], filler=lambda: next(gen, None))
                    for _ in gen:
                        pass
                    wprefetch()
                    wO, wOb, _ = wnext("mo")
                    for oc in range(8):
                        bk = bank_proj.next()

                        def mmo2(e):
                            f = None
                            for hm in range(4):
                                ins = e.matmul(ps[bk][:, :], wO[:, hm, oc * 128:(oc + 1) * 128], moT[:, hm, :],
                                               start=(hm == 0), stop=(hm == 3))
                                f = ins if f is None else f
                            return f, ins
                        op("pe", mmo2, reads=foxb + [wOb], writes=[pb[bk]])
                        op("dve", lambda e: e.tensor_tensor(out=hT[:, oc, :], in0=ps[bk][:, :], in1=hT[:, oc, :], op=ALU.add),
                           reads=[pb[bk], hTb[oc]], writes=[hTb[oc]])
                    wprefetch()
                    op("pool", lambda e: e.dma_start(out=hscr_v[:, :, t0:t0 + 512], in_=hT[:]),
                       reads=hTb, writes=[hsb[b]], dma=True)
                sc.barrier()

            with ExitStack() as fx:
                vec = sb("vecF", [128, NVEC], F32, fx)
                vecb = Buf()
                op("sp", lambda e: e.dma_start(out=vec[:], in_=vecs_d[l, :, :]), writes=[vecb], dma=True)
                wgu = sb("wgu", [128, 8, 2 * DFF], BF16, fx)
                wgub = [Buf() for _ in range(8)]
                wdn = sb("wdn", [128, NFC, 1024], BF16, fx)
                wdnb = [Buf() for _ in range(NFC)]
                for kc in range(8):
                    op("sp", lambda e: e.dma_start(out=wgu[:, kc, :], in_=s_gu[l, :, kc, :]), writes=[wgub[kc]], dma=True)
                for f0 in range(0, NFC, 6):
                    f1 = min(NFC, f0 + 6)
                    op("sp", lambda e: e.dma_start(out=wdn[:, f0:f1, :], in_=s_dn[l, :, f0:f1, :]), writes=wdnb[f0:f1], dma=True)
                hT2 = [sb("hTF%d" % i, [128, 8, 512], F32, fx) for i in range(2)]
                hT2b = [[Buf() for _ in range(8)] for _ in range(2)]
                hb = sb("hbF", [128, 8, 512], BF16, fx)
                hbb = [Buf() for _ in range(8)]
                actT = sb("actT", [128, NFC, 512], BF16, fx)
                actb = [Buf() for _ in range(NFC)]
                sg = [sb("sg%d" % i, [128, 512], F32, fx) for i in range(2)]
                sgb = [Buf() for _ in range(2)]
                fa = [sb("fa%d" % i, [128, 512], F32, fx) for i in range(2)]
                fab = [Buf() for _ in range(2)]
                ot, otb = sg, sgb
                otr = RR(range(2))
                bank_gu = RR([0, 1, 2, 3])
                bank_dn = RR([4, 5])
                bank_st = RR([6, 7])

                def load_h(b):
                    op("pool", lambda e: e.dma_start(out=hT2[b % 2][:], in_=hscr_v[:, :, b * 512:(b + 1) * 512]),
                       reads=[hsb[b]], writes=hT2b[b % 2], dma=True)

                def norm3(b, part):
                    fm_norm(hT2[b % 2], hT2b[b % 2], hb, hbb, vec, vecb, 16, 512, fa[0], fab[0], fa[1], fab[1], 6 + (b % 2), part=part)

                load_h(0)
                norm3(0, None)
                for b in range(NB):
                    t0 = b * 512
                    hT, hTb = hT2[b % 2], hT2b[b % 2]
                    if b + 1 < NB:
                        load_h(b + 1)
                    for fc in range(NFC):
                        bg, bu = bank_gu.next(), bank_gu.next()

                        def mmg(e):
                            f = None
                            for kc in range(8):
                                ins = e.matmul(ps[bg][:, :], wgu[:, kc, fc * 128:(fc + 1) * 128], hb[:, kc, :],
                                               start=(kc == 0), stop=(kc == 7))
                                f = ins if f is None else f
                            return f, ins
                        op("pe", mmg, reads=hbb + wgub, writes=[pb[bg]])

                        def mmu(e):
                            f = None
                            for kc in range(8):
                                ins = e.matmul(ps[bu][:, :], wgu[:, kc, DFF + fc * 128:DFF + (fc + 1) * 128], hb[:, kc, :],
                                               start=(kc == 0), stop=(kc == 7))
                                f = ins if f is None else f
                            return f, ins
                        op("pe", mmu, reads=hbb + wgub, writes=[pb[bu]])
                        si = fc % 2
                        op("act", lambda e: e.activation(out=sg[si][:, :], in_=ps[bg][:, :], func=AF.Silu),
                           reads=[pb[bg]], writes=[sgb[si]])
                        op("dve", lambda e: e.tensor_tensor(out=actT[:, fc, :], in0=ps[bu][:, :], in1=sg[si][:, :], op=ALU.mult),
                           reads=[pb[bu], sgb[si]], writes=[actb[fc]])
                    if b + 1 < NB:
                        norm3(b + 1, "A")
                    for oc in range(8):
                        bk = bank_dn.next()

                        def mmd(e):
                            f = None
                            for fc in range(NFC):
                                ins = e.matmul(ps[bk][:, :], wdn[:, fc, oc * 128:(oc + 1) * 128], actT[:, fc, :],
                                               start=(fc == 0), stop=(fc == NFC - 1))
                                f = ins if f is None else f
                            return f, ins
                        op("pe", mmd, reads=actb + wdnb, writes=[pb[bk]])
                        op("dve", lambda e: e.tensor_tensor(out=hT[:, oc, :], in0=ps[bk][:, :], in1=hT[:, oc, :], op=ALU.add),
                           reads=[pb[bk], hTb[oc]], writes=[hTb[oc]])
                        if oc == 3 and b + 1 < NB:
                            norm3(b + 1, "B")
                    if last_layer:
                        for i in range(4):
                            for hf in range(2):
                                bk = bank_st.next()

                                def trb(e):
                                    f = None
                                    for k in range(4):
                                        ins = e.transpose(ps[bk][:, k * 128:(k + 1) * 128], hT[:, hf * 4 + k, i * 128:(i + 1) * 128], ident)
                                        f = ins if f is None else f
                                    return f, ins
                                op("pe", trb, reads=hTb[hf * 4:(hf + 1) * 4] + [cstb], writes=[pb[bk]])
                                oi = otr.next()
                                op("act", lambda e: e.activation(out=ot[oi][:, :], in_=ps[bk][:, :], func=AF.Copy),
                                   reads=[pb[bk]], writes=[otb[oi]])
                                op("pool", lambda e: e.dma_start(out=out_d[t0 + i * 128:t0 + (i + 1) * 128, hf * 512:(hf + 1) * 512], in_=ot[oi][:, :]),
                                   reads=[otb[oi]], dma=True)
                    else:
                        op("pool", lambda e: e.dma_start(out=hscr_v[:, :, t0:t0 + 512], in_=hT[:]),
                           reads=hTb, writes=[hsb[b]], dma=True)
                sc.barrier()
        sc.finish()
    return nc


def make_consts():
    c = np.zeros((128, 640), np.float32)
    c[:, 0:128] = np.eye(128, dtype=np.float32)
    s = np.arange(128)[:, None]
    t = np.arange(128)[None, :]
    c[:, 128:256] = (s <= t).astype(np.float32)
    c[:, 256:384] = ((s // 64) == (t // 64)).astype(np.float32)
    c[:, 384:512] = 1.0
    for g, w in enumerate((2, 4, 8, 16)):
        c[:, 512 + g * 16:512 + (g + 1) * 16] = (1.0 / np.minimum(np.arange(16) + 1, w)).astype(np.float32)[None, :]
    return c


def make_vecs(inp, NL):
    v = np.zeros((NL, 128, NVEC), np.float32)
    p = np.arange(128)
    for l in range(NL):
        for k, name in enumerate(("g_mix", "g_mem_q", "g_ffn", "g_mem_kv")):
            v[l, :, 8 * k:8 * k + 8] = np.asarray(inp[name][l]).reshape(8, 128).T
        v[l, :, 32] = np.asarray(inp["g_q_fox"][l])[p % 64]
        v[l, :, 33] = np.asarray(inp["g_k_fox"][l])[p % 64]
        v[l, :, 34] = np.asarray(inp["g_q_mem"][l])
        v[l, :, 35] = np.asarray(inp["g_k_mem"][l])
        v[l, :, 36:40] = np.asarray(inp["pool_scale"][l]).reshape(4, 128).T
        v[l, :, 40:72] = np.tile(np.asarray(inp["b_forget"][l]), 4)[None, :]
    return v


_NC_CACHE = {}


def run(inp, S, NL, ncores, trace=False):
    key = (S, NL)
    if key not in _NC_CACHE:
        _NC_CACHE[key] = build(S, NL)
    nc = _NC_CACHE[key]
    consts = make_consts()
    vecs = make_vecs(inp, NL)
    shared = {
        "w_in": np.ascontiguousarray(inp["w_in"], np.float32),
        "w_pool": np.ascontiguousarray(inp["w_pool"], np.float32),
        "w_out": np.ascontiguousarray(inp["w_out"], np.float32),
        "w_mem_q": np.ascontiguousarray(inp["w_mem_q"], np.float32),
        "w_mem_kv": np.ascontiguousarray(inp["w_mem_kv"], np.float32),
        "w_mem_out": np.ascontiguousarray(inp["w_mem_out"], np.float32),
        "w_gate_up": np.ascontiguousarray(inp["w_gate_up"], np.float32),
        "w_down": np.ascontiguousarray(inp["w_down"], np.float32),
        "vecs": vecs,
        "consts": consts,
    }
    x = np.asarray(inp["x"], np.float32)
    mem = np.asarray(inp["mem"], np.float32)
    in_maps = []
    for i in range(ncores):
        m = dict(shared)
        m["x"] = np.ascontiguousarray(x[i])
        m["mem"] = np.ascontiguousarray(mem[i])
        in_maps.append(m)
    res = run_bass_kernel_spmd(nc, in_maps, core_ids=list(range(ncores)), trace=trace)
    out = np.stack([np.asarray(r["out"], np.float32) for r in res.results], axis=0)
    return out, res


def kernel(**inputs):
    x = np.asarray(inputs["x"])
    B, S, _ = x.shape
    NL = int(np.asarray(inputs["w_in"]).shape[0])
    out, _ = run(inputs, S, NL, B)
    return out.astype(np.float32)
```

```python
import numpy as np
from contextlib import ExitStack
import concourse.bass as bass
import concourse.mybir as mybir
from concourse.bass_utils import run_bass_kernel_spmd

F32 = mybir.dt.float32
BF16 = mybir.dt.bfloat16
ALU = mybir.AluOpType
AF = mybir.ActivationFunctionType

D = 1024
EPS = 1e-6
NVEC = 72
DFF = 2816
NFC = 22
MEM = 256


class Tok:
    __slots__ = ("s", "v", "clk")

    def __init__(self, s, v, clk):
        self.s = s
        self.v = v
        self.clk = clk


class Buf:
    __slots__ = ("w", "rd", "excl")

    def __init__(self, excl=False):
        self.w = None
        self.rd = {}
        self.excl = excl


class RR:
    def __init__(self, items):
        self.items = list(items)
        self.i = 0

    def next(self):
        r = self.items[self.i % len(self.items)]
        self.i += 1
        return r


class Sched:
    COMPUTE = ("pe", "act", "dve", "pool")

    def __init__(self, nc, es, nslots=8):
        self.nc = nc
        self.h = {"pe": nc.tensor, "act": nc.scalar, "dve": nc.vector, "pool": nc.gpsimd, "sp": nc.sync}
        self.sems = []
        self.own = {}
        for e in self.COMPUTE:
            self.own[e] = len(self.sems)
            self.sems.append(es.enter_context(nc.semaphore("sem_" + e)))
        self.slots = {}
        for q in ("sp", "pool"):
            self.slots[q] = []
            for i in range(nslots):
                self.slots[q].append(len(self.sems))
                self.sems.append(es.enter_context(nc.semaphore("dma_%s_%d" % (q, i))))
        self.n = len(self.sems)
        self.cnt = [0] * self.n
        self.clk = {e: [0] * self.n for e in self.h}
        self.dman = {"sp": 0, "pool": 0}
        self.nops = 0

    def op(self, eng, fn, reads=(), writes=(), dma=False):
        clk = self.clk[eng]
        own = self.own.get(eng)
        deps = []
        wr = list(writes)
        for b in reads:
            if b.excl:
                wr.append(b)
            elif b.w is not None:
                deps.append((b.w, 0))
        for b in wr:
            if b.w is not None:
                deps.append((b.w, 1))
            for t in b.rd.values():
                deps.append((t, 2))
        waits = {}
        for t, kind in deps:
            if (not dma) and t.s == own:
                if kind != 0 or eng == "pe":
                    continue
            if clk[t.s] >= t.v:
                continue
            waits[t.s] = max(waits.get(t.s, 0), t.v)
            tc = t.clk
            for i in range(self.n):
                if tc[i] > clk[i]:
                    clk[i] = tc[i]
            if clk[t.s] < t.v:
                clk[t.s] = t.v
        if dma:
            k = self.dman[eng]
            self.dman[eng] += 1
            sl = self.slots[eng][k % len(self.slots[eng])]
            prev = self.cnt[sl]
            if clk[sl] < prev:
                waits[sl] = max(waits.get(sl, 0), prev)
                clk[sl] = prev
            self.cnt[sl] += 16
            tok = Tok(sl, self.cnt[sl], list(clk))
            inc = (sl, 16)
        else:
            self.cnt[own] += 1
            snap = list(clk)
            snap[own] = self.cnt[own]
            tok = Tok(own, self.cnt[own], snap)
            inc = (own, 1)
        for b in reads:
            if not b.excl:
                b.rd[tok.s] = tok
        for b in wr:
            b.w = tok
            b.rd = {}
        e = self.h[eng]
        wl = sorted(waits.items())
        attach = None
        if (not dma) and wl:
            attach = wl.pop()
        for s, v in wl:
            e.wait_ge(self.sems[s], v)
        r = fn(e)
        first, last = r if isinstance(r, tuple) else (r, r)
        if attach is not None:
            first._wait_ge(self.sems[attach[0]], attach[1])
        last.then_inc(self.sems[inc[0]], inc[1])
        self.nops += 1
        return tok

    def barrier(self):
        for eng, e in self.h.items():
            clk = self.clk[eng]
            for s in range(self.n):
                if s == self.own.get(eng):
                    continue
                if clk[s] < self.cnt[s]:
                    e.wait_ge(self.sems[s], self.cnt[s])
                    clk[s] = self.cnt[s]
        for eng in self.h:
            own = self.own.get(eng)
            for s in range(self.n):
                if s != own:
                    self.clk[eng][s] = self.cnt[s]

    def finish(self):
        e = self.h["sp"]
        clk = self.clk["sp"]
        for s in range(self.n):
            if clk[s] < self.cnt[s]:
                e.wait_ge(self.sems[s], self.cnt[s])
                clk[s] = self.cnt[s]


def build(S, NL):
    NB = S // 512
    NT = S // 128
    nc = bass.Bass("TRN2", target_bir_lowering=False)

    def din(name, shape, dt=F32):
        return nc.dram_tensor(name, list(shape), dt, kind="ExternalInput").ap()

    def dscr(name, shape, dt):
        return nc.dram_tensor(name, list(shape), dt, kind="Internal").ap()

    x_d = din("x", [S, D])
    mem_d = din("mem", [MEM, D])
    w_in_d = din("w_in", [NL, D, 2056])
    w_pool_d = din("w_pool", [NL, 4, 128, 128])
    w_out_d = din("w_out", [NL, D, D])
    w_mq_d = din("w_mem_q", [NL, D, 512])
    w_mkv_d = din("w_mem_kv", [NL, D, 1024])
    w_mo_d = din("w_mem_out", [NL, 512, D])
    w_gu_d = din("w_gate_up", [NL, D, 2 * DFF])
    w_dn_d = din("w_down", [NL, DFF, D])
    vecs_d = din("vecs", [NL, 128, NVEC])
    consts_d = din("consts", [128, 896])
    out_d = nc.dram_tensor("out", [S, D], F32, kind="ExternalOutput").ap()

    s_in = dscr("s_in", [NL, 128, 8, 2056], BF16)
    s_pool = dscr("s_pool", [NL, 128, 4, 128], BF16)
    s_out = dscr("s_out", [NL, 128, 8, 1024], BF16)
    s_mq = dscr("s_mq", [NL, 128, 8, 512], BF16)
    s_mkv = dscr("s_mkv", [NL, 128, 8, 1024], BF16)
    s_mo = dscr("s_mo", [NL, 128, 4, 1024], BF16)
    s_gu = dscr("s_gu", [NL, 128, 8, 2 * DFF], BF16)
    s_dn = dscr("s_dn", [NL, 128, NFC, 1024], BF16)
    hscr = dscr("hscr", [8, 128, S], F32)
    kscr = dscr("kscr", [4, 128, S], BF16)
    vscr = dscr("vscr", [4, 128, NT, 192], BF16)
    hscr_v = hscr.rearrange("k p s -> p k s")
    kscr_v = kscr.rearrange("c p s -> p c s")

    with ExitStack() as es:
        sc = Sched(nc, es)
        op = sc.op

        uniq = [0]

        def sb(name, shape, dt, stack=es):
            uniq[0] += 1
            return stack.enter_context(nc.sbuf_tensor("%s_%d" % (name, uniq[0]), list(shape), dt))

        ps = [es.enter_context(nc.psum_tensor("ps%d" % i, [128, 512], F32)) for i in range(8)]
        pb = [Buf(excl=True) for _ in range(8)]

        cst = sb("cst", [128, 896], F32)
        cstb = Buf()
        ident = cst[:, 0:128]
        triu_f = cst[:, 128:256]
        ones_f = cst[:, 384:512]
        invcnt = cst[:, 512:576]
        cbf = sb("cbf", [128, 384], BF16)
        cbfb = Buf()
        triu_b = cbf[:, 0:128]
        bd64_b = cbf[:, 128:256]
        ones_b = cbf[:, 256:384]
        epsT = sb("epsT", [128, 2], F32)
        epsb = Buf()
        op("sp", lambda e: e.dma_start(out=cst[:], in_=consts_d[:, :]), writes=[cstb], dma=True)
        op("dve", lambda e: e.tensor_copy(out=cbf[:], in_=cst[:, 128:512]), reads=[cstb], writes=[cbfb])
        cbf2 = sb("cbf2", [128, 256], BF16)
        mneg_b = cbf2[:, 0:128]
        ident_b = cbf2[:, 128:256]
        op("dve", lambda e: e.tensor_copy(out=cbf2[:], in_=cst[:, 640:896]), reads=[cstb], writes=[cbfb])
        op("dve", lambda e: e.memset(epsT[:, 0:1], EPS), writes=[epsb])
        op("dve", lambda e: e.memset(epsT[:, 1:2], 1.0), writes=[epsb])

        with ExitStack() as ps_es:
            CW = 2816
            NST = 3
            stf = [sb("stf%d" % i, [128, CW], F32, ps_es) for i in range(NST)]
            stb = [sb("stb%d" % i, [128, CW], BF16, ps_es) for i in range(NST)]
            stfb = [Buf() for _ in range(NST)]
            stbb = [Buf() for _ in range(NST)]
            cnt = [0]
            cast_eng = ["dve", "act", "pool"]

            def prep(src, dst, npart, ncols):
                for c0 in range(0, ncols, CW):
                    c1 = min(ncols, c0 + CW)
                    w = c1 - c0
                    i = cnt[0] % NST
                    ce = cast_eng[cnt[0] % 3]
                    cnt[0] += 1
                    op("sp", lambda e: e.dma_start(out=stf[i][0:npart, 0:w], in_=src[:, c0:c1]),
                       writes=[stfb[i]], dma=True)
                    if ce == "act":
                        op("act", lambda e: e.activation(out=stb[i][0:npart, 0:w], in_=stf[i][0:npart, 0:w], func=AF.Copy),
                           reads=[stfb[i]], writes=[stbb[i]])
                    else:
                        op(ce, lambda e: e.tensor_copy(out=stb[i][0:npart, 0:w], in_=stf[i][0:npart, 0:w]),
                           reads=[stfb[i]], writes=[stbb[i]])
                    op("pool", lambda e: e.dma_start(out=dst[:, c0:c1], in_=stb[i][0:npart, 0:w]),
                       reads=[stbb[i]], dma=True)

            for l in range(NL):
                for kc in range(8):
                    prep(w_in_d[l, kc * 128:(kc + 1) * 128, :], s_in[l, :, kc, :], 128, 2056)
                for g in range(4):
                    prep(w_pool_d[l, g, :, :], s_pool[l, :, g, :], 128, 128)
                for kc in range(8):
                    prep(w_out_d[l, kc * 128:(kc + 1) * 128, :], s_out[l, :, kc, :], 128, 1024)
                for kc in range(8):
                    prep(w_mq_d[l, kc * 128:(kc + 1) * 128, :], s_mq[l, :, kc, :], 128, 512)
                    prep(w_mkv_d[l, kc * 128:(kc + 1) * 128, :], s_mkv[l, :, kc, :], 128, 1024)
                for hm in range(4):
                    prep(w_mo_d[l, hm * 128:(hm + 1) * 128, :], s_mo[l, :, hm, :], 128, 1024)
                for kc in range(8):
                    prep(w_gu_d[l, kc * 128:(kc + 1) * 128, :], s_gu[l, :, kc, :], 128, 2 * DFF)
                for fc in range(NFC):
                    prep(w_dn_d[l, fc * 128:(fc + 1) * 128, :], s_dn[l, :, fc, :], 128, 1024)
            sc.barrier()

        hsb = [Buf() for _ in range(NB)]

        def fm_norm(hT, hTb, hb, hbb, vec, vecb, gcol, n, A0, A0b, A1, A1b, bank, part=None):
            if part in (None, "A"):
                for kc in range(8):
                    op("act", lambda e: e.activation(out=hb[:, kc, 0:n], in_=hT[:, kc, 0:n], func=AF.Square),
                       reads=[hTb[kc]], writes=[hbb[kc]])
            if part == "A":
                return

            def mm(e):
                f = None
                for kc in range(8):
                    i = e.matmul(ps[bank][:, 0:n], ones_b, hb[:, kc, 0:n], start=(kc == 0), stop=(kc == 7))
                    f = i if f is None else f
                return f, i
            op("pe", mm, reads=hbb + [cbfb], writes=[pb[bank]])
            op("act", lambda e: e.activation(out=A0[:, 0:n], in_=ps[bank][:, 0:n], func=AF.Sqrt,
                                             bias=epsT[:, 0:1], scale=1.0 / D),
               reads=[epsb, pb[bank]], writes=[A0b])
            op("dve", lambda e: e.reciprocal(out=A1[:, 0:n], in_=A0[:, 0:n]), reads=[A0b], writes=[A1b])
            for kc in range(8):
                op("dve", lambda e: e.scalar_tensor_tensor(out=hb[:, kc, 0:n], in0=hT[:, kc, 0:n],
                                                           scalar=vec[:, gcol + kc:gcol + kc + 1], in1=A1[:, 0:n],
                                                           op0=ALU.mult, op1=ALU.mult),
                   reads=[hTb[kc], A1b, vecb], writes=[hbb[kc]])

        def head_norm(bank, bank2, stat_lhsT, inv_n, n, vec, gcol, vecb, dsts, ysq, ysqb, rA, rAb, rB, rBb):
            op("act", lambda e: e.activation(out=ysq[:, 0:n], in_=ps[bank][:, 0:n], func=AF.Square),
               reads=[pb[bank]], writes=[ysqb])
            op("pe", lambda e: e.matmul(ps[bank2][:, 0:n], stat_lhsT, ysq[:, 0:n], start=True, stop=True),
               reads=[ysqb, cbfb], writes=[pb[bank2]])
            op("act", lambda e: e.activation(out=rA[:, 0:n], in_=ps[bank2][:, 0:n], func=AF.Sqrt,
                                             bias=epsT[:, 0:1], scale=inv_n),
               reads=[epsb, pb[bank2]], writes=[rAb])
            op("dve", lambda e: e.reciprocal(out=rB[:, 0:n], in_=rA[:, 0:n]), reads=[rAb], writes=[rBb])
            for (r0_, r1_, dst, dstb) in dsts:
                op("dve", lambda e: e.scalar_tensor_tensor(out=dst, in0=ps[bank][r0_:r1_, 0:n], scalar=vec[r0_:r1_, gcol:gcol + 1],
                                                           in1=rB[r0_:r1_, 0:n], op0=ALU.mult, op1=ALU.mult),
                   reads=[rBb, pb[bank], vecb], writes=[dstb])

        def head_norm_multi(items, n, vec, vecb):
            for (bank, bank2, lhs, inv_n, gcol, dsts, ysq, ysqb, r, rb) in items:
                op("act", lambda e: e.activation(out=ysq[:, 0:n], in_=ps[bank][:, 0:n], func=AF.Square),
                   reads=[pb[bank]], writes=[ysqb])
            for (bank, bank2, lhs, inv_n, gcol, dsts, ysq, ysqb, r, rb) in items:
                op("pe", lambda e: e.matmul(ps[bank2][:, 0:n], lhs, ysq[:, 0:n], start=True, stop=True),
                   reads=[ysqb, cbfb], writes=[pb[bank2]])
            for (bank, bank2, lhs, inv_n, gcol, dsts, ysq, ysqb, r, rb) in items:
                op("act", lambda e: e.activation(out=r[:, 0:n], in_=ps[bank2][:, 0:n], func=AF.Sqrt,
                                                 bias=epsT[:, 0:1], scale=inv_n),
                   reads=[epsb, pb[bank2]], writes=[rb])
            for (bank, bank2, lhs, inv_n, gcol, dsts, ysq, ysqb, r, rb) in items:
                op("dve", lambda e: e.reciprocal(out=r[:, 0:n], in_=r[:, 0:n]), reads=[rb], writes=[rb])
            for (bank, bank2, lhs, inv_n, gcol, dsts, ysq, ysqb, r, rb) in items:
                for (r0_, r1_, dst, dstb) in dsts:
                    op("dve", lambda e: e.scalar_tensor_tensor(out=dst, in0=ps[bank][r0_:r1_, 0:n], scalar=vec[r0_:r1_, gcol:gcol + 1],
                                                               in1=r[r0_:r1_, 0:n], op0=ALU.mult, op1=ALU.mult),
                       reads=[rb, pb[bank], vecb], writes=[dstb])

        def pipeline(stage_lists, delays, filler=None):
            n = len(stage_lists)
            for t in range(n + max(delays)):
                for k, d in enumerate(delays):
                    i = t - d
                    if 0 <= i < n:
                        stage_lists[i][k]()
                if filler is not None:
                    filler()

        def norm_stages(bank, bank2, lhs, inv_n, n, vec, vecb, gcol, dsts, ysq, ysqb, r, rb):
            def sq():
                op("act", lambda e: e.activation(out=ysq[:, 0:n], in_=ps[bank][:, 0:n], func=AF.Square),
                   reads=[pb[bank]], writes=[ysqb])

            def st():
                op("pe", lambda e: e.matmul(ps[bank2][:, 0:n], lhs, ysq[:, 0:n], start=True, stop=True),
                   reads=[ysqb, cbfb], writes=[pb[bank2]])

            def sr():
                op("act", lambda e: e.activation(out=r[:, 0:n], in_=ps[bank2][:, 0:n], func=AF.Sqrt,
                                                 bias=epsT[:, 0:1], scale=inv_n),
                   reads=[epsb, pb[bank2]], writes=[rb])

            def rc():
                op("dve", lambda e: e.reciprocal(out=r[:, 0:n], in_=r[:, 0:n]), reads=[rb], writes=[rb])

            def stt():
                for (r0_, r1_, dst, dstb) in dsts:
                    op("dve", lambda e: e.scalar_tensor_tensor(out=dst, in0=ps[bank][r0_:r1_, 0:n], scalar=vec[r0_:r1_, gcol:gcol + 1],
                                                               in1=r[r0_:r1_, 0:n], op0=ALU.mult, op1=ALU.mult),
                       reads=[rb, pb[bank], vecb], writes=[dstb])
            return [sq, st, sr, rc, stt]

        for l in range(NL):
            first_layer = (l == 0)
            last_layer = (l == NL - 1)
            with ExitStack() as mx:
                vec = sb("vec", [128, NVEC], F32, mx)
                vecb = Buf()
                op("sp", lambda e: e.dma_start(out=vec[:], in_=vecs_d[l, :, :]), writes=[vecb], dma=True)
                wf = sb("wf", [128, 8, 8], BF16, mx)
                wfb = Buf()
                op("sp", lambda e: e.dma_start(out=wf[:], in_=s_in[l, :, :, 2048:2056]), writes=[wfb], dma=True)
                wpl = sb("wpl", [128, 4, 128], BF16, mx)
                wplb = Buf()
                op("sp", lambda e: e.dma_start(out=wpl[:], in_=s_pool[l, :, :, :]), writes=[wplb], dma=True)

                hT2 = [sb("hT%d" % i, [128, 8, 512], F32, mx) for i in range(2)]
                hT2b = [[Buf() for _ in range(8)] for _ in range(2)]
                hb1 = sb("hb1", [128, 8, 512], BF16, mx)
                hb1b = [Buf() for _ in range(8)]
                hb2 = sb("hb2", [128, 8, 512], BF16, mx)
                hb2b = [Buf() for _ in range(8)]
                pa = [sb("pa%d" % i, [128, 512], F32, mx) for i in range(2)]
                pab = [Buf() for _ in range(2)]
                bank_pro = RR([6, 7])
                NF = 6
                ft = [sb("ft%d" % i, [128, 512], F32, mx) for i in range(NF)]
                ftb = [Buf() for _ in range(NF)]
                ftr = RR(range(NF))
                qT = sb("qT", [128, 8, 512], BF16, mx)
                qTb = [Buf() for _ in range(8)]
                kcur = sb("kcur", [128, 4, 512], BF16, mx)
                kcb = [Buf() for _ in range(4)]
                vcur = sb("vcur", [128, 4, 4, 192], BF16, mx)
                vcb = [Buf() for _ in range(4)]
                NR = 4
                ring = [sb("ring%d" % i, [128, 4096], BF16, mx) for i in range(NR)]
                ringb = [Buf() for _ in range(NR)]
                NPT = 4
                PT = [sb("PT%d" % i, [128, 512], BF16, mx) for i in range(NPT)]
                PTb = [Buf() for _ in range(NPT)]
                ptr = RR(range(NPT))
                biasb = [sb("biasb%d" % i, [128, NT], F32, mx) for i in range(8)]
                biasbb = [Buf() for _ in range(8)]
                foxT = sb("foxT", [128, 4, 512], BF16, mx)
                foxb = [Buf() for _ in range(8)]
                pin = sb("pin", [128, 4, 528], F32, mx)
                pinb = [Buf() for _ in range(4)]
                ptmp = [sb("ptmp%d" % i, [128, 528], F32, mx) for i in range(2)]
                ptmpb = [Buf() for _ in range(2)]
                mixed = sb("mixed", [128, 4, 512], BF16, mx)
                mixb = [Buf() for _ in range(4)]
                poolT = sb("poolT", [128, 4, 512], BF16, mx)
                poolb = [Buf() for _ in range(4)]
                ysq = [sb("ysq%d" % i, [128, 512], BF16, mx) for i in range(4)]
                ysqb = [Buf() for _ in range(4)]
                mqT = [sb("mqT%d" % i, [128, 512], BF16, mx) for i in range(4)]
                mqTb = [Buf() for _ in range(4)]
                xin = [sb("xin%d" % i, [128, 512], F32, mx) for i in range(3)]
                xinb = [Buf() for _ in range(3)]
                xr = RR(range(3))
                negc = sb("negc", [128, NT, 8], F32, mx)
                negcb = Buf()
                cend = sb("cend", [128, NT + 1, 8], F32, mx)
                cendb = Buf()
                fl = sb("fl", [128, 3, 32], F32, mx)
                flb = [Buf() for _ in range(3)]
                mkT = sb("mkT", [128, 4, MEM], BF16, mx)
                mkTb = Buf()
                mv = sb("mv", [128, 2, 512], BF16, mx)
                mvb = Buf()
                ksb = [Buf() for _ in range(NB)]
                vsb = [Buf() for _ in range(NB)]

                bank_proj = RR([0, 1])
                bank_stat = RR([7, 4])
                bank_S = RR([2, 3, 4])
                bank_O = RR([5, 6])
                bank_O2 = RR([(5, 6), (7, 0)])
                BANK_RB = 1

                op("pool", lambda e: e.memset(qT[:], 0.0), writes=qTb)
                op("pool", lambda e: e.memset(vcur[:, :, :, 64:128], 1.0), writes=vcb)
                op("dve", lambda e: e.memset(cend[:, 0, :], 0.0), writes=[cendb])
                op("pool", lambda e: e.memset(pin[:, :, 0:16], 0.0), writes=pinb)

                chunks = []
                for b in range(NB):
                    chunks.append(("inQ", s_in[l, :, :, 0:512], 128, (8, 512)))
                    chunks.append(("inK", s_in[l, :, :, 512:1024], 128, (8, 512)))
                    chunks.append(("inV", s_in[l, :, :, 1024:1536], 128, (8, 512)))
                    chunks.append(("inP", s_in[l, :, :, 1536:2048], 128, (8, 512)))
                    for hf in range(2):
                        chunks.append(("out", s_out[l, :, :, hf * 512:(hf + 1) * 512], 128, (8, 512)))
                    chunks.append(("mq", s_mq[l, :, :, :], 128, (8, 512)))
                    chunks.append(("mo", s_mo[l, :, :, :], 128, (4, 1024)))
                wstate = {"issued": 0, "cur": 0}

                def wview(k):
                    name, src, npart, (a, bcols) = chunks[k]
                    return ring[k % NR][0:npart, 0:a * bcols].rearrange("p (a b) -> p a b", a=a)

                def wissue(upto):
                    while wstate["issued"] < min(upto, len(chunks)):
                        k = wstate["issued"]
                        src = chunks[k][1]
                        dst = wview(k)
                        op("sp", lambda e: e.dma_start(out=dst, in_=src), writes=[ringb[k % NR]], dma=True)
                        wstate["issued"] += 1

                def wnext(name):
                    k = wstate["cur"]
                    assert chunks[k][0] == name, (chunks[k][0], name)
                    wissue(k + 1)
                    wstate["cur"] += 1
                    return wview(k), ringb[k % NR], k

                def wprefetch():
                    wissue(wstate["cur"] + NR)

                with ExitStack() as mk:
                    memT = sb("memT", [128, 8, MEM], F32, mk)
                    memTb = [Buf() for _ in range(8)]
                    mnb = sb("mnb", [128, 8, MEM], BF16, mk)
                    mnbb = [Buf() for _ in range(8)]
                    wkv = sb("wkv", [128, 8, 1024], BF16, mk)
                    wkvb = Buf()
                    op("sp", lambda e: e.dma_start(out=wkv[:], in_=s_mkv[l, :, :, :]), writes=[wkvb], dma=True)
                    for i in range(2):
                        for hf in range(2):
                            xi = xr.next()
                            op("pool", lambda e: e.dma_start(out=xin[xi][:], in_=mem_d[i * 128:(i + 1) * 128, hf * 512:(hf + 1) * 512]),
                               writes=[xinb[xi]], dma=True)
                            bk = bank_proj.next()

                            def tr(e):
                                f = None
                                for k in range(4):
                                    ins = e.transpose(ps[bk][:, k * 128:(k + 1) * 128], xin[xi][:, k * 128:(k + 1) * 128], ident)
                                    f = ins if f is None else f
                                return f, ins
                            op("pe", tr, reads=[xinb[xi], cstb], writes=[pb[bk]])
                            op("dve", lambda e: e.tensor_copy(out=memT[:, hf * 4:(hf + 1) * 4, i * 128:(i + 1) * 128],
                                                              in_=ps[bk][:, 0:512].rearrange("p (k t) -> p k t", k=4)),
                               reads=[pb[bk]], writes=memTb[hf * 4:(hf + 1) * 4])
                    a0, a1 = ftr.next(), ftr.next()
                    fm_norm(memT, memTb, mnb, mnbb, vec, vecb, 24, MEM, ft[a0], ftb[a0], ft[a1], ftb[a1], bank_stat.next())
                    for hm in range(4):
                        bk = bank_proj.next()

                        def mmk(e):
                            f = None
                            for kc in range(8):
                                ins = e.matmul(ps[bk][:, 0:MEM], wkv[:, kc, hm * 128:(hm + 1) * 128], mnb[:, kc, :],
                                               start=(kc == 0), stop=(kc == 7))
                                f = ins if f is None else f
                            return f, ins
                        op("pe", mmk, reads=mnbb + [wkvb], writes=[pb[bk]])
                        a0, a1 = ftr.next(), ftr.next()
                        head_norm(bk, bank_stat.next(), ones_b, 1.0 / 128, MEM, vec, 35, vecb, [(0, 128, mkT[:, hm, :], mkTb)],
                                  ysq[hm % 2], ysqb[hm % 2], ft[a0], ftb[a0], ft[a1], ftb[a1])
                    for mt in range(2):
                        bk = bank_proj.next()

                        def mmv(e):
                            f = None
                            for kc in range(8):
                                ins = e.matmul(ps[bk][:, :], mnb[:, kc, mt * 128:(mt + 1) * 128], wkv[:, kc, 512:1024],
                                               start=(kc == 0), stop=(kc == 7))
                                f = ins if f is None else f
                            return f, ins
                        op("pe", mmv, reads=mnbb + [wkvb], writes=[pb[bk]])
                        op("dve", lambda e: e.tensor_copy(out=mv[:, mt, :], in_=ps[bk][:, :]), reads=[pb[bk]], writes=[mvb])
                    sc.barrier()

                CK = 8
                NCH = 4
                Kc = [sb("Kc%d" % i, [128, CK * 128], BF16, mx) for i in range(NCH)]
                Vc = [sb("Vc%d" % i, [128, CK, 192], BF16, mx) for i in range(NCH)]
                kvb = [Buf() for _ in range(NCH)]
                fz = [sb("fz%d" % i, [128, 512], F32, mx) for i in range(4)]
                fzb = [Buf() for _ in range(4)]
                fzr = RR(range(4))
                kvchunks = []
                kvbase = {}
                for b_ in range(1, NB):
                    for c_ in range(4):
                        kvbase[(b_, c_)] = len(kvchunks)
                        for lo in range(0, 4 * b_, CK):
                            kvchunks.append((b_, c_, lo, min(lo + CK, 4 * b_)))
                kvstate = {"issued": 0, "released": 0}

                def kv_issue():
                    lim = min(len(kvchunks), kvstate["released"] + NCH)
                    while kvstate["issued"] < lim:
                        k = kvstate["issued"]
                        b_, c_, lo, hi = kvchunks[k]
                        if b_ > kvstate["maxb"]:
                            break
                        sl = k % NCH
                        op("sp", lambda e: e.dma_start(out=Kc[sl][:, 0:(hi - lo) * 128], in_=kscr[c_, :, lo * 128:hi * 128]),
                           reads=ksb[0:b_], writes=[kvb[sl]], dma=True)
                        op("sp", lambda e: e.dma_start(out=Vc[sl][:, 0:hi - lo, :], in_=vscr[c_, :, lo:hi, :]),
                           reads=vsb[0:b_], writes=[kvb[sl]], dma=True)
                        kvstate["issued"] += 1
                kvstate["maxb"] = 0

                def prologue(bn):
                    hTn, hTnb = hT2[bn % 2], hT2b[bn % 2]
                    tn = bn * 512
                    if first_layer:
                        for i in range(4):
                            for hf in range(2):
                                xi = xr.next()
                                op("pool", lambda e: e.dma_start(out=xin[xi][:], in_=x_d[tn + i * 128:tn + (i + 1) * 128, hf * 512:(hf + 1) * 512]),
                                   writes=[xinb[xi]], dma=True)
                                bk = bank_pro.next()

                                def tr(e):
                                    f = None
                                    for k in range(4):
                                        ins = e.transpose(ps[bk][:, k * 128:(k + 1) * 128], xin[xi][:, k * 128:(k + 1) * 128], ident)
                                        f = ins if f is None else f
                                    return f, ins
                                op("pe", tr, reads=[xinb[xi], cstb], writes=[pb[bk]])
                                op("dve", lambda e: e.tensor_copy(out=hTn[:, hf * 4:(hf + 1) * 4, i * 128:(i + 1) * 128],
                                                                  in_=ps[bk][:, 0:512].rearrange("p (k t) -> p k t", k=4)),
                                   reads=[pb[bk]], writes=hTnb[hf * 4:(hf + 1) * 4])
                            yield
                    else:
                        op("pool", lambda e: e.dma_start(out=hTn[:], in_=hscr_v[:, :, tn:tn + 512]),
                           reads=[hsb[bn]], writes=hTnb, dma=True)
                        yield
                    bkn = bank_pro.next()
                    fm_norm(hTn, hTnb, hb1, hb1b, vec, vecb, 0, 512, pa[0], pab[0], pa[1], pab[1], bkn, part="A")
                    yield
                    fm_norm(hTn, hTnb, hb1, hb1b, vec, vecb, 0, 512, pa[0], pab[0], pa[1], pab[1], bkn, part="B")
                    yield

                for _ in prologue(0):
                    pass
                for b in range(NB):
                    t0 = b * 512
                    wprefetch()
                    hT, hTb = hT2[b % 2], hT2b[b % 2]
                    hb, hbb = hb1, hb1b

                    wq_, wqb_, _ = wnext("inQ")
                    wk_, wkb_, _ = wnext("inK")
                    ysl = [(ysq[i], ysqb[i]) for i in range(4)]
                    stage_lists = []
                    for it in range(8):
                        which, c = ("inQ", it) if it < 4 else ("inK", it - 4)
                        wv_, wb_ = (wq_, wqb_) if it < 4 else (wk_, wkb_)
                        bk = it % 4
                        if which == "inQ":
                            dsts, gcol = [(0, 64, qT[0:64, 2 * c, :], qTb[2 * c]), (64, 128, qT[64:128, 2 * c + 1, :], qTb[2 * c + 1])], 32
                        else:
                            dsts, gcol = [(0, 128, kcur[:, c, :], kcb[c])], 33

                        def proj(bk=bk, wv_=wv_, wb_=wb_, c=c):
                            def mmq(e):
                                f = None
                                for kc in range(8):
                                    ins = e.matmul(ps[bk][:, :], wv_[:, kc, c * 128:(c + 1) * 128], hb[:, kc, :],
                                                   start=(kc == 0), stop=(kc == 7))
                                    f = ins if f is None else f
                                return f, ins
                            op("pe", mmq, reads=hbb + [wb_], writes=[pb[bk]])
                        ys, ysb_ = ysl[it % 4]
                        stage_lists.append([proj] + norm_stages(bk, 4 + (it % 2), bd64_b, 1.0 / 64, 512, vec, vecb, gcol, dsts,
                                                                ys, ysb_, ft[it % 4], ftb[it % 4]))
                    pipeline(stage_lists, [0, 1, 1, 2, 2, 3])
                    wprefetch()
                    if b < NB - 1:
                        op("pool", lambda e: e.dma_start(out=kscr_v[:, :, t0:t0 + 512], in_=kcur[:]),
                           reads=kcb, writes=[ksb[b]], dma=True)

                    wv_, wb_, _ = wnext("inV")
                    BF = 4
                    for i in range(4):
                        bk = i

                        def mmv2(e):
                            f = None
                            for kc in range(8):
                                ins = e.matmul(ps[bk][:, :], hb[:, kc, i * 128:(i + 1) * 128], wv_[:, kc, :],
                                               start=(kc == 0), stop=(kc == 7))
                                f = ins if f is None else f
                            return f, ins
                        op("pe", mmv2, reads=hbb + [wb_], writes=[pb[bk]])

                    def mmf(e):
                        f = None
                        for i in range(4):
                            for kc in range(8):
                                ins = e.matmul(ps[BF][:, i * 8:(i + 1) * 8], hb[:, kc, i * 128:(i + 1) * 128], wf[:, kc, :],
                                               start=(kc == 0), stop=(kc == 7))
                                f = ins if f is None else f
                        return f, ins
                    op("pe", mmf, reads=hbb + [wfb], writes=[pb[BF]])
                    for i in range(4):
                        pv4 = ps[i][:, 0:512].rearrange("p (c two d) -> p c two d", c=4, two=2)
                        op("dve", lambda e: e.tensor_copy(out=vcur[:, i, :, 0:64], in_=pv4[:, :, 0, :]),
                           reads=[pb[i]], writes=[vcb[i]])
                        op("dve", lambda e: e.tensor_copy(out=vcur[:, i, :, 128:192], in_=pv4[:, :, 1, :]),
                           reads=[pb[i]], writes=[vcb[i]])
                    op("dve", lambda e: e.tensor_tensor(out=fl[:, 0, :], in0=ps[BF][:, 0:32], in1=vec[:, 40:72], op=ALU.add),
                       reads=[pb[BF], vecb], writes=[flb[0]])
                    op("act", lambda e: e.activation(out=fl[:, 1, :], in_=fl[:, 0, :], func=AF.Exp, scale=-1.0),
                       reads=[flb[0]], writes=[flb[1]])
                    op("act", lambda e: e.activation(out=fl[:, 2, :], in_=fl[:, 1, :], func=AF.Ln, bias=epsT[:, 1:2], scale=1.0),
                       reads=[flb[1], epsb], writes=[flb[2]])
                    BC = 5

                    def mmc(e):
                        f = None
                        for i in range(4):
                            ins = e.matmul(ps[BC][:, i * 8:(i + 1) * 8], ident, cend[:, 4 * b, :], start=True, stop=False)
                            f = ins if f is None else f
                            for i2 in range(i):
                                e.matmul(ps[BC][:, i * 8:(i + 1) * 8], ones_f, fl[:, 2, i2 * 8:(i2 + 1) * 8], start=False, stop=False)
                            e.matmul(ps[BC][:, i * 8:(i + 1) * 8], triu_f, fl[:, 2, i * 8:(i + 1) * 8], start=False, stop=True)
                            e.matmul(ps[BC][:, 32 + i * 8:32 + (i + 1) * 8], ident, cend[:, 4 * b, :], start=True, stop=False)
                            for i2 in range(i + 1):
                                ins = e.matmul(ps[BC][:, 32 + i * 8:32 + (i + 1) * 8], ones_f, fl[:, 2, i2 * 8:(i2 + 1) * 8],
                                               start=False, stop=(i2 == i))
                        return f, ins
                    op("pe", mmc, reads=[flb[2], cstb, cendb], writes=[pb[BC]])
                    op("dve", lambda e: e.tensor_copy(out=negc[:, 4 * b:4 * b + 4, :], in_=ps[BC][:, 0:32].rearrange("p (t h) -> p t h", t=4)),
                       reads=[pb[BC]], writes=[negcb])
                    op("dve", lambda e: e.tensor_copy(out=cend[:, 4 * b + 1:4 * b + 5, :], in_=ps[BC][:, 32:64].rearrange("p (t h) -> p t h", t=4)),
                       reads=[pb[BC]], writes=[cendb])
                    for h in range(8):
                        op("dve", lambda e: e.tensor_scalar(out=biasb[h][:, 0:4 * b + 4], in0=negc[:, 0:4 * b + 4, h],
                                                            scalar1=cend[:, 4 * b + 2, h:h + 1], scalar2=None, op0=ALU.subtract),
                           reads=[negcb, cendb], writes=[biasbb[h]])
                    if b < NB - 1:
                        for c in range(4):
                            op("pool", lambda e: e.dma_start(out=vscr[c, :, 4 * b:4 * b + 4, :],
                                                             in_=vcur[:, :, c, :]),
                               reads=vcb, writes=[vsb[b]], dma=True)
                    wprefetch()

                    wv_, wb_, _ = wnext("inP")
                    for g in range(4):
                        bk = bank_proj.next()

                        def mmp(e):
                            f = None
                            for kc in range(8):
                                ins = e.matmul(ps[bk][:, :], wv_[:, kc, g * 128:(g + 1) * 128], hb[:, kc, :],
                                               start=(kc == 0), stop=(kc == 7))
                                f = ins if f is None else f
                            return f, ins
                        op("pe", mmp, reads=hbb + [wb_], writes=[pb[bk]])
                        op("act", lambda e: e.activation(out=pin[:, g, 16:528], in_=ps[bk][:, :], func=AF.Copy),
                           reads=[pb[bk]], writes=[pinb[g]])
                    wprefetch()

                    for g in range(4):
                        w = 2 << g
                        cur, curb = pin[:, g, :], pinb[g]
                        lo, off = 0, 1
                        for st in range(g + 1):
                            nx, nxb = ptmp[st % 2], ptmpb[st % 2]
                            lo2 = lo + off
                            op("pool", lambda e: e.tensor_tensor(out=nx[:, lo2:528], in0=cur[:, lo2:528], in1=cur[:, lo:528 - off], op=ALU.add),
                               reads=[curb], writes=[nxb])
                            cur, curb = nx[:, :], nxb
                            lo, off = lo2, off * 2
                        sc_t, sc_b = ptmp[(g + 1) % 2], ptmpb[(g + 1) % 2]
                        op("pool", lambda e: e.tensor_scalar(out=sc_t[:, 16:528], in0=cur[:, 16:528], scalar1=1.0 / w, scalar2=None, op0=ALU.mult),
                           reads=[curb], writes=[sc_b])
                        op("pool", lambda e: e.tensor_tensor(out=mixed[:, g, :], in0=sc_t[:, 16:528], in1=pin[:, g, 16:528], op=ALU.subtract),
                           reads=[sc_b, pinb[g]], writes=[mixb[g]])
                        if b == 0:
                            oth, othb = ptmp[(g + 1) % 2], ptmpb[(g + 1) % 2]
                            op("pool", lambda e: e.tensor_tensor(out=oth[:, 0:16], in0=cur[:, 16:32], in1=invcnt[:, g * 16:(g + 1) * 16], op=ALU.mult),
                               reads=[curb, cstb], writes=[othb])
                            op("pool", lambda e: e.tensor_tensor(out=mixed[:, g, 0:16], in0=oth[:, 0:16], in1=pin[:, g, 16:32], op=ALU.subtract),
                               reads=[othb, pinb[g]], writes=[mixb[g]])
                    op("pool", lambda e: e.tensor_copy(out=pin[:, :, 0:16], in_=pin[:, :, 512:528]), reads=[], writes=pinb)

                    nj = 4 * b + 4
                    items = [(2 * c + hh, j) for c in range(4) for j in range(nj) for hh in range(2)]
                    SKEW = 3
                    st_ = {}
                    obank = {}
                    pend = []
                    kvstate["maxb"] = b
                    kv_issue()

                    def kv_of(c, j):
                        q = j // CK
                        k = kvbase[(b, c)] + q
                        assert k < kvstate["issued"], (b, c, j, k, kvstate)
                        return k % NCH, j - q * CK, k

                    def emit_S(h, j):
                        c = h // 2
                        if j < 4 * b:
                            sl, jj, _ = kv_of(c, j)
                            lhsT = Kc[sl][:, jj * 128:(jj + 1) * 128]
                            kb = kvb[sl]
                            col0 = 0
                        else:
                            i = j - 4 * b
                            lhsT = kcur[:, c, i * 128:(i + 1) * 128]
                            kb = kcb[c]
                            col0 = i * 128
                        bs = bank_S.next()
                        if j >= 4 * b:
                            def mms(e):
                                i1 = e.matmul(ps[bs][:, col0:512], lhsT, qT[:, h, col0:512], start=True, stop=True)
                                i2 = e.matmul(ps[bs][:, col0:col0 + 128], ident_b, mneg_b, start=False, stop=True, skip_group_check=True)
                                return i1, i2
                            op("pe", mms, reads=[kb, qTb[h], cbfb], writes=[pb[bs]])
                        else:
                            op("pe", lambda e: e.matmul(ps[bs][:, col0:512], lhsT, qT[:, h, col0:512], start=True, stop=True),
                               reads=[kb, qTb[h]], writes=[pb[bs]])
                        pt = ptr.next()
                        op("act", lambda e: e.activation(out=PT[pt][:, col0:512], in_=ps[bs][:, col0:512], func=AF.Exp,
                                                         bias=biasb[h][:, j:j + 1], scale=0.125),
                           reads=[pb[bs], biasbb[h]], writes=[PTb[pt]])
                        st_[(h, j)] = (pt, col0)

                    def emit_PV(h, j):
                        c = h // 2
                        pt, col0 = st_.pop((h, j))
                        if j == 0 and h % 2 == 0:
                            bo2 = bank_O2.next()
                            obank[h], obank[h + 1] = bo2
                        bo = obank[h]
                        v0 = 0 if h % 2 == 0 else 64
                        if j < 4 * b:
                            sl, jj, k = kv_of(c, j)
                            lhsT = Vc[sl][:, jj, v0:v0 + 128]
                            vb_ = kvb[sl]
                        else:
                            lhsT = vcur[:, j - 4 * b, c, v0:v0 + 128]
                            vb_ = vcb[j - 4 * b]
                        op("pe", lambda e: e.matmul(ps[bo][:, col0:512], lhsT, PT[pt][:, col0:512], start=(j == 0), stop=(j == nj - 1)),
                           reads=[PTb[pt], vb_], writes=[pb[bo]])
                        if j < 4 * b and h % 2 == 1:
                            _, _, k = kv_of(c, j)
                            lo, hi = kvchunks[k][2], kvchunks[k][3]
                            if j == hi - 1:
                                kvstate["released"] = k + 1
                                kv_issue()
                        if j == nj - 1:
                            finalize(h, bo)

                    def finalize(h, bo):
                        c = h // 2
                        r1, r2 = fzr.next(), fzr.next()
                        if h % 2 == 0:
                            rr, o0 = 64, 0
                        else:
                            rr, o0 = 0, 64
                        op("dve", lambda e: e.reciprocal(out=fz[r1][rr:rr + 1, :], in_=ps[bo][rr:rr + 1, :]),
                           reads=[pb[bo]], writes=[fzb[r1]])

                        def fin_pe():
                            if h % 2 == 0:
                                op("pe", lambda e: e.matmul(ps[BANK_RB][0:64, :], ones_f[64:65, 0:64], fz[r1][64:65, :], start=True, stop=True),
                                   reads=[fzb[r1], cstb], writes=[pb[BANK_RB]])
                            else:
                                op("pe", lambda e: e.matmul(ps[BANK_RB][:, :], ones_f[0:1, :], fz[r1][0:1, :], start=True, stop=True),
                                   reads=[fzb[r1], cstb], writes=[pb[BANK_RB]])
                            op("dve", lambda e: e.tensor_copy(out=fz[r2][o0:o0 + 64, :], in_=ps[BANK_RB][o0:o0 + 64, :]),
                               reads=[pb[BANK_RB]], writes=[fzb[r2]])
                            op("dve", lambda e: e.tensor_tensor(out=foxT[o0:o0 + 64, c, :], in0=ps[bo][o0:o0 + 64, :], in1=fz[r2][o0:o0 + 64, :], op=ALU.mult),
                               reads=[pb[bo], fzb[r2]], writes=[foxb[h]])
                        pend.append([6, fin_pe])

                    def tick():
                        for p in list(pend):
                            p[0] -= 1
                            if p[0] <= 0:
                                p[1]()
                                pend.remove(p)

                    for idx in range(len(items) + SKEW):
                        if idx < len(items):
                            emit_S(*items[idx])
                        if idx >= SKEW:
                            emit_PV(*items[idx - SKEW])
                        tick()
                    while pend:
                        tick()
                    kvstate["maxb"] = b + 1
                    kv_issue()

                    for g in range(4):
                        bk = bank_proj.next()
                        op("pe", lambda e: e.matmul(ps[bk][:, :], wpl[:, g, :], mixed[:, g, :], start=True, stop=True),
                           reads=[mixb[g], wplb], writes=[pb[bk]])
                        op("act", lambda e: e.activation(out=poolT[:, g, :], in_=ps[bk][:, :], func=AF.Copy, scale=vec[:, 36 + g:37 + g]),
                           reads=[pb[bk], vecb], writes=[poolb[g]])

                    for hf in range(2):
                        wW, wWb, _ = wnext("out")
                        for oc in range(4):
                            kc = hf * 4 + oc
                            bk = bank_proj.next()

                            def mmo(e):
                                f = None
                                for c in range(4):
                                    ins = e.matmul(ps[bk][:, :], wW[:, c, oc * 128:(oc + 1) * 128], foxT[:, c, :],
                                                   start=(c == 0), stop=False)
                                    f = ins if f is None else f
                                for g in range(4):
                                    ins = e.matmul(ps[bk][:, :], wW[:, 4 + g, oc * 128:(oc + 1) * 128], poolT[:, g, :],
                                                   start=False, stop=(g == 3))
                                return f, ins
                            op("pe", mmo, reads=foxb + poolb + [wWb], writes=[pb[bk]])
                            op("dve", lambda e: e.tensor_tensor(out=hT[:, kc, :], in0=ps[bk][:, :], in1=hT[:, kc, :], op=ALU.add),
                               reads=[pb[bk], hTb[kc]], writes=[hTb[kc]])
                        wprefetch()

                    a0, a1 = ftr.next(), ftr.next()
                    hb, hbb = hb2, hb2b
                    fm_norm(hT, hTb, hb, hbb, vec, vecb, 8, 512, ft[a0], ftb[a0], ft[a1], ftb[a1], bank_stat.next())
                    gen = prologue(b + 1) if b + 1 < NB else iter(())
                    wQ, wQb, _ = wnext("mq")
                    moT = foxT
                    stage_lists = []
                    for hm in range(4):
                        bk = hm
                        p0, p1 = 2 * (hm % 2), 2 * (hm % 2) + 1
                        r1 = 4 + (hm % 2)
                        bst = 4 + (hm % 2)

                        def proj(bk=bk, hm=hm):
                            def mmq2(e):
                                f = None
                                for kc in range(8):
                                    ins = e.matmul(ps[bk][:, :], wQ[:, kc, hm * 128:(hm + 1) * 128], hb[:, kc, :],
                                                   start=(kc == 0), stop=(kc == 7))
                                    f = ins if f is None else f
                                return f, ins
                            op("pe", mmq2, reads=hbb + [wQb], writes=[pb[bk]])

                        def scores(hm=hm):
                            for mt in range(2):
                                op("pe", lambda e: e.matmul(ps[6 + mt][:, :], mkT[:, hm, mt * 128:(mt + 1) * 128], mqT[hm][:, :], start=True, stop=True),
                                   reads=[mkTb, mqTb[hm]], writes=[pb[6 + mt]])

                        def exps(hm=hm, p0=p0):
                            for mt in range(2):
                                op("act", lambda e: e.activation(out=PT[p0 + mt][:, :], in_=ps[6 + mt][:, :], func=AF.Exp, scale=128.0 ** -0.5),
                                   reads=[pb[6 + mt]], writes=[PTb[p0 + mt]])

                        def pv(hm=hm, bk=bk, p0=p0, p1=p1, bst=bst):
                            def mmpv(e):
                                i1 = e.matmul(ps[bk][:, :], mv[:, 0, hm * 128:(hm + 1) * 128], PT[p0][:, :], start=True, stop=False)
                                e.matmul(ps[bk][:, :], mv[:, 1, hm * 128:(hm + 1) * 128], PT[p1][:, :], start=False, stop=True)
                                e.matmul(ps[bst][:, :], ones_b, PT[p0][:, :], start=True, stop=False)
                                i2 = e.matmul(ps[bst][:, :], ones_b, PT[p1][:, :], start=False, stop=True)
                                return i1, i2
                            op("pe", mmpv, reads=[mvb, cbfb, PTb[p0], PTb[p1]], writes=[pb[bk], pb[bst]])

                        def fin(hm=hm, bk=bk, r1=r1, bst=bst):
                            op("dve", lambda e: e.reciprocal(out=ft[r1][:, :], in_=ps[bst][:, :]), reads=[pb[bst]], writes=[ftb[r1]])
                            op("dve", lambda e: e.tensor_tensor(out=moT[:, hm, :], in0=ps[bk][:, :], in1=ft[r1][:, :], op=ALU.mult),
                               reads=[pb[bk], ftb[r1]], writes=[foxb[2 * hm], foxb[2 * hm + 1]])
                        stage_lists.append([proj] + norm_stages(bk, bst, ones_b, 1.0 / 128, 512, vec, vecb, 34,
                                                                [(0, 128, mqT[hm][:, :], mqTb[hm])], ysq[hm], ysqb[hm], ft[hm], ftb[hm])
                                           + [scores, exps, pv, fin])
                    pipeline(stage_lists, [0, 1, 1, 2, 2, 3, 3, 3, 4, 4], filler=lambda: next(gen, None))
                    for _ in gen:
                        pass
                    wprefetch()
                    wO, wOb, _ = wnext("mo")
                    for oc in range(8):
                        bk = bank_proj.next()

                        def mmo2(e):
                            f = None
                            for hm in range(4):
                                ins = e.matmul(ps[bk][:, :], wO[:, hm, oc * 128:(oc + 1) * 128], moT[:, hm, :],
                                               start=(hm == 0), stop=(hm == 3))
                                f = ins if f is None else f
                            return f, ins
                        op("pe", mmo2, reads=foxb + [wOb], writes=[pb[bk]])
                        op("dve", lambda e: e.tensor_tensor(out=hT[:, oc, :], in0=ps[bk][:, :], in1=hT[:, oc, :], op=ALU.add),
                           reads=[pb[bk], hTb[oc]], writes=[hTb[oc]])
                    wprefetch()
                    op("pool", lambda e: e.dma_start(out=hscr_v[:, :, t0:t0 + 512], in_=hT[:]),
                       reads=hTb, writes=[hsb[b]], dma=True)
                sc.barrier()

            with ExitStack() as fx:
                vec = sb("vecF", [128, NVEC], F32, fx)
                vecb = Buf()
                op("sp", lambda e: e.dma_start(out=vec[:], in_=vecs_d[l, :, :]), writes=[vecb], dma=True)
                wgu = sb("wgu", [128, 8, 2 * DFF], BF16, fx)
                wgub = [Buf() for _ in range(8)]
                wdn = sb("wdn", [128, NFC, 1024], BF16, fx)
                wdnb = [Buf() for _ in range(NFC)]
                for kc in range(8):
                    op("sp", lambda e: e.dma_start(out=wgu[:, kc, :], in_=s_gu[l, :, kc, :]), writes=[wgub[kc]], dma=True)
                for f0 in range(0, NFC, 6):
                    f1 = min(NFC, f0 + 6)
                    op("sp", lambda e: e.dma_start(out=wdn[:, f0:f1, :], in_=s_dn[l, :, f0:f1, :]), writes=wdnb[f0:f1], dma=True)
                hT2 = [sb("hTF%d" % i, [128, 8, 512], F32, fx) for i in range(2)]
                hT2b = [[Buf() for _ in range(8)] for _ in range(2)]
                hb = sb("hbF", [128, 8, 512], BF16, fx)
                hbb = [Buf() for _ in range(8)]
                actT = sb("actT", [128, NFC, 512], BF16, fx)
                actb = [Buf() for _ in range(NFC)]
                sg = [sb("sg%d" % i, [128, 512], F32, fx) for i in range(2)]
                sgb = [Buf() for _ in range(2)]
                fa = [sb("fa%d" % i, [128, 512], F32, fx) for i in range(2)]
                fab = [Buf() for _ in range(2)]
                ot, otb = sg, sgb
                otr = RR(range(2))
                bank_gu = RR([0, 1, 2, 3])
                bank_dn = RR([4, 5])
                bank_st = RR([6, 7])

                def load_h(b):
                    op("pool", lambda e: e.dma_start(out=hT2[b % 2][:], in_=hscr_v[:, :, b * 512:(b + 1) * 512]),
                       reads=[hsb[b]], writes=hT2b[b % 2], dma=True)

                def norm3(b, part):
                    fm_norm(hT2[b % 2], hT2b[b % 2], hb, hbb, vec, vecb, 16, 512, fa[0], fab[0], fa[1], fab[1], 6 + (b % 2), part=part)

                load_h(0)
                norm3(0, None)
                for b in range(NB):
                    t0 = b * 512
                    hT, hTb = hT2[b % 2], hT2b[b % 2]
                    if b + 1 < NB:
                        load_h(b + 1)
                    for fc in range(NFC):
                        bg, bu = bank_gu.next(), bank_gu.next()

                        def mmg(e):
                            f = None
                            for kc in range(8):
                                ins = e.matmul(ps[bg][:, :], wgu[:, kc, fc * 128:(fc + 1) * 128], hb[:, kc, :],
                                               start=(kc == 0), stop=(kc == 7))
                                f = ins if f is None else f
                            return f, ins
                        op("pe", mmg, reads=hbb + wgub, writes=[pb[bg]])

                        def mmu(e):
                            f = None
                            for kc in range(8):
                                ins = e.matmul(ps[bu][:, :], wgu[:, kc, DFF + fc * 128:DFF + (fc + 1) * 128], hb[:, kc, :],
                                               start=(kc == 0), stop=(kc == 7))
                                f = ins if f is None else f
                            return f, ins
                        op("pe", mmu, reads=hbb + wgub, writes=[pb[bu]])
                        si = fc % 2
                        op("act", lambda e: e.activation(out=sg[si][:, :], in_=ps[bg][:, :], func=AF.Silu),
                           reads=[pb[bg]], writes=[sgb[si]])
                        op("dve", lambda e: e.tensor_tensor(out=actT[:, fc, :], in0=ps[bu][:, :], in1=sg[si][:, :], op=ALU.mult),
                           reads=[pb[bu], sgb[si]], writes=[actb[fc]])
                    if b + 1 < NB:
                        norm3(b + 1, "A")
                    for oc in range(8):
                        bk = bank_dn.next()

                        def mmd(e):
                            f = None
                            for fc in range(NFC):
                                ins = e.matmul(ps[bk][:, :], wdn[:, fc, oc * 128:(oc + 1) * 128], actT[:, fc, :],
                                               start=(fc == 0), stop=(fc == NFC - 1))
                                f = ins if f is None else f
                            return f, ins
                        op("pe", mmd, reads=actb + wdnb, writes=[pb[bk]])
                        op("dve", lambda e: e.tensor_tensor(out=hT[:, oc, :], in0=ps[bk][:, :], in1=hT[:, oc, :], op=ALU.add),
                           reads=[pb[bk], hTb[oc]], writes=[hTb[oc]])
                        if oc == 3 and b + 1 < NB:
                            norm3(b + 1, "B")
                    if last_layer:
                        for i in range(4):
                            for hf in range(2):
                                bk = bank_st.next()

                                def trb(e):
                                    f = None
                                    for k in range(4):
                                        ins = e.transpose(ps[bk][:, k * 128:(k + 1) * 128], hT[:, hf * 4 + k, i * 128:(i + 1) * 128], ident)
                                        f = ins if f is None else f
                                    return f, ins
                                op("pe", trb, reads=hTb[hf * 4:(hf + 1) * 4] + [cstb], writes=[pb[bk]])
                                oi = otr.next()
                                op("act", lambda e: e.activation(out=ot[oi][:, :], in_=ps[bk][:, :], func=AF.Copy),
                                   reads=[pb[bk]], writes=[otb[oi]])
                                op("pool", lambda e: e.dma_start(out=out_d[t0 + i * 128:t0 + (i + 1) * 128, hf * 512:(hf + 1) * 512], in_=ot[oi][:, :]),
                                   reads=[otb[oi]], dma=True)
                    else:
                        op("pool", lambda e: e.dma_start(out=hscr_v[:, :, t0:t0 + 512], in_=hT[:]),
                           reads=hTb, writes=[hsb[b]], dma=True)
                sc.barrier()
        sc.finish()
    return nc


def make_consts():
    c = np.zeros((128, 896), np.float32)
    c[:, 0:128] = np.eye(128, dtype=np.float32)
    s = np.arange(128)[:, None]
    t = np.arange(128)[None, :]
    c[:, 128:256] = (s <= t).astype(np.float32)
    c[:, 256:384] = ((s // 64) == (t // 64)).astype(np.float32)
    c[:, 384:512] = 1.0
    for g, w in enumerate((2, 4, 8, 16)):
        c[:, 512 + g * 16:512 + (g + 1) * 16] = (1.0 / np.minimum(np.arange(16) + 1, w)).astype(np.float32)[None, :]
    c[:, 640:768] = np.where(s > t, -30000.0, 0.0).astype(np.float32)
    c[:, 768:896] = np.eye(128, dtype=np.float32)
    return c


def make_vecs(inp, NL):
    v = np.zeros((NL, 128, NVEC), np.float32)
    p = np.arange(128)
    for l in range(NL):
        for k, name in enumerate(("g_mix", "g_mem_q", "g_ffn", "g_mem_kv")):
            v[l, :, 8 * k:8 * k + 8] = np.asarray(inp[name][l]).reshape(8, 128).T
        v[l, :, 32] = np.asarray(inp["g_q_fox"][l])[p % 64]
        v[l, :, 33] = np.asarray(inp["g_k_fox"][l])[p % 64]
        v[l, :, 34] = np.asarray(inp["g_q_mem"][l])
        v[l, :, 35] = np.asarray(inp["g_k_mem"][l])
        v[l, :, 36:40] = np.asarray(inp["pool_scale"][l]).reshape(4, 128).T
        v[l, :, 40:72] = np.tile(np.asarray(inp["b_forget"][l]), 4)[None, :]
    return v


_NC_CACHE = {}


def run(inp, S, NL, ncores, trace=False):
    key = (S, NL)
    if key not in _NC_CACHE:
        _NC_CACHE[key] = build(S, NL)
    nc = _NC_CACHE[key]
    consts = make_consts()
    vecs = make_vecs(inp, NL)
    shared = {
        "w_in": np.ascontiguousarray(inp["w_in"], np.float32),
        "w_pool": np.ascontiguousarray(inp["w_pool"], np.float32),
        "w_out": np.ascontiguousarray(inp["w_out"], np.float32),
        "w_mem_q": np.ascontiguousarray(inp["w_mem_q"], np.float32),
        "w_mem_kv": np.ascontiguousarray(inp["w_mem_kv"], np.float32),
        "w_mem_out": np.ascontiguousarray(inp["w_mem_out"], np.float32),
        "w_gate_up": np.ascontiguousarray(inp["w_gate_up"], np.float32),
        "w_down": np.ascontiguousarray(inp["w_down"], np.float32),
        "vecs": vecs,
        "consts": consts,
    }
    x = np.asarray(inp["x"], np.float32)
    mem = np.asarray(inp["mem"], np.float32)
    in_maps = []
    for i in range(ncores):
        m = dict(shared)
        m["x"] = np.ascontiguousarray(x[i])
        m["mem"] = np.ascontiguousarray(mem[i])
        in_maps.append(m)
    res = run_bass_kernel_spmd(nc, in_maps, core_ids=list(range(ncores)), trace=trace)
    out = np.stack([np.asarray(r["out"], np.float32) for r in res.results], axis=0)
    return out, res


def kernel(**inputs):
    x = np.asarray(inputs["x"])
    B, S, _ = x.shape
    NL = int(np.asarray(inputs["w_in"]).shape[0])
    out, _ = run(inputs, S, NL, B)
    return out.astype(np.float32)
```

```python
import numpy as np
from contextlib import ExitStack
import concourse.bass as bass
import concourse.mybir as mybir
from concourse.bass_utils import run_bass_kernel_spmd

F32 = mybir.dt.float32
BF16 = mybir.dt.bfloat16
ALU = mybir.AluOpType
AF = mybir.ActivationFunctionType

D = 1024
EPS = 1e-6
NVEC = 72
DFF = 2816
NFC = 22
MEM = 256


class Tok:
    __slots__ = ("s", "v", "clk")

    def __init__(self, s, v, clk):
        self.s = s
        self.v = v
        self.clk = clk


class Buf:
    __slots__ = ("w", "rd", "excl")

    def __init__(self, excl=False):
        self.w = None
        self.rd = {}
        self.excl = excl


class RR:
    def __init__(self, items):
        self.items = list(items)
        self.i = 0

    def next(self):
        r = self.items[self.i % len(self.items)]
        self.i += 1
        return r


class Sched:
    COMPUTE = ("pe", "act", "dve", "pool")

    def __init__(self, nc, es, nslots=8):
        self.nc = nc
        self.h = {"pe": nc.tensor, "act": nc.scalar, "dve": nc.vector, "pool": nc.gpsimd, "sp": nc.sync}
        self.sems = []
        self.own = {}
        for e in self.COMPUTE:
            self.own[e] = len(self.sems)
            self.sems.append(es.enter_context(nc.semaphore("sem_" + e)))
        self.slots = {}
        for q in ("sp", "pool"):
            self.slots[q] = []
            for i in range(nslots):
                self.slots[q].append(len(self.sems))
                self.sems.append(es.enter_context(nc.semaphore("dma_%s_%d" % (q, i))))
        self.n = len(self.sems)
        self.cnt = [0] * self.n
        self.clk = {e: [0] * self.n for e in self.h}
        self.dman = {"sp": 0, "pool": 0}
        self.nops = 0

    def op(self, eng, fn, reads=(), writes=(), dma=False):
        clk = self.clk[eng]
        own = self.own.get(eng)
        deps = []
        wr = list(writes)
        for b in reads:
            if b.excl:
                wr.append(b)
            elif b.w is not None:
                deps.append((b.w, 0))
        for b in wr:
            if b.w is not None:
                deps.append((b.w, 1))
            for t in b.rd.values():
                deps.append((t, 2))
        waits = {}
        for t, kind in deps:
            if (not dma) and t.s == own:
                if kind != 0 or eng == "pe":
                    continue
            if clk[t.s] >= t.v:
                continue
            waits[t.s] = max(waits.get(t.s, 0), t.v)
            tc = t.clk
            for i in range(self.n):
                if tc[i] > clk[i]:
                    clk[i] = tc[i]
            if clk[t.s] < t.v:
                clk[t.s] = t.v
        if dma:
            k = self.dman[eng]
            self.dman[eng] += 1
            sl = self.slots[eng][k % len(self.slots[eng])]
            prev = self.cnt[sl]
            if clk[sl] < prev:
                waits[sl] = max(waits.get(sl, 0), prev)
                clk[sl] = prev
            self.cnt[sl] += 16
            tok = Tok(sl, self.cnt[sl], list(clk))
            inc = (sl, 16)
        else:
            self.cnt[own] += 1
            snap = list(clk)
            snap[own] = self.cnt[own]
            tok = Tok(own, self.cnt[own], snap)
            inc = (own, 1)
        for b in reads:
            if not b.excl:
                b.rd[tok.s] = tok
        for b in wr:
            b.w = tok
            b.rd = {}
        e = self.h[eng]
        wl = sorted(waits.items())
        attach = None
        if (not dma) and wl:
            attach = wl.pop()
        for s, v in wl:
            e.wait_ge(self.sems[s], v)
        r = fn(e)
        first, last = r if isinstance(r, tuple) else (r, r)
        if attach is not None:
            first._wait_ge(self.sems[attach[0]], attach[1])
        last.then_inc(self.sems[inc[0]], inc[1])
        self.nops += 1
        return tok

    def barrier(self):
        for eng, e in self.h.items():
            clk = self.clk[eng]
            for s in range(self.n):
                if s == self.own.get(eng):
                    continue
                if clk[s] < self.cnt[s]:
                    e.wait_ge(self.sems[s], self.cnt[s])
                    clk[s] = self.cnt[s]
        for eng in self.h:
            own = self.own.get(eng)
            for s in range(self.n):
                if s != own:
                    self.clk[eng][s] = self.cnt[s]

    def finish(self):
        e = self.h["sp"]
        clk = self.clk["sp"]
        for s in range(self.n):
            if clk[s] < self.cnt[s]:
                e.wait_ge(self.sems[s], self.cnt[s])
                clk[s] = self.cnt[s]


def build(S, NL):
    NB = S // 512
    NT = S // 128
    nc = bass.Bass("TRN2", target_bir_lowering=False)

    def din(name, shape, dt=F32):
        return nc.dram_tensor(name, list(shape), dt, kind="ExternalInput").ap()

    def dscr(name, shape, dt):
        return nc.dram_tensor(name, list(shape), dt, kind="Internal").ap()

    x_d = din("x", [S, D])
    mem_d = din("mem", [MEM, D])
    w_in_d = din("w_in", [NL, D, 2056])
    w_pool_d = din("w_pool", [NL, 4, 128, 128])
    w_out_d = din("w_out", [NL, D, D])
    w_mq_d = din("w_mem_q", [NL, D, 512])
    w_mkv_d = din("w_mem_kv", [NL, D, 1024])
    w_mo_d = din("w_mem_out", [NL, 512, D])
    w_gu_d = din("w_gate_up", [NL, D, 2 * DFF])
    w_dn_d = din("w_down", [NL, DFF, D])
    vecs_d = din("vecs", [NL, 128, NVEC])
    consts_d = din("consts", [128, 896])
    out_d = nc.dram_tensor("out", [S, D], F32, kind="ExternalOutput").ap()

    s_in = dscr("s_in", [NL, 128, 8, 2056], BF16)
    s_pool = dscr("s_pool", [NL, 128, 4, 128], BF16)
    s_out = dscr("s_out", [NL, 128, 8, 1024], BF16)
    s_mq = dscr("s_mq", [NL, 128, 8, 512], BF16)
    s_mkv = dscr("s_mkv", [NL, 128, 8, 1024], BF16)
    s_mo = dscr("s_mo", [NL, 128, 4, 1024], BF16)
    s_gu = dscr("s_gu", [NL, 128, 8, 2 * DFF], BF16)
    s_dn = dscr("s_dn", [NL, 128, NFC, 1024], BF16)
    hscr = dscr("hscr", [8, 128, S], F32)
    kscr = dscr("kscr", [4, 128, S], BF16)
    vscr = dscr("vscr", [4, 128, NT, 192], BF16)
    hscr_v = hscr.rearrange("k p s -> p k s")
    kscr_v = kscr.rearrange("c p s -> p c s")

    with ExitStack() as es:
        sc = Sched(nc, es)
        op = sc.op

        uniq = [0]

        def sb(name, shape, dt, stack=es):
            uniq[0] += 1
            return stack.enter_context(nc.sbuf_tensor("%s_%d" % (name, uniq[0]), list(shape), dt))

        ps = [es.enter_context(nc.psum_tensor("ps%d" % i, [128, 512], F32)) for i in range(8)]
        pb = [Buf(excl=True) for _ in range(8)]

        cst = sb("cst", [128, 896], F32)
        cstb = Buf()
        ident = cst[:, 0:128]
        triu_f = cst[:, 128:256]
        ones_f = cst[:, 384:512]
        invcnt = cst[:, 512:576]
        cbf = sb("cbf", [128, 384], BF16)
        cbfb = Buf()
        triu_b = cbf[:, 0:128]
        bd64_b = cbf[:, 128:256]
        ones_b = cbf[:, 256:384]
        epsT = sb("epsT", [128, 2], F32)
        epsb = Buf()
        op("sp", lambda e: e.dma_start(out=cst[:], in_=consts_d[:, :]), writes=[cstb], dma=True)
        op("dve", lambda e: e.tensor_copy(out=cbf[:], in_=cst[:, 128:512]), reads=[cstb], writes=[cbfb])
        cbf2 = sb("cbf2", [128, 256], BF16)
        mneg_b = cbf2[:, 0:128]
        ident_b = cbf2[:, 128:256]
        op("dve", lambda e: e.tensor_copy(out=cbf2[:], in_=cst[:, 640:896]), reads=[cstb], writes=[cbfb])
        op("dve", lambda e: e.memset(epsT[:, 0:1], EPS), writes=[epsb])
        op("dve", lambda e: e.memset(epsT[:, 1:2], 1.0), writes=[epsb])

        with ExitStack() as ps_es:
            CW = 2816
            NST = 3
            stf = [sb("stf%d" % i, [128, CW], F32, ps_es) for i in range(NST)]
            stb = [sb("stb%d" % i, [128, CW], BF16, ps_es) for i in range(NST)]
            stfb = [Buf() for _ in range(NST)]
            stbb = [Buf() for _ in range(NST)]
            cnt = [0]
            cast_eng = ["dve", "act", "pool"]

            def prep(src, dst, npart, ncols):
                for c0 in range(0, ncols, CW):
                    c1 = min(ncols, c0 + CW)
                    w = c1 - c0
                    i = cnt[0] % NST
                    ce = cast_eng[cnt[0] % 3]
                    cnt[0] += 1
                    op("sp", lambda e: e.dma_start(out=stf[i][0:npart, 0:w], in_=src[:, c0:c1]),
                       writes=[stfb[i]], dma=True)
                    if ce == "act":
                        op("act", lambda e: e.activation(out=stb[i][0:npart, 0:w], in_=stf[i][0:npart, 0:w], func=AF.Copy),
                           reads=[stfb[i]], writes=[stbb[i]])
                    else:
                        op(ce, lambda e: e.tensor_copy(out=stb[i][0:npart, 0:w], in_=stf[i][0:npart, 0:w]),
                           reads=[stfb[i]], writes=[stbb[i]])
                    op("pool", lambda e: e.dma_start(out=dst[:, c0:c1], in_=stb[i][0:npart, 0:w]),
                       reads=[stbb[i]], dma=True)

            for l in range(NL):
                for kc in range(8):
                    prep(w_in_d[l, kc * 128:(kc + 1) * 128, :], s_in[l, :, kc, :], 128, 2056)
                for g in range(4):
                    prep(w_pool_d[l, g, :, :], s_pool[l, :, g, :], 128, 128)
                for kc in range(8):
                    prep(w_out_d[l, kc * 128:(kc + 1) * 128, :], s_out[l, :, kc, :], 128, 1024)
                for kc in range(8):
                    prep(w_mq_d[l, kc * 128:(kc + 1) * 128, :], s_mq[l, :, kc, :], 128, 512)
                    prep(w_mkv_d[l, kc * 128:(kc + 1) * 128, :], s_mkv[l, :, kc, :], 128, 1024)
                for hm in range(4):
                    prep(w_mo_d[l, hm * 128:(hm + 1) * 128, :], s_mo[l, :, hm, :], 128, 1024)
                for kc in range(8):
                    prep(w_gu_d[l, kc * 128:(kc + 1) * 128, :], s_gu[l, :, kc, :], 128, 2 * DFF)
                for fc in range(NFC):
                    prep(w_dn_d[l, fc * 128:(fc + 1) * 128, :], s_dn[l, :, fc, :], 128, 1024)
            sc.barrier()

        hsb = [Buf() for _ in range(NB)]

        def fm_norm(hT, hTb, hb, hbb, vec, vecb, gcol, n, A0, A0b, A1, A1b, bank, part=None):
            if part in (None, "A"):
                for kc in range(8):
                    op("act", lambda e: e.activation(out=hb[:, kc, 0:n], in_=hT[:, kc, 0:n], func=AF.Square),
                       reads=[hTb[kc]], writes=[hbb[kc]])
            if part == "A":
                return

            def mm(e):
                f = None
                for kc in range(8):
                    i = e.matmul(ps[bank][:, 0:n], ones_b, hb[:, kc, 0:n], start=(kc == 0), stop=(kc == 7))
                    f = i if f is None else f
                return f, i
            op("pe", mm, reads=hbb + [cbfb], writes=[pb[bank]])
            op("act", lambda e: e.activation(out=A0[:, 0:n], in_=ps[bank][:, 0:n], func=AF.Ln,
                                             bias=epsT[:, 0:1], scale=1.0 / D),
               reads=[epsb, pb[bank]], writes=[A0b])
            op("act", lambda e: e.activation(out=A1[:, 0:n], in_=A0[:, 0:n], func=AF.Exp, scale=-0.5), reads=[A0b], writes=[A1b])
            for kc in range(8):
                op("dve", lambda e: e.scalar_tensor_tensor(out=hb[:, kc, 0:n], in0=hT[:, kc, 0:n],
                                                           scalar=vec[:, gcol + kc:gcol + kc + 1], in1=A1[:, 0:n],
                                                           op0=ALU.mult, op1=ALU.mult),
                   reads=[hTb[kc], A1b, vecb], writes=[hbb[kc]])

        def head_norm(bank, bank2, stat_lhsT, inv_n, n, vec, gcol, vecb, dsts, ysq, ysqb, rA, rAb, rB, rBb):
            op("act", lambda e: e.activation(out=ysq[:, 0:n], in_=ps[bank][:, 0:n], func=AF.Square),
               reads=[pb[bank]], writes=[ysqb])
            op("pe", lambda e: e.matmul(ps[bank2][:, 0:n], stat_lhsT, ysq[:, 0:n], start=True, stop=True),
               reads=[ysqb, cbfb], writes=[pb[bank2]])
            op("act", lambda e: e.activation(out=rA[:, 0:n], in_=ps[bank2][:, 0:n], func=AF.Ln,
                                             bias=epsT[:, 0:1], scale=inv_n),
               reads=[epsb, pb[bank2]], writes=[rAb])
            op("act", lambda e: e.activation(out=rB[:, 0:n], in_=rA[:, 0:n], func=AF.Exp, scale=-0.5), reads=[rAb], writes=[rBb])
            for (r0_, r1_, dst, dstb) in dsts:
                op("dve", lambda e: e.scalar_tensor_tensor(out=dst, in0=ps[bank][r0_:r1_, 0:n], scalar=vec[r0_:r1_, gcol:gcol + 1],
                                                           in1=rB[r0_:r1_, 0:n], op0=ALU.mult, op1=ALU.mult),
                   reads=[rBb, pb[bank], vecb], writes=[dstb])

        def head_norm_multi(items, n, vec, vecb):
            for (bank, bank2, lhs, inv_n, gcol, dsts, ysq, ysqb, r, rb) in items:
                op("act", lambda e: e.activation(out=ysq[:, 0:n], in_=ps[bank][:, 0:n], func=AF.Square),
                   reads=[pb[bank]], writes=[ysqb])
            for (bank, bank2, lhs, inv_n, gcol, dsts, ysq, ysqb, r, rb) in items:
                op("pe", lambda e: e.matmul(ps[bank2][:, 0:n], lhs, ysq[:, 0:n], start=True, stop=True),
                   reads=[ysqb, cbfb], writes=[pb[bank2]])
            for (bank, bank2, lhs, inv_n, gcol, dsts, ysq, ysqb, r, rb) in items:
                op("act", lambda e: e.activation(out=r[:, 0:n], in_=ps[bank2][:, 0:n], func=AF.Sqrt,
                                                 bias=epsT[:, 0:1], scale=inv_n),
                   reads=[epsb, pb[bank2]], writes=[rb])
            for (bank, bank2, lhs, inv_n, gcol, dsts, ysq, ysqb, r, rb) in items:
                op("dve", lambda e: e.reciprocal(out=r[:, 0:n], in_=r[:, 0:n]), reads=[rb], writes=[rb])
            for (bank, bank2, lhs, inv_n, gcol, dsts, ysq, ysqb, r, rb) in items:
                for (r0_, r1_, dst, dstb) in dsts:
                    op("dve", lambda e: e.scalar_tensor_tensor(out=dst, in0=ps[bank][r0_:r1_, 0:n], scalar=vec[r0_:r1_, gcol:gcol + 1],
                                                               in1=r[r0_:r1_, 0:n], op0=ALU.mult, op1=ALU.mult),
                       reads=[rb, pb[bank], vecb], writes=[dstb])

        def pipeline(stage_lists, delays, filler=None):
            n = len(stage_lists)
            for t in range(n + max(delays)):
                for k, d in enumerate(delays):
                    i = t - d
                    if 0 <= i < n:
                        stage_lists[i][k]()
                if filler is not None:
                    filler()

        def norm_stages(bank, bank2, lhs, inv_n, n, vec, vecb, gcol, dsts, ysq, ysqb, r, rb):
            def sq():
                op("act", lambda e: e.activation(out=ysq[:, 0:n], in_=ps[bank][:, 0:n], func=AF.Square),
                   reads=[pb[bank]], writes=[ysqb])

            def st():
                op("pe", lambda e: e.matmul(ps[bank2][:, 0:n], lhs, ysq[:, 0:n], start=True, stop=True),
                   reads=[ysqb, cbfb], writes=[pb[bank2]])

            def sr():
                op("act", lambda e: e.activation(out=r[:, 0:n], in_=ps[bank2][:, 0:n], func=AF.Ln,
                                                 bias=epsT[:, 0:1], scale=inv_n),
                   reads=[epsb, pb[bank2]], writes=[rb])

            def rc():
                op("act", lambda e: e.activation(out=r[:, 0:n], in_=r[:, 0:n], func=AF.Exp, scale=-0.5), reads=[rb], writes=[rb])

            def stt():
                for (r0_, r1_, dst, dstb) in dsts:
                    op("dve", lambda e: e.scalar_tensor_tensor(out=dst, in0=ps[bank][r0_:r1_, 0:n], scalar=vec[r0_:r1_, gcol:gcol + 1],
                                                               in1=r[r0_:r1_, 0:n], op0=ALU.mult, op1=ALU.mult),
                       reads=[rb, pb[bank], vecb], writes=[dstb])
            return [sq, st, sr, rc, stt]

        for l in range(NL):
            first_layer = (l == 0)
            last_layer = (l == NL - 1)
            with ExitStack() as mx:
                vec = sb("vec", [128, NVEC], F32, mx)
                vecb = Buf()
                op("sp", lambda e: e.dma_start(out=vec[:], in_=vecs_d[l, :, :]), writes=[vecb], dma=True)
                wf = sb("wf", [128, 8, 8], BF16, mx)
                wfb = Buf()
                op("sp", lambda e: e.dma_start(out=wf[:], in_=s_in[l, :, :, 2048:2056]), writes=[wfb], dma=True)
                wpl = sb("wpl", [128, 4, 128], BF16, mx)
                wplb = Buf()
                op("sp", lambda e: e.dma_start(out=wpl[:], in_=s_pool[l, :, :, :]), writes=[wplb], dma=True)

                hT2 = [sb("hT%d" % i, [128, 8, 512], F32, mx) for i in range(2)]
                hT2b = [[Buf() for _ in range(8)] for _ in range(2)]
                hb1 = sb("hb1", [128, 8, 512], BF16, mx)
                hb1b = [Buf() for _ in range(8)]
                hb2 = sb("hb2", [128, 8, 512], BF16, mx)
                hb2b = [Buf() for _ in range(8)]
                pa = [sb("pa%d" % i, [128, 512], F32, mx) for i in range(2)]
                pab = [Buf() for _ in range(2)]
                bank_pro = RR([6, 7])
                NF = 6
                ft = [sb("ft%d" % i, [128, 512], F32, mx) for i in range(NF)]
                ftb = [Buf() for _ in range(NF)]
                ftr = RR(range(NF))
                qT = sb("qT", [128, 8, 512], BF16, mx)
                qTb = [Buf() for _ in range(8)]
                kcur = sb("kcur", [128, 4, 512], BF16, mx)
                kcb = [Buf() for _ in range(4)]
                vcur = sb("vcur", [128, 4, 4, 192], BF16, mx)
                vcb = [Buf() for _ in range(4)]
                NR = 4
                ring = [sb("ring%d" % i, [128, 4096], BF16, mx) for i in range(NR)]
                ringb = [Buf() for _ in range(NR)]
                NPT = 4
                PT = [sb("PT%d" % i, [128, 512], BF16, mx) for i in range(NPT)]
                PTb = [Buf() for _ in range(NPT)]
                ptr = RR(range(NPT))
                biasb = [sb("biasb%d" % i, [128, NT], F32, mx) for i in range(8)]
                biasbb = [Buf() for _ in range(8)]
                foxT = sb("foxT", [128, 4, 512], BF16, mx)
                foxb = [Buf() for _ in range(8)]
                pin = sb("pin", [128, 4, 528], F32, mx)
                pinb = [Buf() for _ in range(4)]
                ptmp = [sb("ptmp%d" % i, [128, 528], F32, mx) for i in range(2)]
                ptmpb = [Buf() for _ in range(2)]
                mixed = sb("mixed", [128, 4, 512], BF16, mx)
                mixb = [Buf() for _ in range(4)]
                poolT = sb("poolT", [128, 4, 512], BF16, mx)
                poolb = [Buf() for _ in range(4)]
                ysq = [sb("ysq%d" % i, [128, 512], BF16, mx) for i in range(4)]
                ysqb = [Buf() for _ in range(4)]
                mqT = [sb("mqT%d" % i, [128, 512], BF16, mx) for i in range(4)]
                mqTb = [Buf() for _ in range(4)]
                xin = [sb("xin%d" % i, [128, 512], F32, mx) for i in range(3)]
                xinb = [Buf() for _ in range(3)]
                xr = RR(range(3))
                negc = sb("negc", [128, NT, 8], F32, mx)
                negcb = Buf()
                cend = sb("cend", [128, NT + 1, 8], F32, mx)
                cendb = Buf()
                fl = sb("fl", [128, 3, 32], F32, mx)
                flb = [Buf() for _ in range(3)]
                mkT = sb("mkT", [128, 4, MEM], BF16, mx)
                mkTb = Buf()
                mv = sb("mv", [128, 2, 512], BF16, mx)
                mvb = Buf()
                ksb = [Buf() for _ in range(NB)]
                vsb = [Buf() for _ in range(NB)]

                bank_proj = RR([0, 1])
                bank_stat = RR([7, 4])
                bank_S = RR([2, 3, 4])
                bank_O = RR([5, 6])
                bank_O2 = RR([(5, 6), (7, 0)])
                BANK_RB = 1

                op("pool", lambda e: e.memset(qT[:], 0.0), writes=qTb)
                op("pool", lambda e: e.memset(vcur[:, :, :, 64:128], 1.0), writes=vcb)
                op("dve", lambda e: e.memset(cend[:, 0, :], 0.0), writes=[cendb])
                op("pool", lambda e: e.memset(pin[:, :, 0:16], 0.0), writes=pinb)

                chunks = []
                for b in range(NB):
                    chunks.append(("inQ", s_in[l, :, :, 0:512], 128, (8, 512)))
                    chunks.append(("inK", s_in[l, :, :, 512:1024], 128, (8, 512)))
                    chunks.append(("inV", s_in[l, :, :, 1024:1536], 128, (8, 512)))
                    chunks.append(("inP", s_in[l, :, :, 1536:2048], 128, (8, 512)))
                    for hf in range(2):
                        chunks.append(("out", s_out[l, :, :, hf * 512:(hf + 1) * 512], 128, (8, 512)))
                    chunks.append(("mq", s_mq[l, :, :, :], 128, (8, 512)))
                    chunks.append(("mo", s_mo[l, :, :, :], 128, (4, 1024)))
                wstate = {"issued": 0, "cur": 0}

                def wview(k):
                    name, src, npart, (a, bcols) = chunks[k]
                    return ring[k % NR][0:npart, 0:a * bcols].rearrange("p (a b) -> p a b", a=a)

                def wissue(upto):
                    while wstate["issued"] < min(upto, len(chunks)):
                        k = wstate["issued"]
                        src = chunks[k][1]
                        dst = wview(k)
                        op("sp", lambda e: e.dma_start(out=dst, in_=src), writes=[ringb[k % NR]], dma=True)
                        wstate["issued"] += 1

                def wnext(name):
                    k = wstate["cur"]
                    assert chunks[k][0] == name, (chunks[k][0], name)
                    wissue(k + 1)
                    wstate["cur"] += 1
                    return wview(k), ringb[k % NR], k

                def wprefetch():
                    wissue(wstate["cur"] + NR)

                with ExitStack() as mk:
                    memT = sb("memT", [128, 8, MEM], F32, mk)
                    memTb = [Buf() for _ in range(8)]
                    mnb = sb("mnb", [128, 8, MEM], BF16, mk)
                    mnbb = [Buf() for _ in range(8)]
                    wkv = sb("wkv", [128, 8, 1024], BF16, mk)
                    wkvb = Buf()
                    op("sp", lambda e: e.dma_start(out=wkv[:], in_=s_mkv[l, :, :, :]), writes=[wkvb], dma=True)
                    for i in range(2):
                        for hf in range(2):
                            xi = xr.next()
                            op("pool", lambda e: e.dma_start(out=xin[xi][:], in_=mem_d[i * 128:(i + 1) * 128, hf * 512:(hf + 1) * 512]),
                               writes=[xinb[xi]], dma=True)
                            bk = bank_proj.next()

                            def tr(e):
                                f = None
                                for k in range(4):
                                    ins = e.transpose(ps[bk][:, k * 128:(k + 1) * 128], xin[xi][:, k * 128:(k + 1) * 128], ident)
                                    f = ins if f is None else f
                                return f, ins
                            op("pe", tr, reads=[xinb[xi], cstb], writes=[pb[bk]])
                            op("dve", lambda e: e.tensor_copy(out=memT[:, hf * 4:(hf + 1) * 4, i * 128:(i + 1) * 128],
                                                              in_=ps[bk][:, 0:512].rearrange("p (k t) -> p k t", k=4)),
                               reads=[pb[bk]], writes=memTb[hf * 4:(hf + 1) * 4])
                    a0, a1 = ftr.next(), ftr.next()
                    fm_norm(memT, memTb, mnb, mnbb, vec, vecb, 24, MEM, ft[a0], ftb[a0], ft[a1], ftb[a1], bank_stat.next())
                    for hm in range(4):
                        bk = bank_proj.next()

                        def mmk(e):
                            f = None
                            for kc in range(8):
                                ins = e.matmul(ps[bk][:, 0:MEM], wkv[:, kc, hm * 128:(hm + 1) * 128], mnb[:, kc, :],
                                               start=(kc == 0), stop=(kc == 7))
                                f = ins if f is None else f
                            return f, ins
                        op("pe", mmk, reads=mnbb + [wkvb], writes=[pb[bk]])
                        a0, a1 = ftr.next(), ftr.next()
                        head_norm(bk, bank_stat.next(), ones_b, 1.0 / 128, MEM, vec, 35, vecb, [(0, 128, mkT[:, hm, :], mkTb)],
                                  ysq[hm % 2], ysqb[hm % 2], ft[a0], ftb[a0], ft[a1], ftb[a1])
                    for mt in range(2):
                        bk = bank_proj.next()

                        def mmv(e):
                            f = None
                            for kc in range(8):
                                ins = e.matmul(ps[bk][:, :], mnb[:, kc, mt * 128:(mt + 1) * 128], wkv[:, kc, 512:1024],
                                               start=(kc == 0), stop=(kc == 7))
                                f = ins if f is None else f
                            return f, ins
                        op("pe", mmv, reads=mnbb + [wkvb], writes=[pb[bk]])
                        op("dve", lambda e: e.tensor_copy(out=mv[:, mt, :], in_=ps[bk][:, :]), reads=[pb[bk]], writes=[mvb])
                    sc.barrier()

                CK = 8
                NCH = 4
                Kc = [sb("Kc%d" % i, [128, CK * 128], BF16, mx) for i in range(NCH)]
                Vc = [sb("Vc%d" % i, [128, CK, 192], BF16, mx) for i in range(NCH)]
                kvb = [Buf() for _ in range(NCH)]
                fz = [sb("fz%d" % i, [128, 512], F32, mx) for i in range(4)]
                fzb = [Buf() for _ in range(4)]
                fzr = RR(range(4))
                kvchunks = []
                kvbase = {}
                for b_ in range(1, NB):
                    for c_ in range(4):
                        kvbase[(b_, c_)] = len(kvchunks)
                        for lo in range(0, 4 * b_, CK):
                            kvchunks.append((b_, c_, lo, min(lo + CK, 4 * b_)))
                kvstate = {"issued": 0, "released": 0}

                def kv_issue():
                    lim = min(len(kvchunks), kvstate["released"] + NCH)
                    while kvstate["issued"] < lim:
                        k = kvstate["issued"]
                        b_, c_, lo, hi = kvchunks[k]
                        if b_ > kvstate["maxb"]:
                            break
                        sl = k % NCH
                        op("sp", lambda e: e.dma_start(out=Kc[sl][:, 0:(hi - lo) * 128], in_=kscr[c_, :, lo * 128:hi * 128]),
                           reads=ksb[0:b_], writes=[kvb[sl]], dma=True)
                        op("sp", lambda e: e.dma_start(out=Vc[sl][:, 0:hi - lo, :], in_=vscr[c_, :, lo:hi, :]),
                           reads=vsb[0:b_], writes=[kvb[sl]], dma=True)
                        kvstate["issued"] += 1
                kvstate["maxb"] = 0

                def prologue(bn):
                    hTn, hTnb = hT2[bn % 2], hT2b[bn % 2]
                    tn = bn * 512
                    if first_layer:
                        for i in range(4):
                            for hf in range(2):
                                xi = xr.next()
                                op("pool", lambda e: e.dma_start(out=xin[xi][:], in_=x_d[tn + i * 128:tn + (i + 1) * 128, hf * 512:(hf + 1) * 512]),
                                   writes=[xinb[xi]], dma=True)
                                bk = bank_pro.next()

                                def tr(e):
                                    f = None
                                    for k in range(4):
                                        ins = e.transpose(ps[bk][:, k * 128:(k + 1) * 128], xin[xi][:, k * 128:(k + 1) * 128], ident)
                                        f = ins if f is None else f
                                    return f, ins
                                op("pe", tr, reads=[xinb[xi], cstb], writes=[pb[bk]])
                                op("dve", lambda e: e.tensor_copy(out=hTn[:, hf * 4:(hf + 1) * 4, i * 128:(i + 1) * 128],
                                                                  in_=ps[bk][:, 0:512].rearrange("p (k t) -> p k t", k=4)),
                                   reads=[pb[bk]], writes=hTnb[hf * 4:(hf + 1) * 4])
                            yield
                    else:
                        op("pool", lambda e: e.dma_start(out=hTn[:], in_=hscr_v[:, :, tn:tn + 512]),
                           reads=[hsb[bn]], writes=hTnb, dma=True)
                        yield
                    bkn = bank_pro.next()
                    fm_norm(hTn, hTnb, hb1, hb1b, vec, vecb, 0, 512, pa[0], pab[0], pa[1], pab[1], bkn, part="A")
                    yield
                    fm_norm(hTn, hTnb, hb1, hb1b, vec, vecb, 0, 512, pa[0], pab[0], pa[1], pab[1], bkn, part="B")
                    yield

                for _ in prologue(0):
                    pass
                for b in range(NB):
                    t0 = b * 512
                    wprefetch()
                    hT, hTb = hT2[b % 2], hT2b[b % 2]
                    hb, hbb = hb1, hb1b

                    wq_, wqb_, _ = wnext("inQ")
                    wk_, wkb_, _ = wnext("inK")
                    ysl = [(ysq[i], ysqb[i]) for i in range(4)]
                    stage_lists = []
                    for it in range(8):
                        which, c = ("inQ", it) if it < 4 else ("inK", it - 4)
                        wv_, wb_ = (wq_, wqb_) if it < 4 else (wk_, wkb_)
                        bk = it % 4
                        if which == "inQ":
                            dsts, gcol = [(0, 64, qT[0:64, 2 * c, :], qTb[2 * c]), (64, 128, qT[64:128, 2 * c + 1, :], qTb[2 * c + 1])], 32
                        else:
                            dsts, gcol = [(0, 128, kcur[:, c, :], kcb[c])], 33

                        def proj(bk=bk, wv_=wv_, wb_=wb_, c=c):
                            def mmq(e):
                                f = None
                                for kc in range(8):
                                    ins = e.matmul(ps[bk][:, :], wv_[:, kc, c * 128:(c + 1) * 128], hb[:, kc, :],
                                                   start=(kc == 0), stop=(kc == 7))
                                    f = ins if f is None else f
                                return f, ins
                            op("pe", mmq, reads=hbb + [wb_], writes=[pb[bk]])
                        ys, ysb_ = ysl[it % 4]
                        stage_lists.append([proj] + norm_stages(bk, 4 + (it % 2), bd64_b, 1.0 / 64, 512, vec, vecb, gcol, dsts,
                                                                ys, ysb_, ft[it % 4], ftb[it % 4]))
                    pipeline(stage_lists, [0, 1, 1, 2, 2, 3])
                    wprefetch()
                    if b < NB - 1:
                        op("pool", lambda e: e.dma_start(out=kscr_v[:, :, t0:t0 + 512], in_=kcur[:]),
                           reads=kcb, writes=[ksb[b]], dma=True)

                    wv_, wb_, _ = wnext("inV")
                    BF = 4
                    for i in range(4):
                        bk = i

                        def mmv2(e):
                            f = None
                            for kc in range(8):
                                ins = e.matmul(ps[bk][:, :], hb[:, kc, i * 128:(i + 1) * 128], wv_[:, kc, :],
                                               start=(kc == 0), stop=(kc == 7))
                                f = ins if f is None else f
                            return f, ins
                        op("pe", mmv2, reads=hbb + [wb_], writes=[pb[bk]])

                    def mmf(e):
                        f = None
                        for i in range(4):
                            for kc in range(8):
                                ins = e.matmul(ps[BF][:, i * 8:(i + 1) * 8], hb[:, kc, i * 128:(i + 1) * 128], wf[:, kc, :],
                                               start=(kc == 0), stop=(kc == 7))
                                f = ins if f is None else f
                        return f, ins
                    op("pe", mmf, reads=hbb + [wfb], writes=[pb[BF]])
                    for i in range(4):
                        pv4 = ps[i][:, 0:512].rearrange("p (c two d) -> p c two d", c=4, two=2)
                        op("dve", lambda e: e.tensor_copy(out=vcur[:, i, :, 0:64], in_=pv4[:, :, 0, :]),
                           reads=[pb[i]], writes=[vcb[i]])
                        op("dve", lambda e: e.tensor_copy(out=vcur[:, i, :, 128:192], in_=pv4[:, :, 1, :]),
                           reads=[pb[i]], writes=[vcb[i]])
                    op("dve", lambda e: e.tensor_tensor(out=fl[:, 0, :], in0=ps[BF][:, 0:32], in1=vec[:, 40:72], op=ALU.add),
                       reads=[pb[BF], vecb], writes=[flb[0]])
                    op("act", lambda e: e.activation(out=fl[:, 1, :], in_=fl[:, 0, :], func=AF.Exp, scale=-1.0),
                       reads=[flb[0]], writes=[flb[1]])
                    op("act", lambda e: e.activation(out=fl[:, 2, :], in_=fl[:, 1, :], func=AF.Ln, bias=epsT[:, 1:2], scale=1.0),
                       reads=[flb[1], epsb], writes=[flb[2]])
                    BC = 5

                    def mmc(e):
                        f = None
                        for i in range(4):
                            ins = e.matmul(ps[BC][:, i * 8:(i + 1) * 8], ident, cend[:, 4 * b, :], start=True, stop=False)
                            f = ins if f is None else f
                            for i2 in range(i):
                                e.matmul(ps[BC][:, i * 8:(i + 1) * 8], ones_f, fl[:, 2, i2 * 8:(i2 + 1) * 8], start=False, stop=False)
                            e.matmul(ps[BC][:, i * 8:(i + 1) * 8], triu_f, fl[:, 2, i * 8:(i + 1) * 8], start=False, stop=True)
                            e.matmul(ps[BC][:, 32 + i * 8:32 + (i + 1) * 8], ident, cend[:, 4 * b, :], start=True, stop=False)
                            for i2 in range(i + 1):
                                ins = e.matmul(ps[BC][:, 32 + i * 8:32 + (i + 1) * 8], ones_f, fl[:, 2, i2 * 8:(i2 + 1) * 8],
                                               start=False, stop=(i2 == i))
                        return f, ins
                    op("pe", mmc, reads=[flb[2], cstb, cendb], writes=[pb[BC]])
                    op("dve", lambda e: e.tensor_copy(out=negc[:, 4 * b:4 * b + 4, :], in_=ps[BC][:, 0:32].rearrange("p (t h) -> p t h", t=4)),
                       reads=[pb[BC]], writes=[negcb])
                    op("dve", lambda e: e.tensor_copy(out=cend[:, 4 * b + 1:4 * b + 5, :], in_=ps[BC][:, 32:64].rearrange("p (t h) -> p t h", t=4)),
                       reads=[pb[BC]], writes=[cendb])
                    for h in range(8):
                        op("dve", lambda e: e.tensor_scalar(out=biasb[h][:, 0:4 * b + 4], in0=negc[:, 0:4 * b + 4, h],
                                                            scalar1=cend[:, 4 * b + 2, h:h + 1], scalar2=None, op0=ALU.subtract),
                           reads=[negcb, cendb], writes=[biasbb[h]])
                    if b < NB - 1:
                        for c in range(4):
                            op("pool", lambda e: e.dma_start(out=vscr[c, :, 4 * b:4 * b + 4, :],
                                                             in_=vcur[:, :, c, :]),
                               reads=vcb, writes=[vsb[b]], dma=True)
                    wprefetch()

                    wv_, wb_, _ = wnext("inP")
                    for g in range(4):
                        bk = bank_proj.next()

                        def mmp(e):
                            f = None
                            for kc in range(8):
                                ins = e.matmul(ps[bk][:, :], wv_[:, kc, g * 128:(g + 1) * 128], hb[:, kc, :],
                                               start=(kc == 0), stop=(kc == 7))
                                f = ins if f is None else f
                            return f, ins
                        op("pe", mmp, reads=hbb + [wb_], writes=[pb[bk]])
                        op("act", lambda e: e.activation(out=pin[:, g, 16:528], in_=ps[bk][:, :], func=AF.Copy),
                           reads=[pb[bk]], writes=[pinb[g]])
                    wprefetch()

                    for g in range(4):
                        w = 2 << g
                        cur, curb = pin[:, g, :], pinb[g]
                        lo, off = 0, 1
                        for st in range(g + 1):
                            nx, nxb = ptmp[st % 2], ptmpb[st % 2]
                            lo2 = lo + off
                            op("pool", lambda e: e.tensor_tensor(out=nx[:, lo2:528], in0=cur[:, lo2:528], in1=cur[:, lo:528 - off], op=ALU.add),
                               reads=[curb], writes=[nxb])
                            cur, curb = nx[:, :], nxb
                            lo, off = lo2, off * 2
                        sc_t, sc_b = ptmp[(g + 1) % 2], ptmpb[(g + 1) % 2]
                        op("pool", lambda e: e.tensor_scalar(out=sc_t[:, 16:528], in0=cur[:, 16:528], scalar1=1.0 / w, scalar2=None, op0=ALU.mult),
                           reads=[curb], writes=[sc_b])
                        op("pool", lambda e: e.tensor_tensor(out=mixed[:, g, :], in0=sc_t[:, 16:528], in1=pin[:, g, 16:528], op=ALU.subtract),
                           reads=[sc_b, pinb[g]], writes=[mixb[g]])
                        if b == 0:
                            oth, othb = ptmp[(g + 1) % 2], ptmpb[(g + 1) % 2]
                            op("pool", lambda e: e.tensor_tensor(out=oth[:, 0:16], in0=cur[:, 16:32], in1=invcnt[:, g * 16:(g + 1) * 16], op=ALU.mult),
                               reads=[curb, cstb], writes=[othb])
                            op("pool", lambda e: e.tensor_tensor(out=mixed[:, g, 0:16], in0=oth[:, 0:16], in1=pin[:, g, 16:32], op=ALU.subtract),
                               reads=[othb, pinb[g]], writes=[mixb[g]])
                    op("pool", lambda e: e.tensor_copy(out=pin[:, :, 0:16], in_=pin[:, :, 512:528]), reads=[], writes=pinb)

                    nj = 4 * b + 4
                    items = [(2 * c + hh, j) for c in range(4) for j in range(nj) for hh in range(2)]
                    SKEW = 3
                    st_ = {}
                    obank = {}
                    pend = []
                    kvstate["maxb"] = b
                    kv_issue()

                    def kv_of(c, j):
                        q = j // CK
                        k = kvbase[(b, c)] + q
                        assert k < kvstate["issued"], (b, c, j, k, kvstate)
                        return k % NCH, j - q * CK, k

                    def emit_S(h, j):
                        c = h // 2
                        if j < 4 * b:
                            sl, jj, _ = kv_of(c, j)
                            lhsT = Kc[sl][:, jj * 128:(jj + 1) * 128]
                            kb = kvb[sl]
                            col0 = 0
                        else:
                            i = j - 4 * b
                            lhsT = kcur[:, c, i * 128:(i + 1) * 128]
                            kb = kcb[c]
                            col0 = i * 128
                        bs = bank_S.next()
                        if j >= 4 * b:
                            def mms(e):
                                i1 = e.matmul(ps[bs][:, col0:512], lhsT, qT[:, h, col0:512], start=True, stop=True)
                                i2 = e.matmul(ps[bs][:, col0:col0 + 128], ident_b, mneg_b, start=False, stop=True, skip_group_check=True)
                                return i1, i2
                            op("pe", mms, reads=[kb, qTb[h], cbfb], writes=[pb[bs]])
                        else:
                            op("pe", lambda e: e.matmul(ps[bs][:, col0:512], lhsT, qT[:, h, col0:512], start=True, stop=True),
                               reads=[kb, qTb[h]], writes=[pb[bs]])
                        pt = ptr.next()
                        op("act", lambda e: e.activation(out=PT[pt][:, col0:512], in_=ps[bs][:, col0:512], func=AF.Exp,
                                                         bias=biasb[h][:, j:j + 1], scale=0.125),
                           reads=[pb[bs], biasbb[h]], writes=[PTb[pt]])
                        st_[(h, j)] = (pt, col0)

                    def emit_PV(h, j):
                        c = h // 2
                        pt, col0 = st_.pop((h, j))
                        if j == 0 and h % 2 == 0:
                            bo2 = bank_O2.next()
                            obank[h], obank[h + 1] = bo2
                        bo = obank[h]
                        v0 = 0 if h % 2 == 0 else 64
                        if j < 4 * b:
                            sl, jj, k = kv_of(c, j)
                            lhsT = Vc[sl][:, jj, v0:v0 + 128]
                            vb_ = kvb[sl]
                        else:
                            lhsT = vcur[:, j - 4 * b, c, v0:v0 + 128]
                            vb_ = vcb[j - 4 * b]
                        op("pe", lambda e: e.matmul(ps[bo][:, col0:512], lhsT, PT[pt][:, col0:512], start=(j == 0), stop=(j == nj - 1)),
                           reads=[PTb[pt], vb_], writes=[pb[bo]])
                        if j < 4 * b and h % 2 == 1:
                            _, _, k = kv_of(c, j)
                            lo, hi = kvchunks[k][2], kvchunks[k][3]
                            if j == hi - 1:
                                kvstate["released"] = k + 1
                                kv_issue()
                        if j == nj - 1:
                            finalize(h, bo)

                    def finalize(h, bo):
                        c = h // 2
                        r1, r2 = fzr.next(), fzr.next()
                        if h % 2 == 0:
                            rr, o0 = 64, 0
                        else:
                            rr, o0 = 0, 64
                        op("dve", lambda e: e.reciprocal(out=fz[r1][rr:rr + 1, :], in_=ps[bo][rr:rr + 1, :]),
                           reads=[pb[bo]], writes=[fzb[r1]])

                        def fin_pe():
                            if h % 2 == 0:
                                op("pe", lambda e: e.matmul(ps[BANK_RB][0:64, :], ones_f[64:65, 0:64], fz[r1][64:65, :], start=True, stop=True),
                                   reads=[fzb[r1], cstb], writes=[pb[BANK_RB]])
                            else:
                                op("pe", lambda e: e.matmul(ps[BANK_RB][:, :], ones_f[0:1, :], fz[r1][0:1, :], start=True, stop=True),
                                   reads=[fzb[r1], cstb], writes=[pb[BANK_RB]])
                            op("dve", lambda e: e.tensor_copy(out=fz[r2][o0:o0 + 64, :], in_=ps[BANK_RB][o0:o0 + 64, :]),
                               reads=[pb[BANK_RB]], writes=[fzb[r2]])
                            op("dve", lambda e: e.tensor_tensor(out=foxT[o0:o0 + 64, c, :], in0=ps[bo][o0:o0 + 64, :], in1=fz[r2][o0:o0 + 64, :], op=ALU.mult),
                               reads=[pb[bo], fzb[r2]], writes=[foxb[h]])
                        pend.append([6, fin_pe])

                    def tick():
                        for p in list(pend):
                            p[0] -= 1
                            if p[0] <= 0:
                                p[1]()
                                pend.remove(p)

                    for idx in range(len(items) + SKEW):
                        if idx < len(items):
                            emit_S(*items[idx])
                        if idx >= SKEW:
                            emit_PV(*items[idx - SKEW])
                        tick()
                    while pend:
                        tick()
                    kvstate["maxb"] = b + 1
                    kv_issue()

                    for g in range(4):
                        bk = bank_proj.next()
                        op("pe", lambda e: e.matmul(ps[bk][:, :], wpl[:, g, :], mixed[:, g, :], start=True, stop=True),
                           reads=[mixb[g], wplb], writes=[pb[bk]])
                        op("act", lambda e: e.activation(out=poolT[:, g, :], in_=ps[bk][:, :], func=AF.Copy, scale=vec[:, 36 + g:37 + g]),
                           reads=[pb[bk], vecb], writes=[poolb[g]])

                    for hf in range(2):
                        wW, wWb, _ = wnext("out")
                        for oc in range(4):
                            kc = hf * 4 + oc
                            bk = bank_proj.next()

                            def mmo(e):
                                f = None
                                for c in range(4):
                                    ins = e.matmul(ps[bk][:, :], wW[:, c, oc * 128:(oc + 1) * 128], foxT[:, c, :],
                                                   start=(c == 0), stop=False)
                                    f = ins if f is None else f
                                for g in range(4):
                                    ins = e.matmul(ps[bk][:, :], wW[:, 4 + g, oc * 128:(oc + 1) * 128], poolT[:, g, :],
                                                   start=False, stop=(g == 3))
                                return f, ins
                            op("pe", mmo, reads=foxb + poolb + [wWb], writes=[pb[bk]])
                            op("dve", lambda e: e.tensor_tensor(out=hT[:, kc, :], in0=ps[bk][:, :], in1=hT[:, kc, :], op=ALU.add),
                               reads=[pb[bk], hTb[kc]], writes=[hTb[kc]])
                        wprefetch()

                    a0, a1 = ftr.next(), ftr.next()
                    hb, hbb = hb2, hb2b
                    fm_norm(hT, hTb, hb, hbb, vec, vecb, 8, 512, ft[a0], ftb[a0], ft[a1], ftb[a1], bank_stat.next())
                    gen = prologue(b + 1) if b + 1 < NB else iter(())
                    wQ, wQb, _ = wnext("mq")
                    moT = foxT
                    stage_lists = []
                    for hm in range(4):
                        bk = hm
                        p0, p1 = 2 * (hm % 2), 2 * (hm % 2) + 1
                        r1 = 4 + (hm % 2)
                        bst = 4 + (hm % 2)

                        def proj(bk=bk, hm=hm):
                            def mmq2(e):
                                f = None
                                for kc in range(8):
                                    ins = e.matmul(ps[bk][:, :], wQ[:, kc, hm * 128:(hm + 1) * 128], hb[:, kc, :],
                                                   start=(kc == 0), stop=(kc == 7))
                                    f = ins if f is None else f
                                return f, ins
                            op("pe", mmq2, reads=hbb + [wQb], writes=[pb[bk]])

                        def scores(hm=hm):
                            for mt in range(2):
                                op("pe", lambda e: e.matmul(ps[6 + mt][:, :], mkT[:, hm, mt * 128:(mt + 1) * 128], mqT[hm][:, :], start=True, stop=True),
                                   reads=[mkTb, mqTb[hm]], writes=[pb[6 + mt]])

                        def exps(hm=hm, p0=p0):
                            for mt in range(2):
                                op("act", lambda e: e.activation(out=PT[p0 + mt][:, :], in_=ps[6 + mt][:, :], func=AF.Exp, scale=128.0 ** -0.5),
                                   reads=[pb[6 + mt]], writes=[PTb[p0 + mt]])

                        def pv(hm=hm, bk=bk, p0=p0, p1=p1, bst=bst):
                            def mmpv(e):
                                i1 = e.matmul(ps[bk][:, :], mv[:, 0, hm * 128:(hm + 1) * 128], PT[p0][:, :], start=True, stop=False)
                                e.matmul(ps[bk][:, :], mv[:, 1, hm * 128:(hm + 1) * 128], PT[p1][:, :], start=False, stop=True)
                                e.matmul(ps[bst][:, :], ones_b, PT[p0][:, :], start=True, stop=False)
                                i2 = e.matmul(ps[bst][:, :], ones_b, PT[p1][:, :], start=False, stop=True)
                                return i1, i2
                            op("pe", mmpv, reads=[mvb, cbfb, PTb[p0], PTb[p1]], writes=[pb[bk], pb[bst]])

                        def fin(hm=hm, bk=bk, r1=r1, bst=bst):
                            op("act", lambda e: e.activation(out=ft[r1][:, :], in_=ps[bst][:, :], func=AF.Ln), reads=[pb[bst]], writes=[ftb[r1]])
                            op("act", lambda e: e.activation(out=ft[r1][:, :], in_=ft[r1][:, :], func=AF.Exp, scale=-1.0), reads=[ftb[r1]], writes=[ftb[r1]])
                            op("dve", lambda e: e.tensor_tensor(out=moT[:, hm, :], in0=ps[bk][:, :], in1=ft[r1][:, :], op=ALU.mult),
                               reads=[pb[bk], ftb[r1]], writes=[foxb[2 * hm], foxb[2 * hm + 1]])
                        stage_lists.append([proj] + norm_stages(bk, bst, ones_b, 1.0 / 128, 512, vec, vecb, 34,
                                                                [(0, 128, mqT[hm][:, :], mqTb[hm])], ysq[hm], ysqb[hm], ft[hm], ftb[hm])
                                           + [scores, exps, pv, fin])
                    pipeline(stage_lists, [0, 1, 1, 2, 2, 3, 3, 3, 4, 4], filler=lambda: next(gen, None))
                    for _ in gen:
                        pass
                    wprefetch()
                    wO, wOb, _ = wnext("mo")
                    for oc in range(8):
                        bk = bank_proj.next()

                        def mmo2(e):
                            f = None
                            for hm in range(4):
                                ins = e.matmul(ps[bk][:, :], wO[:, hm, oc * 128:(oc + 1) * 128], moT[:, hm, :],
                                               start=(hm == 0), stop=(hm == 3))
                                f = ins if f is None else f
                            return f, ins
                        op("pe", mmo2, reads=foxb + [wOb], writes=[pb[bk]])
                        op("dve", lambda e: e.tensor_tensor(out=hT[:, oc, :], in0=ps[bk][:, :], in1=hT[:, oc, :], op=ALU.add),
                           reads=[pb[bk], hTb[oc]], writes=[hTb[oc]])
                    wprefetch()
                    op("pool", lambda e: e.dma_start(out=hscr_v[:, :, t0:t0 + 512], in_=hT[:]),
                       reads=hTb, writes=[hsb[b]], dma=True)
                sc.barrier()

            with ExitStack() as fx:
                vec = sb("vecF", [128, NVEC], F32, fx)
                vecb = Buf()
                op("sp", lambda e: e.dma_start(out=vec[:], in_=vecs_d[l, :, :]), writes=[vecb], dma=True)
                wgu = sb("wgu", [128, 8, 2 * DFF], BF16, fx)
                wgub = [Buf() for _ in range(8)]
                wdn = sb("wdn", [128, NFC, 1024], BF16, fx)
                wdnb = [Buf() for _ in range(NFC)]
                for kc in range(8):
                    op("sp", lambda e: e.dma_start(out=wgu[:, kc, :], in_=s_gu[l, :, kc, :]), writes=[wgub[kc]], dma=True)
                for f0 in range(0, NFC, 6):
                    f1 = min(NFC, f0 + 6)
                    op("sp", lambda e: e.dma_start(out=wdn[:, f0:f1, :], in_=s_dn[l, :, f0:f1, :]), writes=wdnb[f0:f1], dma=True)
                hT2 = [sb("hTF%d" % i, [128, 8, 512], F32, fx) for i in range(2)]
                hT2b = [[Buf() for _ in range(8)] for _ in range(2)]
                hb = sb("hbF", [128, 8, 512], BF16, fx)
                hbb = [Buf() for _ in range(8)]
                actT = sb("actT", [128, NFC, 512], BF16, fx)
                actb = [Buf() for _ in range(NFC)]
                sg = [sb("sg%d" % i, [128, 512], F32, fx) for i in range(2)]
                sgb = [Buf() for _ in range(2)]
                fa = [sb("fa%d" % i, [128, 512], F32, fx) for i in range(2)]
                fab = [Buf() for _ in range(2)]
                ot, otb = sg, sgb
                otr = RR(range(2))
                bank_gu = RR([0, 1, 2, 3])
                bank_dn = RR([4, 5])
                bank_st = RR([6, 7])

                def load_h(b):
                    op("pool", lambda e: e.dma_start(out=hT2[b % 2][:], in_=hscr_v[:, :, b * 512:(b + 1) * 512]),
                       reads=[hsb[b]], writes=hT2b[b % 2], dma=True)

                def norm3(b, part):
                    fm_norm(hT2[b % 2], hT2b[b % 2], hb, hbb, vec, vecb, 16, 512, fa[0], fab[0], fa[1], fab[1], 6 + (b % 2), part=part)

                load_h(0)
                norm3(0, None)
                for b in range(NB):
                    t0 = b * 512
                    hT, hTb = hT2[b % 2], hT2b[b % 2]
                    if b + 1 < NB:
                        load_h(b + 1)
                    for fc in range(NFC):
                        bg, bu = bank_gu.next(), bank_gu.next()

                        def mmg(e):
                            f = None
                            for kc in range(8):
                                ins = e.matmul(ps[bg][:, :], wgu[:, kc, fc * 128:(fc + 1) * 128], hb[:, kc, :],
                                               start=(kc == 0), stop=(kc == 7))
                                f = ins if f is None else f
                            return f, ins
                        op("pe", mmg, reads=hbb + wgub, writes=[pb[bg]])

                        def mmu(e):
                            f = None
                            for kc in range(8):
                                ins = e.matmul(ps[bu][:, :], wgu[:, kc, DFF + fc * 128:DFF + (fc + 1) * 128], hb[:, kc, :],
                                               start=(kc == 0), stop=(kc == 7))
                                f = ins if f is None else f
                            return f, ins
                        op("pe", mmu, reads=hbb + wgub, writes=[pb[bu]])
                        si = fc % 2
                        op("act", lambda e: e.activation(out=sg[si][:, :], in_=ps[bg][:, :], func=AF.Silu),
                           reads=[pb[bg]], writes=[sgb[si]])
                        op("dve", lambda e: e.tensor_tensor(out=actT[:, fc, :], in0=ps[bu][:, :], in1=sg[si][:, :], op=ALU.mult),
                           reads=[pb[bu], sgb[si]], writes=[actb[fc]])
                    if b + 1 < NB:
                        norm3(b + 1, "A")
                    for oc in range(8):
                        bk = bank_dn.next()

                        def mmd(e):
                            f = None
                            for fc in range(NFC):
                                ins = e.matmul(ps[bk][:, :], wdn[:, fc, oc * 128:(oc + 1) * 128], actT[:, fc, :],
                                               start=(fc == 0), stop=(fc == NFC - 1))
                                f = ins if f is None else f
                            return f, ins
                        op("pe", mmd, reads=actb + wdnb, writes=[pb[bk]])
                        op("dve", lambda e: e.tensor_tensor(out=hT[:, oc, :], in0=ps[bk][:, :], in1=hT[:, oc, :], op=ALU.add),
                           reads=[pb[bk], hTb[oc]], writes=[hTb[oc]])
                        if oc == 3 and b + 1 < NB:
                            norm3(b + 1, "B")
                    if last_layer:
                        for i in range(4):
                            for hf in range(2):
                                bk = bank_st.next()

                                def trb(e):
                                    f = None
                                    for k in range(4):
                                        ins = e.transpose(ps[bk][:, k * 128:(k + 1) * 128], hT[:, hf * 4 + k, i * 128:(i + 1) * 128], ident)
                                        f = ins if f is None else f
                                    return f, ins
                                op("pe", trb, reads=hTb[hf * 4:(hf + 1) * 4] + [cstb], writes=[pb[bk]])
                                oi = otr.next()
                                op("act", lambda e: e.activation(out=ot[oi][:, :], in_=ps[bk][:, :], func=AF.Copy),
                                   reads=[pb[bk]], writes=[otb[oi]])
                                op("pool", lambda e: e.dma_start(out=out_d[t0 + i * 128:t0 + (i + 1) * 128, hf * 512:(hf + 1) * 512], in_=ot[oi][:, :]),
                                   reads=[otb[oi]], dma=True)
                    else:
                        op("pool", lambda e: e.dma_start(out=hscr_v[:, :, t0:t0 + 512], in_=hT[:]),
                           reads=hTb, writes=[hsb[b]], dma=True)
                sc.barrier()
        sc.finish()
    return nc


def make_consts():
    c = np.zeros((128, 896), np.float32)
    c[:, 0:128] = np.eye(128, dtype=np.float32)
    s = np.arange(128)[:, None]
    t = np.arange(128)[None, :]
    c[:, 128:256] = (s <= t).astype(np.float32)
    c[:, 256:384] = ((s // 64) == (t // 64)).astype(np.float32)
    c[:, 384:512] = 1.0
    for g, w in enumerate((2, 4, 8, 16)):
        c[:, 512 + g * 16:512 + (g + 1) * 16] = (1.0 / np.minimum(np.arange(16) + 1, w)).astype(np.float32)[None, :]
    c[:, 640:768] = np.where(s > t, -30000.0, 0.0).astype(np.float32)
    c[:, 768:896] = np.eye(128, dtype=np.float32)
    return c


def make_vecs(inp, NL):
    v = np.zeros((NL, 128, NVEC), np.float32)
    p = np.arange(128)
    for l in range(NL):
        for k, name in enumerate(("g_mix", "g_mem_q", "g_ffn", "g_mem_kv")):
            v[l, :, 8 * k:8 * k + 8] = np.asarray(inp[name][l]).reshape(8, 128).T
        v[l, :, 32] = np.asarray(inp["g_q_fox"][l])[p % 64]
        v[l, :, 33] = np.asarray(inp["g_k_fox"][l])[p % 64]
        v[l, :, 34] = np.asarray(inp["g_q_mem"][l])
        v[l, :, 35] = np.asarray(inp["g_k_mem"][l])
        v[l, :, 36:40] = np.asarray(inp["pool_scale"][l]).reshape(4, 128).T
        v[l, :, 40:72] = np.tile(np.asarray(inp["b_forget"][l]), 4)[None, :]
    return v


_NC_CACHE = {}


def run(inp, S, NL, ncores, trace=False):
    key = (S, NL)
    if key not in _NC_CACHE:
        _NC_CACHE[key] = build(S, NL)
    nc = _NC_CACHE[key]
    consts = make_consts()
    vecs = make_vecs(inp, NL)
    shared = {
        "w_in": np.ascontiguousarray(inp["w_in"], np.float32),
        "w_pool": np.ascontiguousarray(inp["w_pool"], np.float32),
        "w_out": np.ascontiguousarray(inp["w_out"], np.float32),
        "w_mem_q": np.ascontiguousarray(inp["w_mem_q"], np.float32),
        "w_mem_kv": np.ascontiguousarray(inp["w_mem_kv"], np.float32),
        "w_mem_out": np.ascontiguousarray(inp["w_mem_out"], np.float32),
        "w_gate_up": np.ascontiguousarray(inp["w_gate_up"], np.float32),
        "w_down": np.ascontiguousarray(inp["w_down"], np.float32),
        "vecs": vecs,
        "consts": consts,
    }
    x = np.asarray(inp["x"], np.float32)
    mem = np.asarray(inp["mem"], np.float32)
    in_maps = []
    for i in range(ncores):
        m = dict(shared)
        m["x"] = np.ascontiguousarray(x[i])
        m["mem"] = np.ascontiguousarray(mem[i])
        in_maps.append(m)
    res = run_bass_kernel_spmd(nc, in_maps, core_ids=list(range(ncores)), trace=trace)
    out = np.stack([np.asarray(r["out"], np.float32) for r in res.results], axis=0)
    return out, res


def kernel(**inputs):
    x = np.asarray(inputs["x"])
    B, S, _ = x.shape
    NL = int(np.asarray(inputs["w_in"]).shape[0])
    out, _ = run(inputs, S, NL, B)
    return out.astype(np.float32)
```

```python
import numpy as np
from contextlib import ExitStack
import concourse.bass as bass
import concourse.mybir as mybir
from concourse.bass_utils import run_bass_kernel_spmd

F32 = mybir.dt.float32
BF16 = mybir.dt.bfloat16
ALU = mybir.AluOpType
AF = mybir.ActivationFunctionType

D = 1024
EPS = 1e-6
NVEC = 72
DFF = 2816
NFC = 22
MEM = 256


class Tok:
    __slots__ = ("s", "v", "clk")

    def __init__(self, s, v, clk):
        self.s = s
        self.v = v
        self.clk = clk


class Buf:
    __slots__ = ("w", "rd", "excl")

    def __init__(self, excl=False):
        self.w = None
        self.rd = {}
        self.excl = excl


class RR:
    def __init__(self, items):
        self.items = list(items)
        self.i = 0

    def next(self):
        r = self.items[self.i % len(self.items)]
        self.i += 1
        return r


class Sched:
    COMPUTE = ("pe", "act", "dve", "pool")

    def __init__(self, nc, es, nslots=16):
        self.nc = nc
        self.h = {"pe": nc.tensor, "act": nc.scalar, "dve": nc.vector, "pool": nc.gpsimd, "sp": nc.sync}
        self.sems = []
        self.own = {}
        for e in self.COMPUTE:
            self.own[e] = len(self.sems)
            self.sems.append(es.enter_context(nc.semaphore("sem_" + e)))
        self.slots = {}
        for q in ("sp", "pool"):
            self.slots[q] = []
            for i in range(nslots):
                self.slots[q].append(len(self.sems))
                self.sems.append(es.enter_context(nc.semaphore("dma_%s_%d" % (q, i))))
        self.n = len(self.sems)
        self.cnt = [0] * self.n
        self.clk = {e: [0] * self.n for e in self.h}
        self.dman = {"sp": 0, "pool": 0}
        self.nops = 0

    def op(self, eng, fn, reads=(), writes=(), dma=False):
        clk = self.clk[eng]
        own = self.own.get(eng)
        deps = []
        wr = list(writes)
        for b in reads:
            if b.excl:
                wr.append(b)
            elif b.w is not None:
                deps.append((b.w, 0))
        for b in wr:
            if b.w is not None:
                deps.append((b.w, 1))
            for t in b.rd.values():
                deps.append((t, 2))
        waits = {}
        for t, kind in deps:
            if (not dma) and t.s == own:
                if kind != 0 or eng == "pe":
                    continue
            if clk[t.s] >= t.v:
                continue
            waits[t.s] = max(waits.get(t.s, 0), t.v)
            tc = t.clk
            for i in range(self.n):
                if tc[i] > clk[i]:
                    clk[i] = tc[i]
            if clk[t.s] < t.v:
                clk[t.s] = t.v
        if dma:
            k = self.dman[eng]
            self.dman[eng] += 1
            sl = self.slots[eng][k % len(self.slots[eng])]
            prev = self.cnt[sl]
            if clk[sl] < prev:
                waits[sl] = max(waits.get(sl, 0), prev)
                clk[sl] = prev
            self.cnt[sl] += 16
            tok = Tok(sl, self.cnt[sl], list(clk))
            inc = (sl, 16)
        else:
            self.cnt[own] += 1
            snap = list(clk)
            snap[own] = self.cnt[own]
            tok = Tok(own, self.cnt[own], snap)
            inc = (own, 1)
        for b in reads:
            if not b.excl:
                b.rd[tok.s] = tok
        for b in wr:
            b.w = tok
            b.rd = {}
        e = self.h[eng]
        wl = sorted(waits.items())
        attach = None
        if (not dma) and wl:
            attach = wl.pop()
        for s, v in wl:
            e.wait_ge(self.sems[s], v)
        r = fn(e)
        first, last = r if isinstance(r, tuple) else (r, r)
        if attach is not None:
            first._wait_ge(self.sems[attach[0]], attach[1])
        last.then_inc(self.sems[inc[0]], inc[1])
        self.nops += 1
        return tok

    def barrier(self):
        for eng, e in self.h.items():
            clk = self.clk[eng]
            for s in range(self.n):
                if s == self.own.get(eng):
                    continue
                if clk[s] < self.cnt[s]:
                    e.wait_ge(self.sems[s], self.cnt[s])
                    clk[s] = self.cnt[s]
        for eng in self.h:
            own = self.own.get(eng)
            for s in range(self.n):
                if s != own:
                    self.clk[eng][s] = self.cnt[s]

    def finish(self):
        e = self.h["sp"]
        clk = self.clk["sp"]
        for s in range(self.n):
            if clk[s] < self.cnt[s]:
                e.wait_ge(self.sems[s], self.cnt[s])
                clk[s] = self.cnt[s]


def build(S, NL):
    NB = S // 512
    NT = S // 128
    nc = bass.Bass("TRN2", target_bir_lowering=False)

    def din(name, shape, dt=F32):
        return nc.dram_tensor(name, list(shape), dt, kind="ExternalInput").ap()

    def dscr(name, shape, dt):
        return nc.dram_tensor(name, list(shape), dt, kind="Internal").ap()

    x_d = din("x", [S, D])
    mem_d = din("mem", [MEM, D])
    w_in_d = din("w_in", [NL, D, 2056])
    w_pool_d = din("w_pool", [NL, 4, 128, 128])
    w_out_d = din("w_out", [NL, D, D])
    w_mq_d = din("w_mem_q", [NL, D, 512])
    w_mkv_d = din("w_mem_kv", [NL, D, 1024])
    w_mo_d = din("w_mem_out", [NL, 512, D])
    w_gu_d = din("w_gate_up", [NL, D, 2 * DFF])
    w_dn_d = din("w_down", [NL, DFF, D])
    vecs_d = din("vecs", [NL, 128, NVEC])
    consts_d = din("consts", [128, 896])
    out_d = nc.dram_tensor("out", [S, D], F32, kind="ExternalOutput").ap()

    s_in = dscr("s_in", [NL, 128, 8, 2056], BF16)
    s_pool = dscr("s_pool", [NL, 128, 4, 128], BF16)
    s_out = dscr("s_out", [NL, 128, 8, 1024], BF16)
    s_mq = dscr("s_mq", [NL, 128, 8, 512], BF16)
    s_mkv = dscr("s_mkv", [NL, 128, 8, 1024], BF16)
    s_mo = dscr("s_mo", [NL, 128, 4, 1024], BF16)
    s_gu = dscr("s_gu", [NL, 128, 8, 2 * DFF], BF16)
    s_dn = dscr("s_dn", [NL, 128, NFC, 1024], BF16)
    hscr = dscr("hscr", [8, 128, S], F32)
    kscr = dscr("kscr", [4, 128, S], BF16)
    vscr = dscr("vscr", [4, 128, NT, 192], BF16)
    hscr_v = hscr.rearrange("k p s -> p k s")
    kscr_v = kscr.rearrange("c p s -> p c s")

    with ExitStack() as es:
        sc = Sched(nc, es)
        op = sc.op

        uniq = [0]

        def sb(name, shape, dt, stack=es):
            uniq[0] += 1
            return stack.enter_context(nc.sbuf_tensor("%s_%d" % (name, uniq[0]), list(shape), dt))

        ps = [es.enter_context(nc.psum_tensor("ps%d" % i, [128, 512], F32)) for i in range(8)]
        pb = [Buf(excl=True) for _ in range(8)]

        cst = sb("cst", [128, 896], F32)
        cstb = Buf()
        ident = cst[:, 0:128]
        triu_f = cst[:, 128:256]
        ones_f = cst[:, 384:512]
        invcnt = cst[:, 512:576]
        cbf = sb("cbf", [128, 384], BF16)
        cbfb = Buf()
        triu_b = cbf[:, 0:128]
        bd64_b = cbf[:, 128:256]
        ones_b = cbf[:, 256:384]
        epsT = sb("epsT", [128, 2], F32)
        epsb = Buf()
        op("sp", lambda e: e.dma_start(out=cst[:], in_=consts_d[:, :]), writes=[cstb], dma=True)
        op("dve", lambda e: e.tensor_copy(out=cbf[:], in_=cst[:, 128:512]), reads=[cstb], writes=[cbfb])
        cbf2 = sb("cbf2", [128, 256], BF16)
        mneg_b = cbf2[:, 0:128]
        ident_b = cbf2[:, 128:256]
        op("dve", lambda e: e.tensor_copy(out=cbf2[:], in_=cst[:, 640:896]), reads=[cstb], writes=[cbfb])
        op("dve", lambda e: e.memset(epsT[:, 0:1], EPS), writes=[epsb])
        op("dve", lambda e: e.memset(epsT[:, 1:2], 1.0), writes=[epsb])

        scrb = {}

        def castdma(key, dst, src):
            bb = Buf()
            scrb.setdefault(key, []).append(bb)
            op("pool", lambda e: e.dma_start(out=dst, in_=src), writes=[bb], dma=True)

        for l in range(NL):
            for kc in range(8):
                castdma(("in", l), s_in[l, :, kc, :], w_in_d[l, kc * 128:(kc + 1) * 128, :])
            for g in range(4):
                castdma(("pool", l), s_pool[l, :, g, :], w_pool_d[l, g, :, :])
            for kc in range(8):
                castdma(("mkv", l), s_mkv[l, :, kc, :], w_mkv_d[l, kc * 128:(kc + 1) * 128, :])
            for kc in range(8):
                castdma(("out", l), s_out[l, :, kc, :], w_out_d[l, kc * 128:(kc + 1) * 128, :])
            for kc in range(8):
                castdma(("mq", l), s_mq[l, :, kc, :], w_mq_d[l, kc * 128:(kc + 1) * 128, :])
            for hm in range(4):
                castdma(("mo", l), s_mo[l, :, hm, :], w_mo_d[l, hm * 128:(hm + 1) * 128, :])
            for kc in range(8):
                castdma(("gu", l), s_gu[l, :, kc, :], w_gu_d[l, kc * 128:(kc + 1) * 128, :])
            for fc in range(NFC):
                castdma(("dn", l), s_dn[l, :, fc, :], w_dn_d[l, fc * 128:(fc + 1) * 128, :])

        hsb = [Buf() for _ in range(NB)]

        def fm_norm(hT, hTb, hb, hbb, vec, vecb, gcol, n, A0, A0b, A1, A1b, bank, part=None):
            if part in (None, "A"):
                for kc in range(8):
                    op("act", lambda e: e.activation(out=hb[:, kc, 0:n], in_=hT[:, kc, 0:n], func=AF.Square),
                       reads=[hTb[kc]], writes=[hbb[kc]])
            if part == "A":
                return

            def mm(e):
                f = None
                for kc in range(8):
                    i = e.matmul(ps[bank][:, 0:n], ones_b, hb[:, kc, 0:n], start=(kc == 0), stop=(kc == 7))
                    f = i if f is None else f
                return f, i
            op("pe", mm, reads=hbb + [cbfb], writes=[pb[bank]])
            op("act", lambda e: e.activation(out=A0[:, 0:n], in_=ps[bank][:, 0:n], func=AF.Ln,
                                             bias=epsT[:, 0:1], scale=1.0 / D),
               reads=[epsb, pb[bank]], writes=[A0b])
            op("act", lambda e: e.activation(out=A1[:, 0:n], in_=A0[:, 0:n], func=AF.Exp, scale=-0.5), reads=[A0b], writes=[A1b])
            for kc in range(8):
                op("dve", lambda e: e.scalar_tensor_tensor(out=hb[:, kc, 0:n], in0=hT[:, kc, 0:n],
                                                           scalar=vec[:, gcol + kc:gcol + kc + 1], in1=A1[:, 0:n],
                                                           op0=ALU.mult, op1=ALU.mult),
                   reads=[hTb[kc], A1b, vecb], writes=[hbb[kc]])

        def head_norm(bank, bank2, stat_lhsT, inv_n, n, vec, gcol, vecb, dsts, ysq, ysqb, rA, rAb, rB, rBb):
            op("act", lambda e: e.activation(out=ysq[:, 0:n], in_=ps[bank][:, 0:n], func=AF.Square),
               reads=[pb[bank]], writes=[ysqb])
            op("pe", lambda e: e.matmul(ps[bank2][:, 0:n], stat_lhsT, ysq[:, 0:n], start=True, stop=True),
               reads=[ysqb, cbfb], writes=[pb[bank2]])
            op("act", lambda e: e.activation(out=rA[:, 0:n], in_=ps[bank2][:, 0:n], func=AF.Ln,
                                             bias=epsT[:, 0:1], scale=inv_n),
               reads=[epsb, pb[bank2]], writes=[rAb])
            op("act", lambda e: e.activation(out=rB[:, 0:n], in_=rA[:, 0:n], func=AF.Exp, scale=-0.5), reads=[rAb], writes=[rBb])
            for (r0_, r1_, dst, dstb) in dsts:
                op("dve", lambda e: e.scalar_tensor_tensor(out=dst, in0=ps[bank][r0_:r1_, 0:n], scalar=vec[r0_:r1_, gcol:gcol + 1],
                                                           in1=rB[r0_:r1_, 0:n], op0=ALU.mult, op1=ALU.mult),
                   reads=[rBb, pb[bank], vecb], writes=[dstb])

        def head_norm_multi(items, n, vec, vecb):
            for (bank, bank2, lhs, inv_n, gcol, dsts, ysq, ysqb, r, rb) in items:
                op("act", lambda e: e.activation(out=ysq[:, 0:n], in_=ps[bank][:, 0:n], func=AF.Square),
                   reads=[pb[bank]], writes=[ysqb])
            for (bank, bank2, lhs, inv_n, gcol, dsts, ysq, ysqb, r, rb) in items:
                op("pe", lambda e: e.matmul(ps[bank2][:, 0:n], lhs, ysq[:, 0:n], start=True, stop=True),
                   reads=[ysqb, cbfb], writes=[pb[bank2]])
            for (bank, bank2, lhs, inv_n, gcol, dsts, ysq, ysqb, r, rb) in items:
                op("act", lambda e: e.activation(out=r[:, 0:n], in_=ps[bank2][:, 0:n], func=AF.Sqrt,
                                                 bias=epsT[:, 0:1], scale=inv_n),
                   reads=[epsb, pb[bank2]], writes=[rb])
            for (bank, bank2, lhs, inv_n, gcol, dsts, ysq, ysqb, r, rb) in items:
                op("dve", lambda e: e.reciprocal(out=r[:, 0:n], in_=r[:, 0:n]), reads=[rb], writes=[rb])
            for (bank, bank2, lhs, inv_n, gcol, dsts, ysq, ysqb, r, rb) in items:
                for (r0_, r1_, dst, dstb) in dsts:
                    op("dve", lambda e: e.scalar_tensor_tensor(out=dst, in0=ps[bank][r0_:r1_, 0:n], scalar=vec[r0_:r1_, gcol:gcol + 1],
                                                               in1=r[r0_:r1_, 0:n], op0=ALU.mult, op1=ALU.mult),
                       reads=[rb, pb[bank], vecb], writes=[dstb])

        def pipeline(stage_lists, delays, filler=None):
            n = len(stage_lists)
            for t in range(n + max(delays)):
                for k, d in enumerate(delays):
                    i = t - d
                    if 0 <= i < n:
                        stage_lists[i][k]()
                if filler is not None:
                    filler()

        def norm_stages(bank, bank2, lhs, inv_n, n, vec, vecb, gcol, dsts, ysq, ysqb, r, rb):
            def sq():
                op("act", lambda e: e.activation(out=ysq[:, 0:n], in_=ps[bank][:, 0:n], func=AF.Square),
                   reads=[pb[bank]], writes=[ysqb])

            def st():
                op("pe", lambda e: e.matmul(ps[bank2][:, 0:n], lhs, ysq[:, 0:n], start=True, stop=True),
                   reads=[ysqb, cbfb], writes=[pb[bank2]])

            def sr():
                op("act", lambda e: e.activation(out=r[:, 0:n], in_=ps[bank2][:, 0:n], func=AF.Ln,
                                                 bias=epsT[:, 0:1], scale=inv_n),
                   reads=[epsb, pb[bank2]], writes=[rb])

            def rc():
                op("act", lambda e: e.activation(out=r[:, 0:n], in_=r[:, 0:n], func=AF.Exp, scale=-0.5), reads=[rb], writes=[rb])

            def stt():
                for (r0_, r1_, dst, dstb) in dsts:
                    op("dve", lambda e: e.scalar_tensor_tensor(out=dst, in0=ps[bank][r0_:r1_, 0:n], scalar=vec[r0_:r1_, gcol:gcol + 1],
                                                               in1=r[r0_:r1_, 0:n], op0=ALU.mult, op1=ALU.mult),
                       reads=[rb, pb[bank], vecb], writes=[dstb])
            return [sq, st, sr, rc, stt]

        for l in range(NL):
            first_layer = (l == 0)
            last_layer = (l == NL - 1)
            with ExitStack() as mx:
                vec = sb("vec", [128, NVEC], F32, mx)
                vecb = Buf()
                op("sp", lambda e: e.dma_start(out=vec[:], in_=vecs_d[l, :, :]), writes=[vecb], dma=True)
                wf = sb("wf", [128, 8, 8], BF16, mx)
                wfb = Buf()
                op("sp", lambda e: e.dma_start(out=wf[:], in_=s_in[l, :, :, 2048:2056]), reads=scrb[("in", l)], writes=[wfb], dma=True)
                wpl = sb("wpl", [128, 4, 128], BF16, mx)
                wplb = Buf()
                op("sp", lambda e: e.dma_start(out=wpl[:], in_=s_pool[l, :, :, :]), reads=scrb[("pool", l)], writes=[wplb], dma=True)

                hT2 = [sb("hT%d" % i, [128, 8, 512], F32, mx) for i in range(2)]
                hT2b = [[Buf() for _ in range(8)] for _ in range(2)]
                hb1 = sb("hb1", [128, 8, 512], BF16, mx)
                hb1b = [Buf() for _ in range(8)]
                hb2 = sb("hb2", [128, 8, 512], BF16, mx)
                hb2b = [Buf() for _ in range(8)]
                pa = [sb("pa%d" % i, [128, 512], F32, mx) for i in range(2)]
                pab = [Buf() for _ in range(2)]
                bank_pro = RR([6, 7])
                NF = 6
                ft = [sb("ft%d" % i, [128, 512], F32, mx) for i in range(NF)]
                ftb = [Buf() for _ in range(NF)]
                ftr = RR(range(NF))
                qT = sb("qT", [128, 8, 512], BF16, mx)
                qTb = [Buf() for _ in range(8)]
                kcur = sb("kcur", [128, 4, 512], BF16, mx)
                kcb = [Buf() for _ in range(4)]
                vcur = sb("vcur", [128, 4, 4, 192], BF16, mx)
                vcb = [Buf() for _ in range(4)]
                NR = 4
                ring = [sb("ring%d" % i, [128, 4096], BF16, mx) for i in range(NR)]
                ringb = [Buf() for _ in range(NR)]
                NPT = 4
                PT = [sb("PT%d" % i, [128, 512], BF16, mx) for i in range(NPT)]
                PTb = [Buf() for _ in range(NPT)]
                ptr = RR(range(NPT))
                biasb = [sb("biasb%d" % i, [128, NT], F32, mx) for i in range(8)]
                biasbb = [Buf() for _ in range(8)]
                foxT = sb("foxT", [128, 4, 512], BF16, mx)
                foxb = [Buf() for _ in range(8)]
                pin = sb("pin", [128, 4, 528], F32, mx)
                pinb = [Buf() for _ in range(4)]
                ptmp = [sb("ptmp%d" % i, [128, 528], F32, mx) for i in range(2)]
                ptmpb = [Buf() for _ in range(2)]
                mixed = sb("mixed", [128, 4, 512], BF16, mx)
                mixb = [Buf() for _ in range(4)]
                poolT = sb("poolT", [128, 4, 512], BF16, mx)
                poolb = [Buf() for _ in range(4)]
                ysq = [sb("ysq%d" % i, [128, 512], BF16, mx) for i in range(4)]
                ysqb = [Buf() for _ in range(4)]
                mqT = [sb("mqT%d" % i, [128, 512], BF16, mx) for i in range(4)]
                mqTb = [Buf() for _ in range(4)]
                xin = [sb("xin%d" % i, [128, 512], F32, mx) for i in range(3)]
                xinb = [Buf() for _ in range(3)]
                xr = RR(range(3))
                negc = sb("negc", [128, NT, 8], F32, mx)
                negcb = Buf()
                cend = sb("cend", [128, NT + 1, 8], F32, mx)
                cendb = Buf()
                fl = sb("fl", [128, 3, 32], F32, mx)
                flb = [Buf() for _ in range(3)]
                mkT = sb("mkT", [128, 4, MEM], BF16, mx)
                mkTb = Buf()
                mv = sb("mv", [128, 2, 512], BF16, mx)
                mvb = Buf()
                ksb = [Buf() for _ in range(NB)]
                vsb = [Buf() for _ in range(NB)]

                bank_proj = RR([0, 1])
                bank_stat = RR([7, 4])
                bank_S = RR([2, 3, 4])
                bank_O = RR([5, 6])
                bank_O2 = RR([(5, 6), (7, 0)])
                BANK_RB = 1

                op("pool", lambda e: e.memset(qT[:], 0.0), writes=qTb)
                op("pool", lambda e: e.memset(vcur[:, :, :, 64:128], 1.0), writes=vcb)
                op("dve", lambda e: e.memset(cend[:, 0, :], 0.0), writes=[cendb])
                op("pool", lambda e: e.memset(pin[:, :, 0:16], 0.0), writes=pinb)

                chunks = []
                for b in range(NB):
                    chunks.append(("inQ", s_in[l, :, :, 0:512], 128, (8, 512)))
                    chunks.append(("inK", s_in[l, :, :, 512:1024], 128, (8, 512)))
                    chunks.append(("inV", s_in[l, :, :, 1024:1536], 128, (8, 512)))
                    chunks.append(("inP", s_in[l, :, :, 1536:2048], 128, (8, 512)))
                    for hf in range(2):
                        chunks.append(("out", s_out[l, :, :, hf * 512:(hf + 1) * 512], 128, (8, 512)))
                    chunks.append(("mq", s_mq[l, :, :, :], 128, (8, 512)))
                    chunks.append(("mo", s_mo[l, :, :, :], 128, (4, 1024)))
                wstate = {"issued": 0, "cur": 0}

                def wview(k):
                    name, src, npart, (a, bcols) = chunks[k]
                    return ring[k % NR][0:npart, 0:a * bcols].rearrange("p (a b) -> p a b", a=a)

                def wissue(upto):
                    while wstate["issued"] < min(upto, len(chunks)):
                        k = wstate["issued"]
                        src = chunks[k][1]
                        dst = wview(k)
                        skey = {"inQ": "in", "inK": "in", "inV": "in", "inP": "in", "out": "out", "mq": "mq", "mo": "mo"}[chunks[k][0]]
                        op("sp", lambda e: e.dma_start(out=dst, in_=src), reads=scrb[(skey, l)], writes=[ringb[k % NR]], dma=True)
                        wstate["issued"] += 1

                def wnext(name):
                    k = wstate["cur"]
                    assert chunks[k][0] == name, (chunks[k][0], name)
                    wissue(k + 1)
                    wstate["cur"] += 1
                    return wview(k), ringb[k % NR], k

                def wprefetch():
                    wissue(wstate["cur"] + NR)

                with ExitStack() as mk:
                    memT = sb("memT", [128, 8, MEM], F32, mk)
                    memTb = [Buf() for _ in range(8)]
                    mnb = sb("mnb", [128, 8, MEM], BF16, mk)
                    mnbb = [Buf() for _ in range(8)]
                    wkv = sb("wkv", [128, 8, 1024], BF16, mk)
                    wkvb = Buf()
                    op("sp", lambda e: e.dma_start(out=wkv[:], in_=s_mkv[l, :, :, :]), reads=scrb[("mkv", l)], writes=[wkvb], dma=True)
                    for i in range(2):
                        for hf in range(2):
                            xi = xr.next()
                            op("sp", lambda e: e.dma_start(out=xin[xi][:], in_=mem_d[i * 128:(i + 1) * 128, hf * 512:(hf + 1) * 512]),
                               writes=[xinb[xi]], dma=True)
                            bk = bank_proj.next()

                            def tr(e):
                                f = None
                                for k in range(4):
                                    ins = e.transpose(ps[bk][:, k * 128:(k + 1) * 128], xin[xi][:, k * 128:(k + 1) * 128], ident)
                                    f = ins if f is None else f
                                return f, ins
                            op("pe", tr, reads=[xinb[xi], cstb], writes=[pb[bk]])
                            op("dve", lambda e: e.tensor_copy(out=memT[:, hf * 4:(hf + 1) * 4, i * 128:(i + 1) * 128],
                                                              in_=ps[bk][:, 0:512].rearrange("p (k t) -> p k t", k=4)),
                               reads=[pb[bk]], writes=memTb[hf * 4:(hf + 1) * 4])
                    a0, a1 = ftr.next(), ftr.next()
                    fm_norm(memT, memTb, mnb, mnbb, vec, vecb, 24, MEM, ft[a0], ftb[a0], ft[a1], ftb[a1], bank_stat.next())
                    for hm in range(4):
                        bk = bank_proj.next()

                        def mmk(e):
                            f = None
                            for kc in range(8):
                                ins = e.matmul(ps[bk][:, 0:MEM], wkv[:, kc, hm * 128:(hm + 1) * 128], mnb[:, kc, :],
                                               start=(kc == 0), stop=(kc == 7))
                                f = ins if f is None else f
                            return f, ins
                        op("pe", mmk, reads=mnbb + [wkvb], writes=[pb[bk]])
                        a0, a1 = ftr.next(), ftr.next()
                        head_norm(bk, bank_stat.next(), ones_b, 1.0 / 128, MEM, vec, 35, vecb, [(0, 128, mkT[:, hm, :], mkTb)],
                                  ysq[hm % 2], ysqb[hm % 2], ft[a0], ftb[a0], ft[a1], ftb[a1])
                    for mt in range(2):
                        bk = bank_proj.next()

                        def mmv(e):
                            f = None
                            for kc in range(8):
                                ins = e.matmul(ps[bk][:, :], mnb[:, kc, mt * 128:(mt + 1) * 128], wkv[:, kc, 512:1024],
                                               start=(kc == 0), stop=(kc == 7))
                                f = ins if f is None else f
                            return f, ins
                        op("pe", mmv, reads=mnbb + [wkvb], writes=[pb[bk]])
                        op("dve", lambda e: e.tensor_copy(out=mv[:, mt, :], in_=ps[bk][:, :]), reads=[pb[bk]], writes=[mvb])
                    sc.barrier()

                CK = 8
                NCH = 4
                Kc = [sb("Kc%d" % i, [128, CK * 128], BF16, mx) for i in range(NCH)]
                Vc = [sb("Vc%d" % i, [128, CK, 192], BF16, mx) for i in range(NCH)]
                kvb = [Buf() for _ in range(NCH)]
                fz = [sb("fz%d" % i, [128, 512], F32, mx) for i in range(4)]
                fzb = [Buf() for _ in range(4)]
                fzr = RR(range(4))
                kvchunks = []
                kvbase = {}
                for b_ in range(1, NB):
                    for c_ in range(4):
                        kvbase[(b_, c_)] = len(kvchunks)
                        for lo in range(0, 4 * b_, CK):
                            kvchunks.append((b_, c_, lo, min(lo + CK, 4 * b_)))
                kvstate = {"issued": 0, "released": 0}

                def kv_issue():
                    lim = min(len(kvchunks), kvstate["released"] + NCH)
                    while kvstate["issued"] < lim:
                        k = kvstate["issued"]
                        b_, c_, lo, hi = kvchunks[k]
                        if b_ > kvstate["maxb"]:
                            break
                        sl = k % NCH
                        op("sp", lambda e: e.dma_start(out=Kc[sl][:, 0:(hi - lo) * 128], in_=kscr[c_, :, lo * 128:hi * 128]),
                           reads=ksb[0:b_], writes=[kvb[sl]], dma=True)
                        op("sp", lambda e: e.dma_start(out=Vc[sl][:, 0:hi - lo, :], in_=vscr[c_, :, lo:hi, :]),
                           reads=vsb[0:b_], writes=[kvb[sl]], dma=True)
                        kvstate["issued"] += 1
                kvstate["maxb"] = 0

                def prologue(bn):
                    hTn, hTnb = hT2[bn % 2], hT2b[bn % 2]
                    tn = bn * 512
                    if first_layer:
                        for i in range(4):
                            for hf in range(2):
                                xi = xr.next()
                                op("sp", lambda e: e.dma_start(out=xin[xi][:], in_=x_d[tn + i * 128:tn + (i + 1) * 128, hf * 512:(hf + 1) * 512]),
                                   writes=[xinb[xi]], dma=True)
                                bk = bank_pro.next()

                                def tr(e):
                                    f = None
                                    for k in range(4):
                                        ins = e.transpose(ps[bk][:, k * 128:(k + 1) * 128], xin[xi][:, k * 128:(k + 1) * 128], ident)
                                        f = ins if f is None else f
                                    return f, ins
                                op("pe", tr, reads=[xinb[xi], cstb], writes=[pb[bk]])
                                op("dve", lambda e: e.tensor_copy(out=hTn[:, hf * 4:(hf + 1) * 4, i * 128:(i + 1) * 128],
                                                                  in_=ps[bk][:, 0:512].rearrange("p (k t) -> p k t", k=4)),
                                   reads=[pb[bk]], writes=hTnb[hf * 4:(hf + 1) * 4])
                            yield
                    else:
                        op("sp", lambda e: e.dma_start(out=hTn[:], in_=hscr_v[:, :, tn:tn + 512]),
                           reads=[hsb[bn]], writes=hTnb, dma=True)
                        yield
                    bkn = bank_pro.next()
                    fm_norm(hTn, hTnb, hb1, hb1b, vec, vecb, 0, 512, pa[0], pab[0], pa[1], pab[1], bkn, part="A")
                    yield
                    fm_norm(hTn, hTnb, hb1, hb1b, vec, vecb, 0, 512, pa[0], pab[0], pa[1], pab[1], bkn, part="B")
                    yield

                for _ in prologue(0):
                    pass
                for b in range(NB):
                    t0 = b * 512
                    wprefetch()
                    hT, hTb = hT2[b % 2], hT2b[b % 2]
                    hb, hbb = hb1, hb1b

                    wq_, wqb_, _ = wnext("inQ")
                    wk_, wkb_, _ = wnext("inK")
                    ysl = [(ysq[i], ysqb[i]) for i in range(4)]
                    stage_lists = []
                    for it in range(8):
                        which, c = ("inQ", it) if it < 4 else ("inK", it - 4)
                        wv_, wb_ = (wq_, wqb_) if it < 4 else (wk_, wkb_)
                        bk = it % 4
                        if which == "inQ":
                            dsts, gcol = [(0, 64, qT[0:64, 2 * c, :], qTb[2 * c]), (64, 128, qT[64:128, 2 * c + 1, :], qTb[2 * c + 1])], 32
                        else:
                            dsts, gcol = [(0, 128, kcur[:, c, :], kcb[c])], 33

                        def proj(bk=bk, wv_=wv_, wb_=wb_, c=c):
                            def mmq(e):
                                f = None
                                for kc in range(8):
                                    ins = e.matmul(ps[bk][:, :], wv_[:, kc, c * 128:(c + 1) * 128], hb[:, kc, :],
                                                   start=(kc == 0), stop=(kc == 7))
                                    f = ins if f is None else f
                                return f, ins
                            op("pe", mmq, reads=hbb + [wb_], writes=[pb[bk]])
                        ys, ysb_ = ysl[it % 4]
                        stage_lists.append([proj] + norm_stages(bk, 4 + (it % 2), bd64_b, 1.0 / 64, 512, vec, vecb, gcol, dsts,
                                                                ys, ysb_, ft[it % 4], ftb[it % 4]))
                    pipeline(stage_lists, [0, 1, 1, 2, 2, 3])
                    wprefetch()
                    if b < NB - 1:
                        op("sp", lambda e: e.dma_start(out=kscr_v[:, :, t0:t0 + 512], in_=kcur[:]),
                           reads=kcb, writes=[ksb[b]], dma=True)

                    wv_, wb_, _ = wnext("inV")
                    BF = 4
                    for i in range(4):
                        bk = i

                        def mmv2(e):
                            f = None
                            for kc in range(8):
                                ins = e.matmul(ps[bk][:, :], hb[:, kc, i * 128:(i + 1) * 128], wv_[:, kc, :],
                                               start=(kc == 0), stop=(kc == 7))
                                f = ins if f is None else f
                            return f, ins
                        op("pe", mmv2, reads=hbb + [wb_], writes=[pb[bk]])

                    def mmf(e):
                        f = None
                        for i in range(4):
                            for kc in range(8):
                                ins = e.matmul(ps[BF][:, i * 8:(i + 1) * 8], hb[:, kc, i * 128:(i + 1) * 128], wf[:, kc, :],
                                               start=(kc == 0), stop=(kc == 7))
                                f = ins if f is None else f
                        return f, ins
                    op("pe", mmf, reads=hbb + [wfb], writes=[pb[BF]])
                    for i in range(4):
                        pv4 = ps[i][:, 0:512].rearrange("p (c two d) -> p c two d", c=4, two=2)
                        op("dve", lambda e: e.tensor_copy(out=vcur[:, i, :, 0:64], in_=pv4[:, :, 0, :]),
                           reads=[pb[i]], writes=[vcb[i]])
                        op("dve", lambda e: e.tensor_copy(out=vcur[:, i, :, 128:192], in_=pv4[:, :, 1, :]),
                           reads=[pb[i]], writes=[vcb[i]])
                    op("dve", lambda e: e.tensor_tensor(out=fl[:, 0, :], in0=ps[BF][:, 0:32], in1=vec[:, 40:72], op=ALU.add),
                       reads=[pb[BF], vecb], writes=[flb[0]])
                    op("act", lambda e: e.activation(out=fl[:, 1, :], in_=fl[:, 0, :], func=AF.Exp, scale=-1.0),
                       reads=[flb[0]], writes=[flb[1]])
                    op("act", lambda e: e.activation(out=fl[:, 2, :], in_=fl[:, 1, :], func=AF.Ln, bias=epsT[:, 1:2], scale=1.0),
                       reads=[flb[1], epsb], writes=[flb[2]])
                    BC = 5

                    def mmc(e):
                        f = None
                        for i in range(4):
                            ins = e.matmul(ps[BC][:, i * 8:(i + 1) * 8], ident, cend[:, 4 * b, :], start=True, stop=False)
                            f = ins if f is None else f
                            for i2 in range(i):
                                e.matmul(ps[BC][:, i * 8:(i + 1) * 8], ones_f, fl[:, 2, i2 * 8:(i2 + 1) * 8], start=False, stop=False)
                            e.matmul(ps[BC][:, i * 8:(i + 1) * 8], triu_f, fl[:, 2, i * 8:(i + 1) * 8], start=False, stop=True)
                            e.matmul(ps[BC][:, 32 + i * 8:32 + (i + 1) * 8], ident, cend[:, 4 * b, :], start=True, stop=False)
                            for i2 in range(i + 1):
                                ins = e.matmul(ps[BC][:, 32 + i * 8:32 + (i + 1) * 8], ones_f, fl[:, 2, i2 * 8:(i2 + 1) * 8],
                                               start=False, stop=(i2 == i))
                        return f, ins
                    op("pe", mmc, reads=[flb[2], cstb, cendb], writes=[pb[BC]])
                    op("dve", lambda e: e.tensor_copy(out=negc[:, 4 * b:4 * b + 4, :], in_=ps[BC][:, 0:32].rearrange("p (t h) -> p t h", t=4)),
                       reads=[pb[BC]], writes=[negcb])
                    op("dve", lambda e: e.tensor_copy(out=cend[:, 4 * b + 1:4 * b + 5, :], in_=ps[BC][:, 32:64].rearrange("p (t h) -> p t h", t=4)),
                       reads=[pb[BC]], writes=[cendb])
                    for h in range(8):
                        op("dve", lambda e: e.tensor_scalar(out=biasb[h][:, 0:4 * b + 4], in0=negc[:, 0:4 * b + 4, h],
                                                            scalar1=cend[:, 4 * b + 2, h:h + 1], scalar2=None, op0=ALU.subtract),
                           reads=[negcb, cendb], writes=[biasbb[h]])
                    if b < NB - 1:
                        for c in range(4):
                            op("sp", lambda e: e.dma_start(out=vscr[c, :, 4 * b:4 * b + 4, :],
                                                             in_=vcur[:, :, c, :]),
                               reads=vcb, writes=[vsb[b]], dma=True)
                    wprefetch()

                    wv_, wb_, _ = wnext("inP")
                    for g in range(4):
                        bk = bank_proj.next()

                        def mmp(e):
                            f = None
                            for kc in range(8):
                                ins = e.matmul(ps[bk][:, :], wv_[:, kc, g * 128:(g + 1) * 128], hb[:, kc, :],
                                               start=(kc == 0), stop=(kc == 7))
                                f = ins if f is None else f
                            return f, ins
                        op("pe", mmp, reads=hbb + [wb_], writes=[pb[bk]])
                        op("act", lambda e: e.activation(out=pin[:, g, 16:528], in_=ps[bk][:, :], func=AF.Copy),
                           reads=[pb[bk]], writes=[pinb[g]])
                    wprefetch()

                    for g in range(4):
                        w = 2 << g
                        cur, curb = pin[:, g, :], pinb[g]
                        lo, off = 0, 1
                        for st in range(g + 1):
                            nx, nxb = ptmp[st % 2], ptmpb[st % 2]
                            lo2 = lo + off
                            op("pool", lambda e: e.tensor_tensor(out=nx[:, lo2:528], in0=cur[:, lo2:528], in1=cur[:, lo:528 - off], op=ALU.add),
                               reads=[curb], writes=[nxb])
                            cur, curb = nx[:, :], nxb
                            lo, off = lo2, off * 2
                        sc_t, sc_b = ptmp[(g + 1) % 2], ptmpb[(g + 1) % 2]
                        op("pool", lambda e: e.tensor_scalar(out=sc_t[:, 16:528], in0=cur[:, 16:528], scalar1=1.0 / w, scalar2=None, op0=ALU.mult),
                           reads=[curb], writes=[sc_b])
                        op("pool", lambda e: e.tensor_tensor(out=mixed[:, g, :], in0=sc_t[:, 16:528], in1=pin[:, g, 16:528], op=ALU.subtract),
                           reads=[sc_b, pinb[g]], writes=[mixb[g]])
                        if b == 0:
                            oth, othb = ptmp[(g + 1) % 2], ptmpb[(g + 1) % 2]
                            op("pool", lambda e: e.tensor_tensor(out=oth[:, 0:16], in0=cur[:, 16:32], in1=invcnt[:, g * 16:(g + 1) * 16], op=ALU.mult),
                               reads=[curb, cstb], writes=[othb])
                            op("pool", lambda e: e.tensor_tensor(out=mixed[:, g, 0:16], in0=oth[:, 0:16], in1=pin[:, g, 16:32], op=ALU.subtract),
                               reads=[othb, pinb[g]], writes=[mixb[g]])
                    op("pool", lambda e: e.tensor_copy(out=pin[:, :, 0:16], in_=pin[:, :, 512:528]), reads=[], writes=pinb)

                    nj = 4 * b + 4
                    items = [(2 * c + hh, j) for c in range(4) for j in range(nj) for hh in range(2)]
                    SKEW = 3
                    st_ = {}
                    obank = {}
                    pend = []
                    kvstate["maxb"] = b
                    kv_issue()

                    def kv_of(c, j):
                        q = j // CK
                        k = kvbase[(b, c)] + q
                        assert k < kvstate["issued"], (b, c, j, k, kvstate)
                        return k % NCH, j - q * CK, k

                    def emit_S(h, j):
                        c = h // 2
                        if j < 4 * b:
                            sl, jj, _ = kv_of(c, j)
                            lhsT = Kc[sl][:, jj * 128:(jj + 1) * 128]
                            kb = kvb[sl]
                            col0 = 0
                        else:
                            i = j - 4 * b
                            lhsT = kcur[:, c, i * 128:(i + 1) * 128]
                            kb = kcb[c]
                            col0 = i * 128
                        bs = bank_S.next()
                        if j >= 4 * b:
                            def mms(e):
                                i1 = e.matmul(ps[bs][:, col0:512], lhsT, qT[:, h, col0:512], start=True, stop=True)
                                i2 = e.matmul(ps[bs][:, col0:col0 + 128], ident_b, mneg_b, start=False, stop=True, skip_group_check=True)
                                return i1, i2
                            op("pe", mms, reads=[kb, qTb[h], cbfb], writes=[pb[bs]])
                        else:
                            op("pe", lambda e: e.matmul(ps[bs][:, col0:512], lhsT, qT[:, h, col0:512], start=True, stop=True),
                               reads=[kb, qTb[h]], writes=[pb[bs]])
                        pt = ptr.next()
                        op("act", lambda e: e.activation(out=PT[pt][:, col0:512], in_=ps[bs][:, col0:512], func=AF.Exp,
                                                         bias=biasb[h][:, j:j + 1], scale=0.125),
                           reads=[pb[bs], biasbb[h]], writes=[PTb[pt]])
                        st_[(h, j)] = (pt, col0)

                    def emit_PV(h, j):
                        c = h // 2
                        pt, col0 = st_.pop((h, j))
                        if j == 0 and h % 2 == 0:
                            bo2 = bank_O2.next()
                            obank[h], obank[h + 1] = bo2
                        bo = obank[h]
                        v0 = 0 if h % 2 == 0 else 64
                        if j < 4 * b:
                            sl, jj, k = kv_of(c, j)
                            lhsT = Vc[sl][:, jj, v0:v0 + 128]
                            vb_ = kvb[sl]
                        else:
                            lhsT = vcur[:, j - 4 * b, c, v0:v0 + 128]
                            vb_ = vcb[j - 4 * b]
                        op("pe", lambda e: e.matmul(ps[bo][:, col0:512], lhsT, PT[pt][:, col0:512], start=(j == 0), stop=(j == nj - 1)),
                           reads=[PTb[pt], vb_], writes=[pb[bo]])
                        if j < 4 * b and h % 2 == 1:
                            _, _, k = kv_of(c, j)
                            lo, hi = kvchunks[k][2], kvchunks[k][3]
                            if j == hi - 1:
                                kvstate["released"] = k + 1
                                kv_issue()
                        if j == nj - 1:
                            finalize(h, bo)

                    def finalize(h, bo):
                        c = h // 2
                        r1, r2 = fzr.next(), fzr.next()
                        if h % 2 == 0:
                            rr, o0 = 64, 0
                        else:
                            rr, o0 = 0, 64
                        op("dve", lambda e: e.reciprocal(out=fz[r1][rr:rr + 1, :], in_=ps[bo][rr:rr + 1, :]),
                           reads=[pb[bo]], writes=[fzb[r1]])

                        def fin_pe():
                            if h % 2 == 0:
                                op("pe", lambda e: e.matmul(ps[BANK_RB][0:64, :], ones_f[64:65, 0:64], fz[r1][64:65, :], start=True, stop=True),
                                   reads=[fzb[r1], cstb], writes=[pb[BANK_RB]])
                            else:
                                op("pe", lambda e: e.matmul(ps[BANK_RB][:, :], ones_f[0:1, :], fz[r1][0:1, :], start=True, stop=True),
                                   reads=[fzb[r1], cstb], writes=[pb[BANK_RB]])
                            op("dve", lambda e: e.tensor_copy(out=fz[r2][o0:o0 + 64, :], in_=ps[BANK_RB][o0:o0 + 64, :]),
                               reads=[pb[BANK_RB]], writes=[fzb[r2]])
                            op("dve", lambda e: e.tensor_tensor(out=foxT[o0:o0 + 64, c, :], in0=ps[bo][o0:o0 + 64, :], in1=fz[r2][o0:o0 + 64, :], op=ALU.mult),
                               reads=[pb[bo], fzb[r2]], writes=[foxb[h]])
                        pend.append([6, fin_pe])

                    def tick():
                        for p in list(pend):
                            p[0] -= 1
                            if p[0] <= 0:
                                p[1]()
                                pend.remove(p)

                    for idx in range(len(items) + SKEW):
                        if idx < len(items):
                            emit_S(*items[idx])
                        if idx >= SKEW:
                            emit_PV(*items[idx - SKEW])
                        tick()
                    while pend:
                        tick()
                    kvstate["maxb"] = b + 1
                    kv_issue()

                    for g in range(4):
                        bk = bank_proj.next()
                        op("pe", lambda e: e.matmul(ps[bk][:, :], wpl[:, g, :], mixed[:, g, :], start=True, stop=True),
                           reads=[mixb[g], wplb], writes=[pb[bk]])
                        op("act", lambda e: e.activation(out=poolT[:, g, :], in_=ps[bk][:, :], func=AF.Copy, scale=vec[:, 36 + g:37 + g]),
                           reads=[pb[bk], vecb], writes=[poolb[g]])

                    for hf in range(2):
                        wW, wWb, _ = wnext("out")
                        for oc in range(4):
                            kc = hf * 4 + oc
                            bk = bank_proj.next()

                            def mmo(e):
                                f = None
                                for c in range(4):
                                    ins = e.matmul(ps[bk][:, :], wW[:, c, oc * 128:(oc + 1) * 128], foxT[:, c, :],
                                                   start=(c == 0), stop=False)
                                    f = ins if f is None else f
                                for g in range(4):
                                    ins = e.matmul(ps[bk][:, :], wW[:, 4 + g, oc * 128:(oc + 1) * 128], poolT[:, g, :],
                                                   start=False, stop=(g == 3))
                                return f, ins
                            op("pe", mmo, reads=foxb + poolb + [wWb], writes=[pb[bk]])
                            op("dve", lambda e: e.tensor_tensor(out=hT[:, kc, :], in0=ps[bk][:, :], in1=hT[:, kc, :], op=ALU.add),
                               reads=[pb[bk], hTb[kc]], writes=[hTb[kc]])
                        wprefetch()

                    a0, a1 = ftr.next(), ftr.next()
                    hb, hbb = hb2, hb2b
                    fm_norm(hT, hTb, hb, hbb, vec, vecb, 8, 512, ft[a0], ftb[a0], ft[a1], ftb[a1], bank_stat.next())
                    gen = prologue(b + 1) if b + 1 < NB else iter(())
                    wQ, wQb, _ = wnext("mq")
                    moT = foxT
                    stage_lists = []
                    for hm in range(4):
                        bk = hm
                        p0, p1 = 2 * (hm % 2), 2 * (hm % 2) + 1
                        r1 = 4 + (hm % 2)
                        bst = 4 + (hm % 2)

                        def proj(bk=bk, hm=hm):
                            def mmq2(e):
                                f = None
                                for kc in range(8):
                                    ins = e.matmul(ps[bk][:, :], wQ[:, kc, hm * 128:(hm + 1) * 128], hb[:, kc, :],
                                                   start=(kc == 0), stop=(kc == 7))
                                    f = ins if f is None else f
                                return f, ins
                            op("pe", mmq2, reads=hbb + [wQb], writes=[pb[bk]])

                        def scores(hm=hm):
                            for mt in range(2):
                                op("pe", lambda e: e.matmul(ps[6 + mt][:, :], mkT[:, hm, mt * 128:(mt + 1) * 128], mqT[hm][:, :], start=True, stop=True),
                                   reads=[mkTb, mqTb[hm]], writes=[pb[6 + mt]])

                        def exps(hm=hm, p0=p0):
                            for mt in range(2):
                                op("act", lambda e: e.activation(out=PT[p0 + mt][:, :], in_=ps[6 + mt][:, :], func=AF.Exp, scale=128.0 ** -0.5),
                                   reads=[pb[6 + mt]], writes=[PTb[p0 + mt]])

                        def pv(hm=hm, bk=bk, p0=p0, p1=p1, bst=bst):
                            def mmpv(e):
                                i1 = e.matmul(ps[bk][:, :], mv[:, 0, hm * 128:(hm + 1) * 128], PT[p0][:, :], start=True, stop=False)
                                e.matmul(ps[bk][:, :], mv[:, 1, hm * 128:(hm + 1) * 128], PT[p1][:, :], start=False, stop=True)
                                e.matmul(ps[bst][:, :], ones_b, PT[p0][:, :], start=True, stop=False)
                                i2 = e.matmul(ps[bst][:, :], ones_b, PT[p1][:, :], start=False, stop=True)
                                return i1, i2
                            op("pe", mmpv, reads=[mvb, cbfb, PTb[p0], PTb[p1]], writes=[pb[bk], pb[bst]])

                        def fin(hm=hm, bk=bk, r1=r1, bst=bst):
                            op("act", lambda e: e.activation(out=ft[r1][:, :], in_=ps[bst][:, :], func=AF.Ln), reads=[pb[bst]], writes=[ftb[r1]])
                            op("act", lambda e: e.activation(out=ft[r1][:, :], in_=ft[r1][:, :], func=AF.Exp, scale=-1.0), reads=[ftb[r1]], writes=[ftb[r1]])
                            op("dve", lambda e: e.tensor_tensor(out=moT[:, hm, :], in0=ps[bk][:, :], in1=ft[r1][:, :], op=ALU.mult),
                               reads=[pb[bk], ftb[r1]], writes=[foxb[2 * hm], foxb[2 * hm + 1]])
                        stage_lists.append([proj] + norm_stages(bk, bst, ones_b, 1.0 / 128, 512, vec, vecb, 34,
                                                                [(0, 128, mqT[hm][:, :], mqTb[hm])], ysq[hm], ysqb[hm], ft[hm], ftb[hm])
                                           + [scores, exps, pv, fin])
                    pipeline(stage_lists, [0, 1, 1, 2, 2, 3, 3, 3, 4, 4], filler=lambda: next(gen, None))
                    for _ in gen:
                        pass
                    wprefetch()
                    wO, wOb, _ = wnext("mo")
                    for oc in range(8):
                        bk = bank_proj.next()

                        def mmo2(e):
                            f = None
                            for hm in range(4):
                                ins = e.matmul(ps[bk][:, :], wO[:, hm, oc * 128:(oc + 1) * 128], moT[:, hm, :],
                                               start=(hm == 0), stop=(hm == 3))
                                f = ins if f is None else f
                            return f, ins
                        op("pe", mmo2, reads=foxb + [wOb], writes=[pb[bk]])
                        op("dve", lambda e: e.tensor_tensor(out=hT[:, oc, :], in0=ps[bk][:, :], in1=hT[:, oc, :], op=ALU.add),
                           reads=[pb[bk], hTb[oc]], writes=[hTb[oc]])
                    wprefetch()
                    op("sp", lambda e: e.dma_start(out=hscr_v[:, :, t0:t0 + 512], in_=hT[:]),
                       reads=hTb, writes=[hsb[b]], dma=True)
                sc.barrier()

            with ExitStack() as fx:
                vec = sb("vecF", [128, NVEC], F32, fx)
                vecb = Buf()
                op("sp", lambda e: e.dma_start(out=vec[:], in_=vecs_d[l, :, :]), writes=[vecb], dma=True)
                wgu = sb("wgu", [128, 8, 2 * DFF], BF16, fx)
                wgub = [Buf() for _ in range(8)]
                wdn = sb("wdn", [128, NFC, 1024], BF16, fx)
                wdnb = [Buf() for _ in range(NFC)]
                for kc in range(8):
                    op("sp", lambda e: e.dma_start(out=wgu[:, kc, :], in_=s_gu[l, :, kc, :]), reads=scrb[("gu", l)], writes=[wgub[kc]], dma=True)
                for f0 in range(0, NFC, 6):
                    f1 = min(NFC, f0 + 6)
                    op("sp", lambda e: e.dma_start(out=wdn[:, f0:f1, :], in_=s_dn[l, :, f0:f1, :]), reads=scrb[("dn", l)], writes=wdnb[f0:f1], dma=True)
                hT2 = [sb("hTF%d" % i, [128, 8, 512], F32, fx) for i in range(2)]
                hT2b = [[Buf() for _ in range(8)] for _ in range(2)]
                hb = sb("hbF", [128, 8, 512], BF16, fx)
                hbb = [Buf() for _ in range(8)]
                actT = sb("actT", [128, NFC, 512], BF16, fx)
                actb = [Buf() for _ in range(NFC)]
                sg = [sb("sg%d" % i, [128, 512], F32, fx) for i in range(2)]
                sgb = [Buf() for _ in range(2)]
                fa = [sb("fa%d" % i, [128, 512], F32, fx) for i in range(2)]
                fab = [Buf() for _ in range(2)]
                ot, otb = sg, sgb
                otr = RR(range(2))
                bank_gu = RR([0, 1, 2, 3])
                bank_dn = RR([4, 5])
                bank_st = RR([6, 7])

                def load_h(b):
                    op("sp", lambda e: e.dma_start(out=hT2[b % 2][:], in_=hscr_v[:, :, b * 512:(b + 1) * 512]),
                       reads=[hsb[b]], writes=hT2b[b % 2], dma=True)

                def norm3(b, part):
                    fm_norm(hT2[b % 2], hT2b[b % 2], hb, hbb, vec, vecb, 16, 512, fa[0], fab[0], fa[1], fab[1], 6 + (b % 2), part=part)

                load_h(0)
                norm3(0, None)
                for b in range(NB):
                    t0 = b * 512
                    hT, hTb = hT2[b % 2], hT2b[b % 2]
                    if b + 1 < NB:
                        load_h(b + 1)
                    for fc in range(NFC):
                        bg, bu = bank_gu.next(), bank_gu.next()

                        def mmg(e):
                            f = None
                            for kc in range(8):
                                ins = e.matmul(ps[bg][:, :], wgu[:, kc, fc * 128:(fc + 1) * 128], hb[:, kc, :],
                                               start=(kc == 0), stop=(kc == 7))
                                f = ins if f is None else f
                            return f, ins
                        op("pe", mmg, reads=hbb + wgub, writes=[pb[bg]])

                        def mmu(e):
                            f = None
                            for kc in range(8):
                                ins = e.matmul(ps[bu][:, :], wgu[:, kc, DFF + fc * 128:DFF + (fc + 1) * 128], hb[:, kc, :],
                                               start=(kc == 0), stop=(kc == 7))
                                f = ins if f is None else f
                            return f, ins
                        op("pe", mmu, reads=hbb + wgub, writes=[pb[bu]])
                        si = fc % 2
                        op("act", lambda e: e.activation(out=sg[si][:, :], in_=ps[bg][:, :], func=AF.Silu),
                           reads=[pb[bg]], writes=[sgb[si]])
                        op("dve", lambda e: e.tensor_tensor(out=actT[:, fc, :], in0=ps[bu][:, :], in1=sg[si][:, :], op=ALU.mult),
                           reads=[pb[bu], sgb[si]], writes=[actb[fc]])
                    if b + 1 < NB:
                        norm3(b + 1, "A")
                    for oc in range(8):
                        bk = bank_dn.next()

                        def mmd(e):
                            f = None
                            for fc in range(NFC):
                                ins = e.matmul(ps[bk][:, :], wdn[:, fc, oc * 128:(oc + 1) * 128], actT[:, fc, :],
                                               start=(fc == 0), stop=(fc == NFC - 1))
                                f = ins if f is None else f
                            return f, ins
                        op("pe", mmd, reads=actb + wdnb, writes=[pb[bk]])
                        op("dve", lambda e: e.tensor_tensor(out=hT[:, oc, :], in0=ps[bk][:, :], in1=hT[:, oc, :], op=ALU.add),
                           reads=[pb[bk], hTb[oc]], writes=[hTb[oc]])
                        if oc == 3 and b + 1 < NB:
                            norm3(b + 1, "B")
                    if last_layer:
                        for i in range(4):
                            for hf in range(2):
                                bk = bank_st.next()

                                def trb(e):
                                    f = None
                                    for k in range(4):
                                        ins = e.transpose(ps[bk][:, k * 128:(k + 1) * 128], hT[:, hf * 4 + k, i * 128:(i + 1) * 128], ident)
                                        f = ins if f is None else f
                                    return f, ins
                                op("pe", trb, reads=hTb[hf * 4:(hf + 1) * 4] + [cstb], writes=[pb[bk]])
                                oi = otr.next()
                                op("act", lambda e: e.activation(out=ot[oi][:, :], in_=ps[bk][:, :], func=AF.Copy),
                                   reads=[pb[bk]], writes=[otb[oi]])
                                op("sp", lambda e: e.dma_start(out=out_d[t0 + i * 128:t0 + (i + 1) * 128, hf * 512:(hf + 1) * 512], in_=ot[oi][:, :]),
                                   reads=[otb[oi]], dma=True)
                    else:
                        op("sp", lambda e: e.dma_start(out=hscr_v[:, :, t0:t0 + 512], in_=hT[:]),
                           reads=hTb, writes=[hsb[b]], dma=True)
                sc.barrier()
        sc.finish()
    return nc


def make_consts():
    c = np.zeros((128, 896), np.float32)
    c[:, 0:128] = np.eye(128, dtype=np.float32)
    s = np.arange(128)[:, None]
    t = np.arange(128)[None, :]
    c[:, 128:256] = (s <= t).astype(np.float32)
    c[:, 256:384] = ((s // 64) == (t // 64)).astype(np.float32)
    c[:, 384:512] = 1.0
    for g, w in enumerate((2, 4, 8, 16)):
        c[:, 512 + g * 16:512 + (g + 1) * 16] = (1.0 / np.minimum(np.arange(16) + 1, w)).astype(np.float32)[None, :]
    c[:, 640:768] = np.where(s > t, -30000.0, 0.0).astype(np.float32)
    c[:, 768:896] = np.eye(128, dtype=np.float32)
    return c


def make_vecs(inp, NL):
    v = np.zeros((NL, 128, NVEC), np.float32)
    p = np.arange(128)
    for l in range(NL):
        for k, name in enumerate(("g_mix", "g_mem_q", "g_ffn", "g_mem_kv")):
            v[l, :, 8 * k:8 * k + 8] = np.asarray(inp[name][l]).reshape(8, 128).T
        v[l, :, 32] = np.asarray(inp["g_q_fox"][l])[p % 64]
        v[l, :, 33] = np.asarray(inp["g_k_fox"][l])[p % 64]
        v[l, :, 34] = np.asarray(inp["g_q_mem"][l])
        v[l, :, 35] = np.asarray(inp["g_k_mem"][l])
        v[l, :, 36:40] = np.asarray(inp["pool_scale"][l]).reshape(4, 128).T
        v[l, :, 40:72] = np.tile(np.asarray(inp["b_forget"][l]), 4)[None, :]
    return v


_NC_CACHE = {}


def run(inp, S, NL, ncores, trace=False):
    key = (S, NL)
    if key not in _NC_CACHE:
        _NC_CACHE[key] = build(S, NL)
    nc = _NC_CACHE[key]
    consts = make_consts()
    vecs = make_vecs(inp, NL)
    shared = {
        "w_in": np.ascontiguousarray(inp["w_in"], np.float32),
        "w_pool": np.ascontiguousarray(inp["w_pool"], np.float32),
        "w_out": np.ascontiguousarray(inp["w_out"], np.float32),
        "w_mem_q": np.ascontiguousarray(inp["w_mem_q"], np.float32),
        "w_mem_kv": np.ascontiguousarray(inp["w_mem_kv"], np.float32),
        "w_mem_out": np.ascontiguousarray(inp["w_mem_out"], np.float32),
        "w_gate_up": np.ascontiguousarray(inp["w_gate_up"], np.float32),
        "w_down": np.ascontiguousarray(inp["w_down"], np.float32),
        "vecs": vecs,
        "consts": consts,
    }
    x = np.asarray(inp["x"], np.float32)
    mem = np.asarray(inp["mem"], np.float32)
    in_maps = []
    for i in range(ncores):
        m = dict(shared)
        m["x"] = np.ascontiguousarray(x[i])
        m["mem"] = np.ascontiguousarray(mem[i])
        in_maps.append(m)
    res = run_bass_kernel_spmd(nc, in_maps, core_ids=list(range(ncores)), trace=trace)
    out = np.stack([np.asarray(r["out"], np.float32) for r in res.results], axis=0)
    return out, res


def kernel(**inputs):
    x = np.asarray(inputs["x"])
    B, S, _ = x.shape
    NL = int(np.asarray(inputs["w_in"]).shape[0])
    out, _ = run(inputs, S, NL, B)
    return out.astype(np.float32)
```
